# Optimizing a Trainium2 kernel written in Bass

```python
import math, functools
import jax, jax.numpy as jnp
from jax import lax
import numpy as np


D_MODEL = 1024
BATCH = 8
SEQ = 4096
DEPTH = 2

GRID_W = 64
CTX_LEN = 256
EPS = 1e-6
ROPE_BASE = 10000.0
N_MOD = 6
F32 = jnp.float32
RET_HEADS = 4
RET_DK = 64
RET_DV = 128
RET_QK = RET_HEADS * RET_DK
RET_WIDTH = RET_HEADS * RET_DV
RET_CHUNK = 128
S5_WIDTH = D_MODEL - RET_WIDTH
S5_GROUP = 16
S5_GROUPS = S5_WIDTH // S5_GROUP
S5_STATE = 64
AB_CUTS = (RET_QK, 2 * RET_QK, 2 * RET_QK + RET_WIDTH, 2 * RET_QK + RET_WIDTH + S5_WIDTH)
AB_IN = 2 * RET_QK + 2 * RET_WIDTH + S5_WIDTH
HG_HEADS = 8
HG_DK = D_MODEL // HG_HEADS
HG_DV = D_MODEL // HG_HEADS
HG_CHUNK = 32
D_FF = 4 * D_MODEL
N_EVEN = (DEPTH + 1) // 2
N_ODD = DEPTH // 2

kernel_name = 'hybrid_retention_s5_hgrn2_prefix_dit'


def _rmsnorm(x, g):
    xf = x.astype(F32)
    y = xf * lax.rsqrt(jnp.mean(jnp.square(xf), axis=-1, keepdims=True) + EPS)
    return (y * g.astype(F32)).astype(x.dtype)


def _head_rms(o):
    return o * lax.rsqrt(jnp.mean(jnp.square(o), axis=-1, keepdims=True) + EPS)


def _modulate(h, shift, scale):
    return h * (1.0 + scale) + shift


def _adaln(cond, w, b, n):
    return jnp.split(jax.nn.silu(cond) @ w + b, n, axis=-1)


def _sqrelu_mlp(h, w1, w2):
    return jnp.square(jax.nn.relu(h @ w1)) @ w2


def _flip(t, rev):
    return t[:, ::-1] if rev else t


def _split_heads(t, d):
    return t.astype(F32).reshape(t.shape[0], t.shape[1], -1, d)


def _grid_rope(n_tok):
    rows = n_tok // GRID_W
    row = jnp.broadcast_to(jnp.arange(rows, dtype=F32)[:, None], (rows, GRID_W)).reshape(-1)
    col = jnp.broadcast_to(jnp.arange(GRID_W, dtype=F32)[None, :], (rows, GRID_W)).reshape(-1)
    n_freq = RET_DK // 4
    inv = ROPE_BASE ** (-jnp.arange(n_freq, dtype=F32) / n_freq)
    ang = jnp.concatenate([row[:, None] * inv, col[:, None] * inv], axis=-1)
    return jnp.cos(ang), jnp.sin(ang)


def _rope(t, cos, sin):
    half = t.shape[-1] // 2
    t1, t2 = t[..., :half], t[..., half:]
    cs, sn = cos[None, :, None, :], sin[None, :, None, :]
    return jnp.concatenate([t1 * cs - t2 * sn, t1 * sn + t2 * cs], axis=-1)


def _retention_chunks(q, k, v, log_gamma, s0):
    b, n, h, _ = q.shape
    dv = v.shape[-1]
    c = min(RET_CHUNK, n)
    nc = n // c
    rs = lambda t: jnp.moveaxis(t.reshape(b, nc, c, h, t.shape[-1]), 1, 0)
    qc, kc, vc = rs(q), rs(k), rs(v)
    pos = jnp.arange(c, dtype=F32)
    diff = pos[:, None] - pos[None, :]
    decay = jnp.where(diff >= 0, jnp.exp(jnp.maximum(diff, 0.0)[None] * log_gamma[:, None, None]), 0.0)
    q_dec = jnp.exp((pos[:, None] + 1.0) * log_gamma[None])
    k_dec = jnp.exp((c - 1.0 - pos)[:, None] * log_gamma[None])
    c_dec = jnp.exp(c * log_gamma)
    att = jnp.einsum('nbihd,nbjhd->nbhij', qc, kc) * decay
    o_intra = jnp.einsum('nbhij,nbjhe->nbihe', att, vc)

    def step(s, inp):
        qn, kn, vn = inp
        o = jnp.einsum('bihd,ih,bhde->bihe', qn, q_dec, s)
        s = c_dec[None, :, None, None] * s + jnp.einsum('bjhd,jh,bjhe->bhde', kn, k_dec, vn)
        return s, o

    s_end, o_inter = lax.scan(step, s0, (qc, kc, vc))
    out = jnp.moveaxis(o_intra + o_inter, 0, 1).reshape(b, n, h, dv)
    return out, s_end


def _retention_state(k, v, log_gamma):
    n = k.shape[1]
    w = jnp.exp((n - 1.0 - jnp.arange(n, dtype=F32))[:, None] * log_gamma[None])
    return jnp.einsum('blhd,lh,blhe->bhde', k, w, v)


def _s5_discretise(a_re, a_im, log_dt, b_re, b_im):
    a_re, a_im = a_re.astype(F32), a_im.astype(F32)
    dt = jnp.exp(log_dt.astype(F32))[:, None]
    mag = jnp.exp(a_re * dt)
    ang = a_im * dt
    ab_re, ab_im = mag * jnp.cos(ang), mag * jnp.sin(ang)
    nr, ni = ab_re - 1.0, ab_im
    den = jnp.square(a_re) + jnp.square(a_im)
    fr = (nr * a_re + ni * a_im) / den
    fi = (ni * a_re - nr * a_im) / den
    b_re, b_im = b_re.astype(F32), b_im.astype(F32)
    bb_re = fr[..., None] * b_re - fi[..., None] * b_im
    bb_im = fr[..., None] * b_im + fi[..., None] * b_re
    return ab_re, ab_im, bb_re, bb_im


def _complex_affine_combine(e1, e2):
    a1r, a1i, b1r, b1i = e1
    a2r, a2i, b2r, b2i = e2
    return (a2r * a1r - a2i * a1i,
            a2r * a1i + a2i * a1r,
            a2r * b1r - a2i * b1i + b2r,
            a2r * b1i + a2i * b1r + b2i)


def _s5_scan(u, ab_re, ab_im, bb_re, bb_im, h0_re, h0_im):
    n = u.shape[1]
    x_re = jnp.einsum('blgm,gpm->blgp', u, bb_re)
    x_im = jnp.einsum('blgm,gpm->blgp', u, bb_im)
    x_re = x_re.at[:, 0].add(ab_re * h0_re - ab_im * h0_im)
    x_im = x_im.at[:, 0].add(ab_re * h0_im + ab_im * h0_re)
    shape = (1, n) + ab_re.shape
    a_re = jnp.broadcast_to(ab_re, shape)
    a_im = jnp.broadcast_to(ab_im, shape)
    _, _, h_re, h_im = lax.associative_scan(_complex_affine_combine, (a_re, a_im, x_re, x_im), axis=1)
    return h_re, h_im


def _s5_readout(h_re, h_im, c_re, c_im):
    return (jnp.einsum('blgp,gmp->blgm', h_re, c_re.astype(F32))
            - jnp.einsum('blgp,gmp->blgm', h_im, c_im.astype(F32)))


def _hgrn_gates(raw, lb):
    log_f = jnp.log(lb + (1.0 - lb) * jax.nn.sigmoid(raw))
    k = (1.0 - lb) * jax.nn.sigmoid(-raw)
    return log_f, k


def _hgrn_chunks(q, k, v, log_f, s0):
    b, n, h, _ = q.shape
    dv = v.shape[-1]
    c = min(HG_CHUNK, n)
    nc = n // c
    rs = lambda t: jnp.moveaxis(t.reshape(b, nc, c, h, t.shape[-1]), 1, 0)
    qc, kc, vc, lf = rs(q), rs(k), rs(v), rs(log_f)
    cum = jnp.cumsum(lf, axis=2)
    tot = cum[:, :, -1:]
    q_in = qc * jnp.exp(cum)
    k_in = kc * jnp.exp(-cum)
    k_out = kc * jnp.exp(tot - cum)
    mask = jnp.tril(jnp.ones((c, c), dtype=bool))
    att = jnp.where(mask, jnp.einsum('nbihd,nbjhd->nbhij', q_in, k_in), 0.0)
    o_intra = jnp.einsum('nbhij,nbjhe->nbihe', att, vc)

    def step(s, inp):
        qn, kn, vn, dn = inp
        o = jnp.einsum('bihd,bhde->bihe', qn, s)
        s = jnp.exp(dn)[:, 0, :, :, None] * s + jnp.einsum('bjhd,bjhe->bhde', kn, vn)
        return s, o

    s_end, o_inter = lax.scan(step, s0, (q_in, k_out, vc, tot))
    out = jnp.moveaxis(o_intra + o_inter, 0, 1).reshape(b, n, h, dv)
    return out, s_end


def _hgrn_state(k, v, log_f):
    cum = jnp.cumsum(log_f, axis=1)
    w = jnp.exp(cum[:, -1:] - cum)
    return jnp.einsum('blhd,blhe->bhde', k * w, v)


def _retention_s5_mixer(xl, xc, cos, sin, w_in, w_out, ret_logit, a_re, a_im, log_dt, b_re, b_im,
                        c_re, c_im, d_skip, w_glu, b_glu, ctx_out):
    b, n, _ = xl.shape
    bc, nc_tok, _ = xc.shape
    k_scale = RET_DK ** -0.5
    ql, kl, vl, ul, gl = jnp.split(xl @ w_in, AB_CUTS, axis=-1)
    if ctx_out:
        qc, kc, vc, uc, gc = jnp.split(xc @ w_in, AB_CUTS, axis=-1)
        qc = _split_heads(qc, RET_DK)
    else:
        kc, vc, uc = jnp.split(xc @ w_in[:, RET_QK:AB_CUTS[3]], [RET_QK, RET_QK + RET_WIDTH], axis=-1)
    ql = _rope(_split_heads(ql, RET_DK), cos, sin)
    kl = _rope(_split_heads(kl, RET_DK), cos, sin) * k_scale
    vl = _split_heads(vl, RET_DV)
    kc = _split_heads(kc, RET_DK) * k_scale
    vc = _split_heads(vc, RET_DV)
    ul_g = ul.astype(F32).reshape(b, n, S5_GROUPS, S5_GROUP)
    uc_g = uc.astype(F32).reshape(bc, nc_tok, S5_GROUPS, S5_GROUP)
    log_gamma = jax.nn.log_sigmoid(ret_logit.astype(F32))
    ret_zero = jnp.zeros((bc, RET_HEADS, RET_DK, RET_DV), F32)
    s5_zero = jnp.zeros((bc, S5_GROUPS, S5_STATE), F32)
    ret_l, ret_c, s5_l, s5_c = [], [], [], []
    for d in range(2):
        f = functools.partial(_flip, rev=(d == 1))
        lg = log_gamma[d]
        if ctx_out:
            o, s_ret = _retention_chunks(f(qc), f(kc), f(vc), lg, ret_zero)
            ret_c.append(f(o))
        else:
            s_ret = _retention_state(f(kc), f(vc), lg)
        o, _ = _retention_chunks(f(ql), f(kl), f(vl), lg, s_ret)
        ret_l.append(f(o))
        disc = _s5_discretise(a_re[d], a_im[d], log_dt[d], b_re[d], b_im[d])
        hc_re, hc_im = _s5_scan(f(uc_g), *disc, s5_zero, s5_zero)
        if ctx_out:
            s5_c.append(f(_s5_readout(hc_re, hc_im, c_re[d], c_im[d])))
        hl_re, hl_im = _s5_scan(f(ul_g), *disc, hc_re[:, -1], hc_im[:, -1])
        s5_l.append(f(_s5_readout(hl_re, hl_im, c_re[d], c_im[d])))

    def merge(ret, s5, g, u):
        bb, nn = u.shape[0], u.shape[1]
        r = _head_rms(ret[0] + ret[1]).reshape(bb, nn, RET_WIDTH) * jax.nn.silu(g.astype(F32))
        y = (s5[0] + s5[1]).reshape(bb, nn, S5_WIDTH) + d_skip.astype(F32) * u.astype(F32)
        y = jax.nn.gelu(y)
        y = y * jax.nn.sigmoid(y @ w_glu + b_glu)
        return jnp.concatenate([r, y], axis=-1) @ w_out

    yl = merge(ret_l, s5_l, gl, ul).astype(xl.dtype)
    yc = merge(ret_c, s5_c, gc, uc).astype(xc.dtype) if ctx_out else None
    return yl, yc


def _hgrn2_mixer(xl, xc, w_in, w_out, lower_bound, norm_g, ctx_out):
    D = D_MODEL
    bc = xc.shape[0]
    ql, ffl, fbl, il, gl = jnp.split(xl @ w_in, 5, axis=-1)
    if ctx_out:
        qc, ffc, fbc, ic, gc = jnp.split(xc @ w_in, 5, axis=-1)
        qc = _split_heads(qc, HG_DK)
    else:
        ffc, fbc, ic = jnp.split(xc @ w_in[:, D:4 * D], 3, axis=-1)
    ql = _split_heads(ql, HG_DK)
    il = _split_heads(il, HG_DV)
    ic = _split_heads(ic, HG_DV)
    raw_l, raw_c = (ffl, fbl), (ffc, fbc)
    zero = jnp.zeros((bc, HG_HEADS, HG_DK, HG_DV), F32)
    out_l, out_c = [], []
    for d in range(2):
        f = functools.partial(_flip, rev=(d == 1))
        lb = lower_bound[d].reshape(HG_HEADS, HG_DK)
        lfl, kl = _hgrn_gates(_split_heads(raw_l[d], HG_DK), lb)
        lfc, kc = _hgrn_gates(_split_heads(raw_c[d], HG_DK), lb)
        if ctx_out:
            o, s = _hgrn_chunks(f(qc), f(kc), f(ic), f(lfc), zero)
            out_c.append(f(o))
        else:
            s = _hgrn_state(f(kc), f(ic), f(lfc))
        o, _ = _hgrn_chunks(f(ql), f(kl), f(il), f(lfl), s)
        out_l.append(f(o))

    def merge(outs, g):
        o = _head_rms(outs[0] + outs[1]) * norm_g.astype(F32)
        o = o.reshape(o.shape[0], o.shape[1], D) * jax.nn.silu(g.astype(F32))
        return o @ w_out

    yl = merge(out_l, gl).astype(xl.dtype)
    yc = merge(out_c, gc).astype(xc.dtype) if ctx_out else None
    return yl, yc


def setup_inputs(seed: int = 0) -> dict:
    key = jax.random.key(seed)
    ks = iter(jax.random.split(key, 32))

    def nrm(shape, scale):
        return scale * jax.random.normal(next(ks), shape, F32)

    D = D_MODEL
    x = nrm((BATCH, SEQ, D), 1.0)
    c = nrm((BATCH, D), 1.0)
    ctx = nrm((BATCH, CTX_LEN, D), 1.0)
    c_ctx = nrm((D,), 1.0)
    w_mod = nrm((DEPTH, D, N_MOD * D), 0.5 * D ** -0.5)
    b_mod = nrm((DEPTH, N_MOD * D), 0.02)
    norm_mix = 1.0 + nrm((DEPTH, D), 0.02)
    norm_mlp = 1.0 + nrm((DEPTH, D), 0.02)
    w_mlp_in = nrm((DEPTH, D, D_FF), D ** -0.5)
    w_mlp_out = nrm((DEPTH, D_FF, D), D_FF ** -0.5)
    ab_w_in = nrm((N_EVEN, D, AB_IN), D ** -0.5)
    ab_w_out = nrm((N_EVEN, D, D), D ** -0.5)
    eps_h = np.exp(np.linspace(math.log(1.0 / 32), math.log(1.0 / 512), RET_HEADS))
    ret_logit = jnp.asarray(np.log((1.0 - eps_h) / eps_h), F32) + nrm((N_EVEN, 2, RET_HEADS), 0.05)
    n_idx = jnp.arange(S5_STATE, dtype=F32)
    s5_a_re = -0.5 + nrm((N_EVEN, 2, S5_GROUPS, S5_STATE), 0.01)
    s5_a_im = math.pi * n_idx + nrm((N_EVEN, 2, S5_GROUPS, S5_STATE), 0.01)
    s5_log_dt = jax.random.uniform(next(ks), (N_EVEN, 2, S5_GROUPS), F32, math.log(1e-3), math.log(1e-1))
    s5_b_re = nrm((N_EVEN, 2, S5_GROUPS, S5_STATE, S5_GROUP), (2 * S5_GROUP) ** -0.5)
    s5_b_im = nrm((N_EVEN, 2, S5_GROUPS, S5_STATE, S5_GROUP), (2 * S5_GROUP) ** -0.5)
    s5_c_re = nrm((N_EVEN, 2, S5_GROUPS, S5_GROUP, S5_STATE), S5_STATE ** -0.5)
    s5_c_im = nrm((N_EVEN, 2, S5_GROUPS, S5_GROUP, S5_STATE), S5_STATE ** -0.5)
    s5_d = nrm((N_EVEN, S5_WIDTH), 1.0)
    s5_w_glu = nrm((N_EVEN, S5_WIDTH, S5_WIDTH), S5_WIDTH ** -0.5)
    s5_b_glu = nrm((N_EVEN, S5_WIDTH), 0.02)
    hg_w_in = nrm((N_ODD, D, 5 * D), D ** -0.5)
    hg_w_out = nrm((N_ODD, D, D), D ** -0.5)
    hg_lb_logits = nrm((2, DEPTH, HG_HEADS * HG_DK), 0.1)
    hg_norm = 1.0 + nrm((N_ODD, HG_DV), 0.02)
    norm_final = 1.0 + nrm((D,), 0.02)
    return {'x': x, 'c': c, 'ctx': ctx, 'c_ctx': c_ctx, 'w_mod': w_mod, 'b_mod': b_mod,
            'norm_mix': norm_mix, 'norm_mlp': norm_mlp, 'w_mlp_in': w_mlp_in, 'w_mlp_out': w_mlp_out,
            'ab_w_in': ab_w_in, 'ab_w_out': ab_w_out, 'ret_logit': ret_logit,
            's5_a_re': s5_a_re, 's5_a_im': s5_a_im, 's5_log_dt': s5_log_dt,
            's5_b_re': s5_b_re, 's5_b_im': s5_b_im, 's5_c_re': s5_c_re, 's5_c_im': s5_c_im,
            's5_d': s5_d, 's5_w_glu': s5_w_glu, 's5_b_glu': s5_b_glu,
            'hg_w_in': hg_w_in, 'hg_w_out': hg_w_out, 'hg_lb_logits': hg_lb_logits, 'hg_norm': hg_norm,
            'norm_final': norm_final}


def reference(x, c, ctx, c_ctx, w_mod, b_mod, norm_mix, norm_mlp, w_mlp_in, w_mlp_out,
              ab_w_in, ab_w_out, ret_logit, s5_a_re, s5_a_im, s5_log_dt, s5_b_re, s5_b_im,
              s5_c_re, s5_c_im, s5_d, s5_w_glu, s5_b_glu, hg_w_in, hg_w_out, hg_lb_logits, hg_norm,
              norm_final):
    cos, sin = _grid_rope(x.shape[1])
    gam = jax.nn.softmax(hg_lb_logits.astype(F32), axis=1)
    lower_bounds = jnp.cumsum(gam, axis=1) - gam[:, :1]
    hl, hc = x, ctx
    for l in range(DEPTH):
        ctx_out = l < DEPTH - 1
        ml = [m[:, None, :] for m in _adaln(c, w_mod[l], b_mod[l], N_MOD)]
        if ctx_out:
            mc = _adaln(c_ctx, w_mod[l], b_mod[l], N_MOD)
        else:
            mc = _adaln(c_ctx, w_mod[l, :, :2 * D_MODEL], b_mod[l, :2 * D_MODEL], 2)
        xl = _modulate(_rmsnorm(hl, norm_mix[l]), ml[0], ml[1])
        xc = _modulate(_rmsnorm(hc, norm_mix[l]), mc[0], mc[1])
        j = l // 2
        if l % 2 == 0:
            yl, yc = _retention_s5_mixer(xl, xc, cos, sin, ab_w_in[j], ab_w_out[j], ret_logit[j],
                                         s5_a_re[j], s5_a_im[j], s5_log_dt[j], s5_b_re[j], s5_b_im[j],
                                         s5_c_re[j], s5_c_im[j], s5_d[j], s5_w_glu[j], s5_b_glu[j], ctx_out)
        else:
            yl, yc = _hgrn2_mixer(xl, xc, hg_w_in[j], hg_w_out[j], lower_bounds[:, l], hg_norm[j], ctx_out)
        hl = hl + ml[2] * yl
        hl = hl + ml[5] * _sqrelu_mlp(_modulate(_rmsnorm(hl, norm_mlp[l]), ml[3], ml[4]), w_mlp_in[l], w_mlp_out[l])
        if ctx_out:
            hc = hc + mc[2] * yc
            hc = hc + mc[5] * _sqrelu_mlp(_modulate(_rmsnorm(hc, norm_mlp[l]), mc[3], mc[4]), w_mlp_in[l], w_mlp_out[l])
    return _rmsnorm(hl, norm_final)
```

```python
import contextlib
import numpy as np
import concourse.bass as bass
import concourse.mybir as mybir
from concourse.bass_utils import run_bass_kernel_spmd

F32 = mybir.dt.float32
BF16 = mybir.dt.bfloat16
I32 = mybir.dt.int32
F32R = mybir.dt.float32r
AF = mybir.ActivationFunctionType
ALU = mybir.AluOpType

D = 1024
KC = 8
NCTX = 256
NLAT = 4096
NTOK = NCTX + NLAT
DFF = 4096
EPS = 1e-6
DEBUG_BAR = False

ENGS = ("pe", "dve", "act", "pool", "sp")
NDMA_SEM = 44
NDMA_SW = 14
SAME_ENGINE_SYNC = False


class Buf:
    __slots__ = ("name", "w", "r", "dsem")

    def __init__(self, name):
        self.name = name
        self.w = {}
        self.r = {}
        self.dsem = {}


class Prog:
    def __init__(self, nc, es):
        self.nc = nc
        self.es = es
        self.q = {e: [] for e in ENGS}
        self.chan_ops = {}
        self.needed = {}
        self.seen = {e: {} for e in ENGS}
        for e in ENGS:
            self.chan_ops[e] = 0
            self.needed[e] = set()
        self.free_dma = {"sw": [], "hw": []}
        for i in range(NDMA_SEM):
            c = "dma%d" % i
            self.chan_ops[c] = 0
            self.needed[c] = set()
            self.free_dma["sw" if i < NDMA_SW else "hw"].append(c)
        self.nbuf = 0
        self.cur_bufs = []
        self.phase = "init"
        self.annotate = False

    def buf(self, name=None):
        self.nbuf += 1
        b = Buf(name or ("b%d" % self.nbuf))
        self.cur_bufs.append(b)
        return b

    def bufs(self, n, name="b"):
        return [self.buf("%s%d" % (name, i)) for i in range(n)]

    def _deps(self, eng, chan, reads, writes):
        waits = {}
        is_dma = chan.startswith("dma")

        def need(c, i):
            if c == eng and not is_dma and (eng == "pe" or not SAME_ENGINE_SYNC):
                return
            if self.seen[eng].get(c, -1) >= i:
                return
            if waits.get(c, -1) < i:
                waits[c] = i

        for b in reads:
            for c, i in b.w.items():
                need(c, i)
        for b in writes:
            for c, i in b.w.items():
                need(c, i)
            for c, i in b.r.items():
                need(c, i)
        for c, i in waits.items():
            self.seen[eng][c] = i
            self.needed[c].add(i)
        return waits

    def _commit(self, chan, idx, reads, writes, multi=False):
        for b in reads:
            if b.r.get(chan, -1) < idx:
                b.r[chan] = idx
        for b in writes:
            if multi:
                b.w[chan] = idx
            else:
                b.w = {chan: idx}
                b.r = {}

    def op(self, eng, fn, reads=(), writes=()):
        waits = self._deps(eng, eng, reads, writes)
        idx = self.chan_ops[eng]
        self.chan_ops[eng] = idx + 1
        self.q[eng].append((waits, fn, eng, idx, self.phase))
        self._commit(eng, idx, reads, writes)

    def dma(self, eng, fn, reads=(), writes=(), prim=None, acc=()):
        if prim is None:
            prim = writes[0] if writes else reads[0]
        kind = "sw" if eng == "pool" else "hw"
        if kind not in prim.dsem:
            assert self.free_dma[kind], "out of dma semaphores"
            prim.dsem[kind] = self.free_dma[kind].pop(0)
        chan = prim.dsem[kind]
        waits = self._deps(eng, chan, list(reads), list(writes))
        idx = self.chan_ops[chan]
        self.chan_ops[chan] = idx + 1
        self.needed[chan].add(idx)
        self.q[eng].append((waits, fn, chan, idx, self.phase))
        self._commit(chan, idx, reads, writes)
        self._commit(chan, idx, (), acc, multi=True)

    def wait_all(self, eng, bufs):
        waits = self._deps(eng, eng, bufs, ())
        self.q[eng].append((waits, None, None, None, self.phase))

    def barrier(self):
        last = {c: n - 1 for c, n in self.chan_ops.items() if n > 0}
        for e in ENGS:
            waits = {}
            for c, i in last.items():
                if c == e:
                    continue
                if self.seen[e].get(c, -1) >= i:
                    continue
                waits[c] = i
                self.seen[e][c] = i
                self.needed[c].add(i)
            if waits:
                self.q[e].append((waits, None, None, None, self.phase))
        for b in self.cur_bufs:
            for kind, c in b.dsem.items():
                self.free_dma[kind].append(c)
            b.dsem = {}
        self.cur_bufs = []

    def emit(self):
        nc = self.nc
        val = {}
        for c in self.chan_ops:
            if c.startswith("dma"):
                val[c] = {i: (i + 1) * 16 for i in range(self.chan_ops[c])}
            else:
                v = 0
                d = {}
                need = self.needed[c]
                for i in range(self.chan_ops[c]):
                    if i in need:
                        v += 1
                    d[i] = v
                val[c] = d
        sems = {}
        for c in self.chan_ops:
            if self.chan_ops[c] > 0:
                sems[c] = self.es.enter_context(nc.semaphore("s_" + c))
        handles = {"pe": "tensor", "dve": "vector", "act": "scalar", "pool": "gpsimd", "sp": "sync"}
        self.maxval = {c: (max(val[c].values()) if val[c] else 0) for c in val}

        def run(eng_name):
            def body(e):
                for waits, fn, chan, idx, phase in self.q[eng_name]:
                    for c, i in waits.items():
                        e.wait_ge(sems[c], val[c][i])
                    if fn is None:
                        continue
                    ins = fn(e)
                    if self.annotate:
                        ins.annotate(phase)
                    if chan.startswith("dma"):
                        ins.then_inc(sems[chan], 16)
                    elif idx in self.needed[chan]:
                        ins.then_inc(sems[chan], 1)
            return body

        with nc.Block() as block:
            for en in ENGS:
                if self.q[en]:
                    getattr(block, handles[en])(run(en))


class KB:
    def __init__(self, nc, es):
        self.nc = nc
        self.es = es
        self.P = Prog(nc, es)
        self.n = 0
        self.debug = False
        self.dbg_outs = {}
        self.Bdbg = None

    def dump(self, name, ap, bufs, dt=None):
        if not self.debug or name in self.dbg_outs:
            return
        shape = list(ap.shape)
        t = self.nc.dram_tensor("dbg_" + name, shape, dt or ap.dtype, kind="ExternalOutput").ap()
        self.dbg_outs[name] = t
        if self.Bdbg is None:
            self.Bdbg = self.P.buf("dbg")
        self.P.dma("sp", DMA(t, ap), reads=list(bufs), acc=[self.Bdbg])
        self.P.wait_all("sp", [self.Bdbg])

    def sb(self, es, shape, dt, name=None):
        self.n += 1
        return es.enter_context(self.nc.sbuf_tensor(name or ("t%d" % self.n), list(shape), dt))

    def ps(self, es, shape, dt=F32, name=None):
        self.n += 1
        return es.enter_context(self.nc.psum_tensor(name or ("p%d" % self.n), list(shape), dt))


def lay_kc(w):
    K, N = w.shape
    return np.ascontiguousarray(w.reshape(K // 128, 128, N).transpose(1, 0, 2))


def lay_vec(v):
    return np.ascontiguousarray(v.reshape(-1, 128).T)


def MM(out, lhsT, rhs, start=True, stop=True):
    return lambda e: e.matmul(out, lhsT, rhs, start=start, stop=stop)


def TR(out, in_, ident):
    return lambda e: e.transpose(out, in_, ident)


def ACT(out, in_, func, bias=None, scale=None):
    kw = {}
    if bias is not None:
        kw["bias"] = bias
    if scale is not None:
        kw["scale"] = scale
    return lambda e: e.activation(out, in_, func, **kw)


def TT(out, a, b, op):
    return lambda e: e.tensor_tensor(out, a, b, op)


def STT(out, in0, scalar, in1, op0, op1):
    return lambda e: e.scalar_tensor_tensor(out, in0, scalar, in1, op0, op1)


def TS(out, in0, s1, s2, op0, op1=None):
    if op1 is None:
        return lambda e: e.tensor_scalar(out, in0, s1, None, op0)
    return lambda e: e.tensor_scalar(out, in0, s1, s2, op0, op1)


def CP(out, in_):
    return lambda e: e.tensor_copy(out, in_)


def RCP(out, in_):
    return lambda e: e.reciprocal(out, in_)


def MSET(out, v):
    return lambda e: e.memset(out, v)


def DMA(out, in_, **kw):
    return lambda e: e.dma_start(out=out, in_=in_, **kw)


def SCAN(out, d0, d1, init, op0, op1):
    return lambda e: e.tensor_tensor_scan(out, d0, d1, init, op0, op1)


def interleave(gens, width, slots=2):
    active = []
    it = iter(gens)
    pending = None
    more = True
    while True:
        while more and len(active) < width:
            if pending is None:
                try:
                    pending = next(it)
                except StopIteration:
                    more = False
                    break
            if any(t_ <= pending[0] - slots for t_, _ in active):
                break
            active.append(pending)
            pending = None
        if not active:
            break
        for item in list(active):
            try:
                next(item[1])
            except StopIteration:
                active.remove(item)


class Adaln:
    def __init__(self, kb, d, es, vec, Bvec):
        P = kb.P
        self.kb, self.d, self.vec, self.Bvec = kb, d, vec, Bvec
        self.cc = kb.sb(es, [128, KC, 2], F32)
        self.sc = kb.sb(es, [128, KC, 2], F32)
        self.bm = kb.sb(es, [128, 2, 48], F32)
        self.nr = kb.sb(es, [128, 5, KC], F32)
        self.wbuf = [kb.sb(es, [128, KC, 256], F32) for _ in range(2)]
        self.raw = kb.sb(es, [128, 2, 48], F32)
        self.pm = kb.ps(es, [128, 48, 2], F32)
        self.Bcc, self.Bsc, self.Bbm, self.Bnr, self.Braw, self.Bpm = P.bufs(6, "ada")
        self.Bw = P.bufs(2, "adaw")
        P.dma("sp", DMA(self.cc[:], d["cc"]), writes=[self.Bcc])
        P.dma("sp", DMA(self.bm[:], d["bmod"]), writes=[self.Bbm])
        P.dma("sp", DMA(self.nr[:], d["nrm"]), writes=[self.Bnr])
        P.op("act", ACT(self.sc[:], self.cc[:], AF.Silu), reads=[self.Bcc], writes=[self.Bsc])

    def layer(self, l):
        P = self.kb.P
        d, vec, Bvec = self.d, self.vec, self.Bvec
        pm, sc, raw, nr, bm = self.pm, self.sc, self.raw, self.nr, self.bm
        def load(j):
            P.dma("sp", DMA(self.wbuf[j % 2][:], d["wmod"][l, j // 2][:, :, 256 * (j % 2):256 * (j % 2) + 256]), writes=[self.Bw[j % 2]])
        load(0)
        for j in range(24):
            wb = self.wbuf[j % 2]
            bw = self.Bw[j % 2]
            if j + 1 < 24:
                load(j + 1)
            for c2 in range(2):
                c48 = j * 2 + c2
                for kc in range(KC):
                    P.op("pe", MM(pm[:, c48, :], wb[:, kc, c2 * 128:(c2 + 1) * 128], sc[:, kc, :],
                                  start=(kc == 0), stop=(kc == KC - 1)), reads=[bw, self.Bsc], writes=[self.Bpm])
            yield
        P.op("dve", TT(raw[:], pm[:].rearrange("p c s -> p s c"),
                       bm[:, l, :].unsqueeze(1).broadcast_to([128, 2, 48]), ALU.add),
             reads=[self.Bpm, self.Bbm], writes=[self.Braw])
        for s in range(2):
            r6 = raw[:, s, :].rearrange("p (j k) -> p j k", j=6)
            for (jo, jscale, jshift, jgate, nidx) in ((0, 1, 0, 2, l), (3, 4, 3, 5, 2 + l)):
                P.op("dve", STT(vec[:, l, s, jo, :], r6[:, jscale, :], 1.0, nr[:, nidx, :], ALU.add, ALU.mult),
                     reads=[self.Braw, self.Bnr], writes=[Bvec])
                P.op("dve", CP(vec[:, l, s, jo + 1, :], r6[:, jshift, :]), reads=[self.Braw], writes=[Bvec])
                P.op("dve", CP(vec[:, l, s, jo + 2, :], r6[:, jgate, :]), reads=[self.Braw], writes=[Bvec])
        yield


def phase_adaln(kb, d, vec, Bvec):
    P = kb.P
    P.phase = "adaln"
    with contextlib.ExitStack() as es:
        ada = Adaln(kb, d, es, vec, Bvec)
        for l in range(2):
            for _ in ada.layer(l):
                pass
    P.barrier()


class NormBufs:
    def __init__(self, kb, es, NT):
        P = kb.P
        self.sq = kb.sb(es, [128, KC, NT], BF16)
        self.rstd = kb.sb(es, [128, NT], F32)
        self.tmp = [kb.sb(es, [128, NT], F32) for _ in range(2)]
        self.ss = kb.ps(es, [128, NT], F32)
        self.Bsq, self.Brstd, self.Bss = P.bufs(3, "nb")
        self.Btmp = P.bufs(2, "nbt")


def rstd_of(kb, nb, hT, Bh, nt, ones, Bones):
    P = kb.P
    for kc in range(KC):
        P.op("act", ACT(nb.sq[:, kc, :nt], hT[:, kc, :nt], AF.Square), reads=[Bh], writes=[nb.Bsq])
    for kc in range(KC):
        P.op("pe", MM(nb.ss[:, :nt], ones[:], nb.sq[:, kc, :nt], start=(kc == 0), stop=(kc == KC - 1)),
             reads=[nb.Bsq, Bones], writes=[nb.Bss])
    P.op("act", ACT(nb.rstd[:, :nt], nb.ss[:, :nt], AF.Ln, bias=EPS, scale=1.0 / D), reads=[nb.Bss], writes=[nb.Brstd])
    P.op("act", ACT(nb.rstd[:, :nt], nb.rstd[:, :nt], AF.Exp, scale=-0.5), reads=[nb.Brstd], writes=[nb.Brstd])


def norm_stage_a(kb, nb, hT, Bh, nt):
    P = kb.P
    for kc in range(KC):
        P.op("act", ACT(nb.sq[:, kc, :nt], hT[:, kc, :nt], AF.Square), reads=[Bh], writes=[nb.Bsq])


def norm_stage_b(kb, nb, nt, ones, Bones):
    P = kb.P
    for kc in range(KC):
        P.op("pe", MM(nb.ss[:, :nt], ones[:], nb.sq[:, kc, :nt], start=(kc == 0), stop=(kc == KC - 1)),
             reads=[nb.Bsq, Bones], writes=[nb.Bss])
    P.op("act", ACT(nb.rstd[:, :nt], nb.ss[:, :nt], AF.Ln, bias=EPS, scale=1.0 / D), reads=[nb.Bss], writes=[nb.Brstd])
    P.op("act", ACT(nb.rstd[:, :nt], nb.rstd[:, :nt], AF.Exp, scale=-0.5), reads=[nb.Brstd], writes=[nb.Brstd])


def norm_stage_c(kb, nb, hT, Bh, nt, G, S, Bvec, xm, Bxm):
    P = kb.P
    for kc in range(KC):
        t = nb.tmp[kc % 2]
        bt = nb.Btmp[kc % 2]
        P.op("dve", STT(t[:, :nt], hT[:, kc, :nt], G[:, kc:kc + 1], nb.rstd[:, :nt], ALU.mult, ALU.mult),
             reads=[Bh, nb.Brstd, Bvec], writes=[bt])
        P.op("act", ACT(xm[:, kc, :nt], t[:, :nt], AF.Identity, bias=S[:, kc:kc + 1]), reads=[bt, Bvec], writes=[Bxm])


def norm_mod(kb, nb, hT, Bh, nt, G, S, Bvec, ones, Bones, xm, Bxm):
    P = kb.P
    rstd_of(kb, nb, hT, Bh, nt, ones, Bones)
    for kc in range(KC):
        t = nb.tmp[kc % 2]
        bt = nb.Btmp[kc % 2]
        P.op("dve", STT(t[:, :nt], hT[:, kc, :nt], G[:, kc:kc + 1], nb.rstd[:, :nt], ALU.mult, ALU.mult),
             reads=[Bh, nb.Brstd, Bvec], writes=[bt])
        P.op("act", ACT(xm[:, kc, :nt], t[:, :nt], AF.Identity, bias=S[:, kc:kc + 1]), reads=[bt, Bvec], writes=[Bxm])


def phase_mlp(kb, d, l, hin, hout, tiles, vec, Bvec, ones, Bones, final_g=None):
    P = kb.P
    P.phase = "mlp%d" % l
    NT = max(t[1] for t in tiles)
    NF = DFF // 128
    with contextlib.ExitStack() as es:
        W1 = kb.sb(es, [128, KC, DFF], BF16)
        W2 = kb.sb(es, [128, NF, D], BF16)
        hT = [kb.sb(es, [128, KC, NT], F32) for _ in range(2)]
        xm = [kb.sb(es, [128, KC, NT], BF16) for _ in range(2)]
        a1 = kb.sb(es, [128, NF, NT], BF16)
        rl = [kb.sb(es, [128, NT], BF16) for _ in range(2)]
        nb = NormBufs(kb, es, NT)
        pa = [kb.ps(es, [128, NT], F32) for _ in range(3)]
        po = [kb.ps(es, [128, NT], F32) for _ in range(2)]
        BW1 = P.bufs(4, "W1")
        BW2 = P.bufs(4, "W2")
        Bh = P.bufs(2, "h")
        Bxm = P.bufs(2, "xm")
        Ba1 = P.buf("a1")
        Brl = P.bufs(2, "rl")
        Bpa = P.bufs(3, "pa")
        Bpo = P.bufs(2, "po")
        Bout = P.buf("hout")
        for i in range(4):
            P.dma("pool", DMA(W1[:, 2 * i:2 * i + 2, :], d["w1"][l, :, 2 * i:2 * i + 2, :]), writes=[BW1[i]])
        for i in range(4):
            P.dma("pool", DMA(W2[:, 8 * i:8 * i + 8, :], d["w2"][l, :, 8 * i:8 * i + 8, :]), writes=[BW2[i]])

        def load(i):
            t0, nt, s = tiles[i]
            P.dma("sp", DMA(hT[i % 2][:, :, :nt], hin["ap"][:, :, t0 - hin["off"]:t0 - hin["off"] + nt]), writes=[Bh[i % 2]])

        load(0)

        def prep(i, stage):
            t0_, nt_, s_ = tiles[i]
            if stage == "a":
                norm_stage_a(kb, nb, hT[i % 2], Bh[i % 2], nt_)
            elif stage == "b":
                norm_stage_b(kb, nb, nt_, ones, Bones)
            else:
                norm_stage_c(kb, nb, hT[i % 2], Bh[i % 2], nt_, vec[:, l, s_, 3, :], vec[:, l, s_, 4, :], Bvec, xm[i % 2], Bxm[i % 2])

        for st_ in ("a", "b", "c"):
            prep(0, st_)
        for i, (t0, nt, s) in enumerate(tiles):
            if i + 1 < len(tiles):
                load(i + 1)
            h = hT[i % 2]
            bh = Bh[i % 2]
            xm_ = xm[i % 2]
            bxm = Bxm[i % 2]
            gate = vec[:, l, s, 5, :]
            nxt = i + 1 < len(tiles)
            for fc in range(NF):
                p = pa[fc % 3]
                bp = Bpa[fc % 3]
                for kc in range(KC):
                    P.op("pe", MM(p[:, :nt], W1[:, kc, fc * 128:(fc + 1) * 128], xm_[:, kc, :nt],
                                  start=(kc == 0), stop=(kc == KC - 1)), reads=[BW1[kc // 2], bxm], writes=[bp])
                r = rl[fc % 2]
                br = Brl[fc % 2]
                P.op("act", ACT(r[:, :nt], p[:, :nt], AF.Relu), reads=[bp], writes=[br])
                eng = "pool" if fc % 4 == 0 else "dve"
                P.op(eng, TT(a1[:, fc, :nt], r[:, :nt], r[:, :nt], ALU.mult), reads=[br], writes=[Ba1])
            if nxt:
                prep(i + 1, "a")
            for dc in range(KC):
                p = po[dc % 2]
                bp = Bpo[dc % 2]
                for fc in range(NF):
                    P.op("pe", MM(p[:, :nt], W2[:, fc, dc * 128:(dc + 1) * 128], a1[:, fc, :nt],
                                  start=(fc == 0), stop=(fc == NF - 1)), reads=[BW2[fc // 8], Ba1], writes=[bp])
                P.op("dve", STT(h[:, dc, :nt], p[:, :nt], gate[:, dc:dc + 1], h[:, dc, :nt], ALU.mult, ALU.add),
                     reads=[bp, bh, Bvec], writes=[bh])
                if nxt and dc == 1:
                    prep(i + 1, "b")
                if nxt and dc == 3:
                    prep(i + 1, "c")
            if final_g is not None:
                rstd_of(kb, nb, h, bh, nt, ones, Bones)
                for kc in range(KC):
                    P.op("dve", STT(h[:, kc, :nt], h[:, kc, :nt], final_g[:, kc:kc + 1], nb.rstd[:, :nt], ALU.mult, ALU.mult),
                         reads=[bh, nb.Brstd, Bvec], writes=[bh])
            P.dma("sp", DMA(hout["ap"][:, :, t0 - hout["off"]:t0 - hout["off"] + nt], h[:, :, :nt]), reads=[bh], acc=[Bout])
        P.wait_all("sp", [Bout])
    P.barrier()


def phase_outproj(kb, wout, l, mo, mo_off, hin, hout, tiles, vec, Bvec):
    P = kb.P
    P.phase = "outproj%d" % l
    NT = 512
    with contextlib.ExitStack() as es:
        W = kb.sb(es, [128, KC, D], BF16)
        hT = [kb.sb(es, [128, KC, NT], F32) for _ in range(2)]
        mT = [kb.sb(es, [128, KC, NT], BF16) for _ in range(2)]
        po = [kb.ps(es, [128, NT], F32) for _ in range(2)]
        BW = P.buf("Wo")
        Bh = P.bufs(2, "h")
        Bm = P.bufs(2, "mo")
        Bpo = P.bufs(2, "po")
        Bout = P.buf("hout")
        P.dma("pool", DMA(W[:], wout), writes=[BW])

        def load(i):
            t0, nt, s = tiles[i]
            P.dma("sp", DMA(hT[i % 2][:, :, :nt], hin["ap"][:, :, t0 - hin["off"]:t0 - hin["off"] + nt]), writes=[Bh[i % 2]])
            P.dma("sp", DMA(mT[i % 2][:, :, :nt], mo[:, :, t0 - mo_off:t0 - mo_off + nt]), writes=[Bm[i % 2]])

        load(0)
        for i, (t0, nt, s) in enumerate(tiles):
            if i + 1 < len(tiles):
                load(i + 1)
            h = hT[i % 2]
            bh = Bh[i % 2]
            m = mT[i % 2]
            gate = vec[:, l, s, 2, :]
            for dc in range(KC):
                p = po[dc % 2]
                bp = Bpo[dc % 2]
                for kc in range(KC):
                    P.op("pe", MM(p[:, :nt], W[:, kc, dc * 128:(dc + 1) * 128], m[:, kc, :nt],
                                  start=(kc == 0), stop=(kc == KC - 1)), reads=[BW, Bm[i % 2]], writes=[bp])
                P.op("dve", STT(h[:, dc, :nt], p[:, :nt], gate[:, dc:dc + 1], h[:, dc, :nt], ALU.mult, ALU.add),
                     reads=[bp, bh, Bvec], writes=[bh])
            P.dma("sp", DMA(hout["ap"][:, :, t0 - hout["off"]:t0 - hout["off"] + nt], h[:, :, :nt]), reads=[bh], acc=[Bout])
        P.wait_all("sp", [Bout])
    P.barrier()


def phase_hgrn(kb, d, hin, mo, vec, Bvec, ones, Bones, ident, Bident, xm1=None):
    P = kb.P
    P.phase = "hgrn"
    if xm1 is None:
        xm1 = kb.nc.dram_tensor("xm1_scratch", [128, KC, NTOK], BF16).ap()
    NT = 512
    NCH = NTOK // 128
    tiles = [(0, 256, 1)] + [(256 + 512 * i, 512, 0) for i in range(8)]
    with contextlib.ExitStack() as es0:
        lbv = kb.sb(es0, [128, 2, 8], F32)
        oml = kb.sb(es0, [128, 2, 8], F32)
        noml = kb.sb(es0, [128, 2, 8], F32)
        hgn = kb.sb(es0, [128, 1], F32)
        mfb = kb.sb(es0, [128, 2, 128], F32)
        cmask = kb.sb(es0, [128, NT], F32)
        Blb, Bhgn, Bmfb, Bcm = P.bufs(4, "hgc")
        with contextlib.ExitStack() as es:
            hT = [kb.sb(es, [128, KC, NT], F32) for _ in range(2)]
            Bh = P.bufs(2, "h")
            nb = NormBufs(kb, es, NT)
            xmT = [kb.sb(es, [128, KC, NT], BF16) for _ in range(2)]
            BxmT = P.bufs(2, "xmT")
            Bxm1 = P.buf("xm1")
            lraw = kb.sb(es, [128, 2, 2, 8], F32)
            Blraw = P.buf("lraw")
            P.dma("sp", DMA(lraw[:], d["lbl"]), writes=[Blraw])
            P.dma("sp", DMA(hgn[:], d["hgn"]), writes=[Bhgn])
            P.dma("sp", DMA(mfb[:], d["mfb"]), writes=[Bmfb])
            P.dma("sp", DMA(cmask[:], d["cmask"]), writes=[Bcm])
            P.op("dve", TT(lbv[:], lraw[:, :, 1, :], lraw[:, :, 0, :], ALU.subtract), reads=[Blraw], writes=[Blb])
            P.op("act", ACT(lbv[:], lbv[:], AF.Sigmoid), reads=[Blb], writes=[Blb])
            P.op("dve", TS(noml[:], lbv[:], 1.0, -1.0, ALU.mult, ALU.add), reads=[Blb], writes=[Blb])
            P.op("dve", TS(oml[:], lbv[:], -1.0, 1.0, ALU.mult, ALU.add), reads=[Blb], writes=[Blb])

            def load(i):
                t0, nt, s = tiles[i]
                P.dma("sp", DMA(hT[i % 2][:, :, :nt], hin[:, :, t0:t0 + nt]), writes=[Bh[i % 2]])
            load(0)
            for i, (t0, nt, s) in enumerate(tiles):
                if i + 1 < len(tiles):
                    load(i + 1)
                norm_mod(kb, nb, hT[i % 2], Bh[i % 2], nt, vec[:, 1, s, 0, :], vec[:, 1, s, 1, :], Bvec, ones, Bones,
                         xmT[i % 2], BxmT[i % 2])
                P.dma("sp", DMA(xm1[:, :, t0:t0 + nt], xmT[i % 2][:, :, :nt]), reads=[BxmT[i % 2]], acc=[Bxm1])
            P.wait_all("sp", [Bxm1])
        P.barrier()
        P.phase = "hgrn_heads"
        HT = 256
        htiles = [(0, 256, 1)] + [(256 + HT * i, HT, 0) for i in range(NLAT // HT)]
        with contextlib.ExitStack() as es:
            NS = 4
            Whs = [kb.sb(es, [128, KC, 5, 128], BF16) for _ in range(2)]
            BWhs = P.bufs(2, "Wh")
            xsb = [kb.sb(es, [128, KC, HT], BF16) for _ in range(NS)]
            Bxs = P.bufs(NS, "xs")
            qfb = [kb.sb(es, [128, NTOK], BF16) for _ in range(2)]
            kifb = [kb.sb(es, [128, NTOK], BF16) for _ in range(2)]
            kofb = [kb.sb(es, [128, NCH, 128], BF16) for _ in range(2)]
            ivt = kb.sb(es, [128, NCH, 128], BF16)
            sg = kb.sb(es, [128, NLAT], BF16)
            etfb = [kb.sb(es, [128, NCH], F32) for _ in range(2)]
            SAll = [kb.sb(es, [128, NCH, 128], BF16) for _ in range(2)]
            Sst = [kb.sb(es, [128, 128], F32) for _ in range(2)]
            oh = [kb.sb(es, [128, NT], BF16) for _ in range(2)]
            Bq = P.bufs(2, "q_in")
            Bk = P.bufs(2, "k_in")
            Bko = P.bufs(2, "k_out")
            Bet = P.bufs(2, "et")
            BSAll = P.bufs(2, "SAll")
            BS = P.bufs(2, "S")
            Biv, Bsg = P.bufs(2, "hh")
            Boh = P.bufs(2, "oh")
            Bmo = P.buf("mo")
            def mk(n, dt=F32):
                return [[kb.sb(es, [128, HT], dt) for _ in range(2)] for _ in range(n)]
            sig, lf, kk, cum = mk(NS), mk(NS), mk(NS), mk(NS)
            kot = mk(NS, BF16)
            Bsig = [P.bufs(2, "sig") for _ in range(NS)]
            Blf = [P.bufs(2, "lf") for _ in range(NS)]
            Bkk = [P.bufs(2, "kk") for _ in range(NS)]
            Bcum = [P.bufs(2, "cum") for _ in range(NS)]
            Bkot = [P.bufs(2, "kot") for _ in range(NS)]
            t1 = [kb.sb(es, [128, HT], F32) for _ in range(NS)]
            qs = [kb.sb(es, [128, HT], F32) for _ in range(NS)]
            Bt1 = P.bufs(NS, "t1")
            Bqs = P.bufs(NS, "qs")
            aT = [kb.sb(es, [128, 2, 128], BF16) for _ in range(2)]
            BaT = P.bufs(2, "aT")
            ot = kb.sb(es, [128, NT], F32)
            orstd = kb.sb(es, [128, NT], F32)
            osq = kb.sb(es, [128, NT], BF16)
            Bot, Borstd, Bosq = P.bufs(3, "ho")
            pq = kb.ps(es, [128, NT], F32)
            pi = kb.ps(es, [128, 4, 128], F32)
            pgs = [[kb.ps(es, [128, NT], F32) for _ in range(2)] for _ in range(2)]
            ptr = [kb.ps(es, [128, 4, 128], BF16) for _ in range(2)]
            Bpq, Bpi = P.bufs(2, "hp")
            Bpgs = [P.bufs(2, "pg") for _ in range(2)]
            Bptr = P.bufs(2, "ptr")
            patt = [pgs[0][0][:, 0:256].rearrange("p (a t) -> p a t", a=2), pgs[0][1][:, 0:256].rearrange("p (a t) -> p a t", a=2)]
            Bpatt = [Bpgs[0][0], Bpgs[0][1]]
            psoL = [pgs[1][0], pq]
            BpsoL = [Bpgs[1][0], Bpq]
            puL = [pgs[1][1][:, 0:128], pi[:, 0, :]]
            BpuL = [Bpgs[1][1], Bpi]

            def interleave(gens, width, slots=2):
                active = []
                it = iter(gens)
                pending = None
                more = True
                while True:
                    while more and len(active) < width:
                        if pending is None:
                            try:
                                pending = next(it)
                            except StopIteration:
                                more = False
                                break
                        if any(t_ <= pending[0] - slots for t_, _ in active):
                            break
                        active.append(pending)
                        pending = None
                    if not active:
                        break
                    for item in list(active):
                        try:
                            next(item[1])
                        except StopIteration:
                            active.remove(item)

            for h in range(8):
                P.phase = "hg_tile"
                if h == 0:
                    P.dma("pool", DMA(Whs[0][:], d["hgw"][0]), writes=[BWhs[0]])
                if h + 1 < 8:
                    P.dma("pool", DMA(Whs[(h + 1) % 2][:], d["hgw"][h + 1]), writes=[BWhs[(h + 1) % 2]])
                W = Whs[h % 2]
                bW = BWhs[h % 2]

                def prelude(i, t0, nt, s):
                    sl = i % NS
                    nblk = nt // 128
                    c0 = t0 // 128
                    xs = xsb[sl]
                    BxmAll = Bxs[sl]
                    P.dma("sp", DMA(xs[:, :, :nt], xm1[:, :, t0:t0 + nt]), writes=[Bxs[sl]])
                    for kc in range(KC):
                        P.op("pe", MM(pq[:, :nt], W[:, kc, 0, :], xs[:, kc, :], start=(kc == 0), stop=(kc == KC - 1)),
                             reads=[bW, BxmAll], writes=[Bpq])
                    P.op("act", ACT(qs[sl][:, :nt], pq[:, :nt], AF.Copy), reads=[Bpq], writes=[Bqs[sl]])
                    yield
                    for b in range(nblk):
                        for kc in range(KC):
                            P.op("pe", MM(pi[:, b, :], xs[:, kc, b * 128:(b + 1) * 128], W[:, kc, 3, :], start=(kc == 0), stop=(kc == KC - 1)),
                                 reads=[bW, BxmAll], writes=[Bpi])
                    P.op("act", ACT(ivt[:, c0:c0 + nblk, :], pi[:, :nblk, :], AF.Copy), reads=[Bpi], writes=[Biv])
                    yield
                    if s == 0:
                        for kc in range(KC):
                            P.op("pe", MM(pq[:, :nt], W[:, kc, 4, :], xs[:, kc, :], start=(kc == 0), stop=(kc == KC - 1)),
                                 reads=[bW, BxmAll], writes=[Bpq])
                        P.op("act", ACT(sg[:, t0 - NCTX:t0 - NCTX + nt], pq[:, :nt], AF.Silu), reads=[Bpq], writes=[Bsg])
                        yield

                def chain(i, t0, nt, s, dr):
                    sl = i % NS
                    nblk = nt // 128
                    c0 = t0 // 128
                    xs = xsb[sl]
                    BxmAll = Bxs[sl]
                    p = pgs[i % 2][dr]
                    bp = Bpgs[i % 2][dr]
                    sg_, lf_, kk_, cum_, kot_ = sig[sl][dr], lf[sl][dr], kk[sl][dr], cum[sl][dr], kot[sl][dr]
                    bsg, blf, bkk, bcum, bkot = Bsig[sl][dr], Blf[sl][dr], Bkk[sl][dr], Bcum[sl][dr], Bkot[sl][dr]
                    for kc in range(KC):
                        P.op("pe", MM(p[:, :nt], W[:, kc, 1 + dr, :], xs[:, kc, :], start=(kc == 0), stop=(kc == KC - 1)),
                             reads=[bW, BxmAll], writes=[bp])
                    P.op("act", ACT(sg_[:, :nt], p[:, :nt], AF.Sigmoid), reads=[bp], writes=[bsg])
                    yield
                    P.op("act", ACT(lf_[:, :nt], sg_[:, :nt], AF.Ln, bias=lbv[:, dr, h:h + 1], scale=oml[:, dr, h:h + 1]),
                         reads=[bsg, Blb], writes=[blf])
                    P.op("dve", TS(kk_[:, :nt], sg_[:, :nt], noml[:, dr, h:h + 1], oml[:, dr, h:h + 1], ALU.mult, ALU.add),
                         reads=[bsg, Blb], writes=[bkk])
                    yield
                    P.op("dve", SCAN(cum_[:, :nt], cmask[:, :nt], lf_[:, :nt], 0.0, ALU.mult, ALU.add), reads=[blf, Bcm], writes=[bcum])
                    yield
                    cum3 = cum_[:, :nt].rearrange("p (c t) -> p c t", t=128)
                    et = etfb[dr]
                    P.op("act", ACT(et[:, c0:c0 + nblk], cum3[:, :, 127], AF.Exp), reads=[bcum], writes=[Bet[dr]])
                    if dr == 0:
                        cq = cum_[:, :nt]
                        bcq = bcum
                    else:
                        totb = cum3[:, :, 127:128].broadcast_to([128, nblk, 128])
                        t13 = t1[sl][:, :nt].rearrange("p (c t) -> p c t", t=128)
                        lf3 = lf_[:, :nt].rearrange("p (c t) -> p c t", t=128)
                        P.op("pool", TT(t13, totb, cum3, ALU.subtract), reads=[bcum], writes=[Bt1[sl]])
                        P.op("pool", TT(t13, t13, lf3, ALU.add), reads=[Bt1[sl], blf], writes=[Bt1[sl]])
                        cq = t1[sl][:, :nt]
                        bcq = Bt1[sl]
                    yield
                    P.op("act", ACT(lf_[:, :nt], cq, AF.Exp), reads=[bcq, blf], writes=[blf])
                    P.op("act", ACT(sg_[:, :nt], cq, AF.Exp, scale=-1.0), reads=[bcq, bsg], writes=[bsg])
                    yield
                    P.op("dve", TT(qfb[dr][:, t0:t0 + nt], qs[sl][:, :nt], lf_[:, :nt], ALU.mult), reads=[Bqs[sl], blf], writes=[Bq[dr]])
                    P.op("dve", TT(kk_[:, :nt], kk_[:, :nt], sg_[:, :nt], ALU.mult), reads=[bkk, bsg], writes=[bkk])
                    yield
                    P.op("act", ACT(kifb[dr][:, t0:t0 + nt], kk_[:, :nt], AF.Copy), reads=[bkk], writes=[Bk[dr]])
                    P.op("dve", TT(kot_[:, :nt].rearrange("p (c t) -> p c t", t=128), kk_[:, :nt].rearrange("p (c t) -> p c t", t=128),
                                    et[:, c0:c0 + nblk].unsqueeze(2).broadcast_to([128, nblk, 128]), ALU.mult),
                         reads=[bkk, Bet[dr]], writes=[bkot])
                    yield
                    for b in range(nblk):
                        P.op("pe", TR(ptr[dr][:, b, :], kot_[:, b * 128:(b + 1) * 128], ident[:]), reads=[bkot, Bident], writes=[Bptr[dr]])
                    P.op("dve", CP(kofb[dr][:, c0:c0 + nblk, :], ptr[dr][:, :nblk, :]), reads=[Bptr[dr]], writes=[Bko[dr]])
                    yield

                def all_chains():
                    for i, (t0, nt, s) in enumerate(htiles):
                        yield (i, prelude(i, t0, nt, s))
                        yield (i, chain(i, t0, nt, s, 0))
                        yield (i, chain(i, t0, nt, s, 1))
                interleave(all_chains(), 2 * NS if HG_WIDTH > 1 else 1, slots=NS)
                P.phase = "hg_state"
                for dr in range(2):
                    P.op("pool", MSET(Sst[dr][:], 0.0), writes=[BS[dr]])
                orderF = list(range(NCH))
                orderB = [1, 0] + list(range(NCH - 1, 1, -1))
                for n in range(NCH):
                    for dr, c in ((0, orderF[n]), (1, orderB[n])):
                        P.op("act", ACT(SAll[dr][:, c, :], Sst[dr][:], AF.Copy), reads=[BS[dr]], writes=[BSAll[dr]])
                        if n == NCH - 1:
                            continue
                        u = puL[dr]
                        P.op("pe", MM(u, kofb[dr][:, c, :], ivt[:, c, :]), reads=[Bko[dr], Biv], writes=[BpuL[dr]])
                        P.op("dve", STT(Sst[dr][:], Sst[dr][:], etfb[dr][:, c:c + 1], u, ALU.mult, ALU.add),
                             reads=[BS[dr], Bet[dr], BpuL[dr]], writes=[BS[dr]])
                P.phase = "hg_out"

                def att(c):
                    cs = slice(c * 128, (c + 1) * 128)
                    pa_ = patt[c % 2]
                    P.op("pe", MM(pa_[:, 0, :], kifb[0][:, cs], qfb[0][:, cs]), reads=[Bk[0], Bq[0]], writes=[Bpatt[c % 2]])
                    P.op("pe", MM(pa_[:, 1, :], kifb[1][:, cs], qfb[1][:, cs]), reads=[Bk[1], Bq[1]], writes=[Bpatt[c % 2]])
                    P.op("dve", TT(aT[c % 2][:], pa_, mfb[:], ALU.mult), reads=[Bpatt[c % 2], Bmfb], writes=[BaT[c % 2]])

                att(2)
                for c in range(2, NCH):
                    if c + 1 < NCH:
                        att(c + 1)
                    cs = slice(c * 128, (c + 1) * 128)
                    lt = c - 2
                    ti = lt // 4
                    j = lt % 4
                    a = aT[c % 2]
                    o = psoL[ti % 2]
                    bo = BpsoL[ti % 2]
                    oc = o[:, j * 128:(j + 1) * 128]
                    P.op("pe", MM(oc, ivt[:, c, :], a[:, 0, :], start=True, stop=False), reads=[Biv, BaT[c % 2]], writes=[bo])
                    P.op("pe", MM(oc, ivt[:, c, :], a[:, 1, :], start=False, stop=False), reads=[Biv, BaT[c % 2]], writes=[bo])
                    P.op("pe", MM(oc, SAll[0][:, c, :], qfb[0][:, cs], start=False, stop=False), reads=[BSAll[0], Bq[0]], writes=[bo])
                    P.op("pe", MM(oc, SAll[1][:, c, :], qfb[1][:, cs], start=False, stop=True), reads=[BSAll[1], Bq[1]], writes=[bo])
                    if j == 3:
                        tl = ti * 512
                        P.op("act", ACT(ot[:], o[:], AF.Copy), reads=[bo], writes=[Bot])
                        P.op("act", ACT(osq[:], ot[:], AF.Square), reads=[Bot], writes=[Bosq])
                        P.op("pe", MM(o[:], ones[:], osq[:]), reads=[Bosq, Bones], writes=[bo])
                        P.op("act", ACT(orstd[:], o[:], AF.Ln, bias=EPS, scale=1.0 / 128), reads=[bo], writes=[Borstd])
                        P.op("act", ACT(orstd[:], orstd[:], AF.Exp, scale=-0.5), reads=[Borstd], writes=[Borstd])
                        P.op("dve", STT(ot[:], ot[:], hgn[:, 0:1], orstd[:], ALU.mult, ALU.mult), reads=[Bot, Borstd, Bhgn], writes=[Bot])
                        ob = oh[ti % 2]
                        P.op("pool", TT(ob[:], ot[:], sg[:, tl:tl + 512], ALU.mult), reads=[Bot, Bsg], writes=[Boh[ti % 2]])
                        P.dma("sp", DMA(mo[:, h, tl:tl + 512], ob[:]), reads=[Boh[ti % 2]], acc=[Bmo])
            P.wait_all("sp", [Bmo])
    P.barrier()


LN8 = float(np.log(0.125))
HG_WIDTH = 6
L0_STOP = 99
L0_NT = 99


def phase_l0_proj(kb, d, hin, Qr, Kr, Ktm, Vtm, Bqkv, UT, SG, vec, Bvec, ones, Bones, ident, Bident, side=None):
    P = kb.P
    P.phase = "l0proj"
    NT = 256
    tiles = [(0, 256, 1)] + [(256 + NT * i, NT, 0) for i in range(NLAT // NT)]
    tiles = tiles[:L0_NT]
    with contextlib.ExitStack() as es:
        W = kb.sb(es, [128, KC, 2560], BF16)
        BW = P.bufs(5, "abw")
        hT = [kb.sb(es, [128, KC, NT], F32) for _ in range(2)]
        Bh = P.bufs(2, "h")
        xm = [kb.sb(es, [128, KC, NT], BF16) for _ in range(2)]
        Bxm = P.bufs(2, "xm")
        nb = [NormBufs(kb, es, NT) for _ in range(2)]
        cs = [kb.sb(es, [128, 2, NT], F32) for _ in range(2)]
        Bcs = P.bufs(2, "cs")
        t1 = [kb.sb(es, [128, NT], F32) for _ in range(2)]
        t2 = [kb.sb(es, [128, NT], F32) for _ in range(2)]
        Bt1 = P.bufs(2, "rt1")
        Bt2 = P.bufs(2, "rt2")
        ust = [kb.sb(es, [128, 4, NT], BF16) for _ in range(2)]
        gst = [kb.sb(es, [128, 4, NT], BF16) for _ in range(2)]
        Bust = P.bufs(2, "ust")
        Bgst = P.bufs(2, "gst")
        pq = kb.ps(es, [128, 2, NT], F32)
        ptr = kb.ps(es, [128, 2, 2, 128], BF16)
        pv = kb.ps(es, [128, 512], F32)
        pu_l = [kb.ps(es, [128, NT], F32) for _ in range(2)]
        Bpq, Bptr, Bpv = P.bufs(3, "pp")
        Bpu = P.bufs(2, "pu")
        Bsc = P.buf("scr")
        for i in range(5):
            P.dma("pool", DMA(W[:, :, 512 * i:512 * (i + 1)], d["abw"][:, :, 512 * i:512 * (i + 1)]), writes=[BW[i]])
        rot = W[:, :, 2048:2560].rearrange("p k (h x) -> p k h x", x=64)
        P.op("dve", TS(rot[:, :, :, 0:32], rot[:, :, :, 0:32], -1.0, 0.0, ALU.mult, ALU.add), reads=[BW[4]], writes=[BW[4]])

        def wb(col):
            return BW[col // 512]

        def tile_chain(i):
            t0, nt, s = tiles[i]
            sl = i % 2
            P.dma("sp", DMA(hT[sl][:, :, :nt], hin[:, :, t0:t0 + nt]), writes=[Bh[sl]])
            if s == 0:
                P.dma("sp", DMA(cs[sl][:, :, :nt], d["rope"][:, :, t0 - NCTX:t0 - NCTX + nt]), writes=[Bcs[sl]])
            yield
            c0 = t0 // 128
            nblk = nt // 128
            xm_ = xm[sl]
            bxm = Bxm[sl]
            norm_stage_a(kb, nb[sl], hT[sl], Bh[sl], nt)
            yield
            norm_stage_b(kb, nb[sl], nt, ones, Bones)
            yield
            G_, S_ = vec[:, 0, s, 0, :], vec[:, 0, s, 1, :]
            for kc in range(KC):
                t_ = nb[sl].tmp[kc % 2]
                bt_ = nb[sl].Btmp[kc % 2]
                P.op("dve", STT(t_[:, :nt], hT[sl][:, kc, :nt], G_[:, kc:kc + 1], nb[sl].rstd[:, :nt], ALU.mult, ALU.mult),
                     reads=[Bh[sl], nb[sl].Brstd, Bvec], writes=[bt_])
                P.op("act", ACT(xm_[:, kc, :nt], t_[:, :nt], AF.Identity, bias=S_[:, kc:kc + 1]), reads=[bt_, Bvec], writes=[bxm])
                if kc % 2 == 1:
                    yield
            cst = cs[sl]
            for (dst, base, rbase) in ((Qr, 0, 2048), (Kr, 256, 2304)):
                for pair in range(2):
                    col = base + 128 * pair
                    rcol = rbase + 128 * pair
                    for kc in range(KC):
                        P.op("pe", MM(pq[:, 0, :nt], W[:, kc, col:col + 128], xm_[:, kc, :nt], start=(kc == 0), stop=(kc == KC - 1)),
                             reads=[wb(col), bxm], writes=[Bpq])
                    if s == 0:
                        for kc in range(KC):
                            P.op("pe", MM(pq[:, 1, :nt], W[:, kc, rcol:rcol + 128], xm_[:, kc, :nt], start=(kc == 0), stop=(kc == KC - 1)),
                                 reads=[wb(rcol), bxm], writes=[Bpq])
                        P.op("dve", TT(t1[sl][:, :nt], pq[:, 0, :nt], cst[:, 0, :nt], ALU.mult), reads=[Bpq, Bcs[sl]], writes=[Bt1[sl]])
                        P.op("dve", TT(t2[sl][:, :nt], pq[:, 1, :nt], cst[:, 1, :nt], ALU.mult), reads=[Bpq, Bcs[sl]], writes=[Bt2[sl]])
                        yield
                        P.op("pool", TT(dst[:, pair, t0:t0 + nt], t1[sl][:, :nt], t2[sl][:, :nt], ALU.add), reads=[Bt1[sl], Bt2[sl]], writes=[Bqkv])
                    else:
                        P.op("act", ACT(dst[:, pair, t0:t0 + nt], pq[:, 0, :nt], AF.Copy), reads=[Bpq], writes=[Bqkv])
                        yield
            for b in range(nblk):
                for pair in range(2):
                    P.op("pe", TR(ptr[:, b, pair, :], Kr[:, pair, t0 + b * 128:t0 + (b + 1) * 128], ident[:]), reads=[Bqkv, Bident], writes=[Bptr])
            P.op("dve", CP(Ktm[:, c0:c0 + nblk, :].rearrange("p c (a x) -> p c a x", a=2), ptr[:, :nblk]), reads=[Bptr], writes=[Bqkv])
            yield
            for b in range(nblk):
                for kc in range(KC):
                    P.op("pe", MM(pv[:], xm_[:, kc, b * 128:(b + 1) * 128], W[:, kc, 512:1024], start=(kc == 0), stop=(kc == KC - 1)),
                         reads=[BW[1], bxm], writes=[Bpv])
                P.op("act", ACT(Vtm[:, c0 + b, :], pv[:], AF.Copy), reads=[Bpv], writes=[Bqkv])
                yield
            us = ust[sl]
            gs = gst[sl]
            for j in range(4):
                pp = pu_l[j % 2][:, :nt]
                col = 1024 + 128 * j
                for kc in range(KC):
                    P.op("pe", MM(pp, W[:, kc, col:col + 128], xm_[:, kc, :nt], start=(kc == 0), stop=(kc == KC - 1)),
                         reads=[wb(col), bxm], writes=[Bpu[j % 2]])
                P.op("act", ACT(us[:, j, :nt], pp, AF.Copy), reads=[Bpu[j % 2]], writes=[Bust[sl]])
                yield
            P.dma("sp", DMA(UT[:, :, t0:t0 + nt], us[:, :, :nt]), reads=[Bust[sl]], acc=[Bsc])
            for j in range(4):
                pp = pu_l[j % 2][:, :nt]
                col = 1536 + 128 * j
                for kc in range(KC):
                    P.op("pe", MM(pp, W[:, kc, col:col + 128], xm_[:, kc, :nt], start=(kc == 0), stop=(kc == KC - 1)),
                         reads=[wb(col), bxm], writes=[Bpu[j % 2]])
                P.op("act", ACT(gs[:, j, :nt], pp, AF.Silu), reads=[Bpu[j % 2]], writes=[Bgst[sl]])
                yield
            P.dma("sp", DMA(SG[:, :, t0:t0 + nt], gs[:, :, :nt]), reads=[Bgst[sl]], acc=[Bsc])
            if side is not None:
                for _ in range(2):
                    try:
                        next(side)
                    except StopIteration:
                        pass
            yield

        interleave(((i, tile_chain(i)) for i in range(len(tiles))), 2, slots=2)
        if side is not None:
            for _ in side:
                pass
        P.wait_all("sp", [Bsc])
    P.barrier()


def phase_ret(kb, d, Qr, Kr, Ktm, Vtm, Bqkv, SG, mo, ones, Bones, mfb, Bmfb):
    P = kb.P
    P.phase = "ret"
    NCH = NTOK // 128
    with contextlib.ExitStack() as es:
        retl = kb.sb(es, [128, 8], F32)
        retlP = kb.sb(es, [128, 2, 2], F32)
        cdP = kb.sb(es, [128, 2, 2], F32)
        dF = kb.sb(es, [128, 2, 128], F32)
        posc = kb.sb(es, [128, 2, 128], F32)
        jcol = kb.sb(es, [128, 2, 4], F32)
        hm = kb.sb(es, [128, 4], F32)
        Bhm = P.buf("hm")
        P.dma("sp", DMA(hm[:], d["hm"]), writes=[Bhm])
        qzb = [kb.sb(es, [128, 4, 128], BF16) for _ in range(2)]
        Bqz = P.bufs(2, "qz")
        Dm = kb.sb(es, [128, 4, 128], F32)
        QD = kb.sb(es, [128, 2, 2, 128], F32)
        kd = kb.sb(es, [128, 2, 4], F32)
        e1 = kb.sb(es, [128, 2, 128], F32)
        Bc = P.buf("retc")
        Bin = P.bufs(5, "retin")
        P.dma("sp", DMA(retl[:], d["retl"]), writes=[Bin[0]])
        P.dma("sp", DMA(retlP[:], d["retlP"]), writes=[Bin[1]])
        P.dma("sp", DMA(dF[:], d["dF"]), writes=[Bin[2]])
        P.dma("sp", DMA(posc[:], d["posc"]), writes=[Bin[3]])
        P.dma("sp", DMA(jcol[:], d["jcol"]), writes=[Bin[4]])
        P.op("act", ACT(retl[:], retl[:], AF.Sigmoid), reads=[Bin[0]], writes=[Bin[0]])
        P.op("act", ACT(retlP[:], retlP[:], AF.Sigmoid), reads=[Bin[1]], writes=[Bin[1]])
        P.op("act", ACT(retl[:], retl[:], AF.Ln), reads=[Bin[0]], writes=[Bin[0]])
        P.op("act", ACT(retlP[:], retlP[:], AF.Ln), reads=[Bin[1]], writes=[Bin[1]])
        for h in range(4):
            P.op("act", ACT(e1[:, 0, :], dF[:, 0, :], AF.Exp, bias=LN8, scale=retl[:, h:h + 1]), reads=[Bin[0], Bin[2], Bc], writes=[Bc])
            P.op("act", ACT(e1[:, 1, :], dF[:, 1, :], AF.Exp, bias=LN8, scale=retl[:, 4 + h:5 + h]), reads=[Bin[0], Bin[2], Bc], writes=[Bc])
            P.op("dve", TT(e1[:], e1[:], mfb[:], ALU.mult), reads=[Bc, Bmfb], writes=[Bc])
            P.op("dve", TT(Dm[:, h, :], e1[:, 0, :], e1[:, 1, :], ALU.add), reads=[Bc], writes=[Bc])
        for dr in range(2):
            for pair in range(2):
                P.op("act", ACT(QD[:, dr, pair, :], posc[:, dr, :], AF.Exp, scale=retlP[:, dr, pair:pair + 1]), reads=[Bin[1], Bin[3], Bc], writes=[Bc])
            P.op("dve", TT(kd[:, dr, :], jcol[:, dr, :], retl[:, 4 * dr:4 * dr + 4], ALU.mult), reads=[Bin[0], Bin[4], Bc], writes=[Bc])
        P.op("act", ACT(kd[:], kd[:], AF.Exp, bias=LN8), reads=[Bc], writes=[Bc])
        P.op("act", ACT(cdP[:], retlP[:], AF.Exp, scale=128.0), reads=[Bin[1], Bc], writes=[Bc])
        SAll = [kb.sb(es, [128, NCH, 2, 128], BF16) for _ in range(2)]
        Sst = [kb.sb(es, [128, 2, 128], F32) for _ in range(2)]
        kdt = [kb.sb(es, [128, 4, 64], BF16) for _ in range(2)]
        aT = [kb.sb(es, [128, 4, 128], BF16) for _ in range(2)]
        qfb = [kb.sb(es, [128, 2, 2, 128], BF16) for _ in range(2)]
        sgt = [kb.sb(es, [128, 4, 128], BF16) for _ in range(2)]
        ot = [kb.sb(es, [128, 4, 128], F32) for _ in range(2)]
        osq = [kb.sb(es, [128, 4, 128], BF16) for _ in range(2)]
        orstd = [kb.sb(es, [128, 4, 128], F32) for _ in range(2)]
        ob = [kb.sb(es, [128, 4, 128], BF16) for _ in range(2)]
        BSAll = P.bufs(2, "SAll")
        BS = P.bufs(2, "S")
        Bkdt = P.bufs(2, "kdt")
        BaT = P.bufs(2, "aT")
        Bqfb = P.bufs(2, "qfb")
        Bsgt = P.bufs(2, "sgt")
        Bot = P.bufs(2, "ot")
        Bosq = P.bufs(2, "osq")
        Borstd = P.bufs(2, "orstd")
        Bob = P.bufs(2, "ob")
        Bmo = P.buf("mo")
        pu_r = [kb.ps(es, [128, 2, 128], F32) for _ in range(2)]
        patt = [kb.ps(es, [128, 4, 128], F32) for _ in range(2)]
        po = [kb.ps(es, [128, 4, 128], F32) for _ in range(2)]
        pss0 = kb.ps(es, [128, 4, 128], F32)
        pss = [pss0, pss0]
        Bpu = P.bufs(2, "pu")
        Bpatt = P.bufs(2, "patt")
        Bpo = P.bufs(2, "po")
        Bpss0 = P.buf("pss")
        Bpss = [Bpss0, Bpss0]
        for dr in range(2):
            P.op("pool", MSET(Sst[dr][:], 0.0), writes=[BS[dr]])
        orderF = list(range(NCH))
        orderB = [1, 0] + list(range(NCH - 1, 1, -1))
        for n in range(NCH):
            for dr, c in ((0, orderF[n]), (1, orderB[n])):
                S = Sst[dr]
                P.op("act", ACT(SAll[dr][:, c], S[:], AF.Copy), reads=[BS[dr]], writes=[BSAll[dr]])
                if n == NCH - 1:
                    continue
                k_ = kdt[dr]
                P.op("pool", TT(k_[:], Ktm[:, c, :].rearrange("p (h x) -> p h x", h=4),
                                kd[:, dr, :].unsqueeze(2).broadcast_to([128, 4, 64]), ALU.mult), reads=[Bqkv, Bc], writes=[Bkdt[dr]])
                u = pu_r[dr]
                for h in range(4):
                    hp = (h % 2) * 64
                    P.op("pe", MM(u[hp:hp + 64, h // 2, :], k_[:, h, :], Vtm[:, c, h * 128:(h + 1) * 128]), reads=[Bkdt[dr], Bqkv], writes=[Bpu[dr]])
                P.op("dve", TT(S[:], S[:], cdP[:, dr, :].unsqueeze(2).broadcast_to([128, 2, 128]), ALU.mult), reads=[BS[dr], Bc], writes=[BS[dr]])
                P.op("dve", TT(S[:], S[:], u[:], ALU.add), reads=[BS[dr], Bpu[dr]], writes=[BS[dr]])

        def out_chain(c):
            sl = c % 2
            cs_ = slice(c * 128, (c + 1) * 128)
            P.dma("sp", DMA(sgt[sl][:], SG[:, :, cs_]), writes=[Bsgt[sl]])
            qz = qzb[sl]
            P.op("pool", TT(qz[:].rearrange("p (a b) i -> p a b i", a=2), Qr[:, :, cs_].unsqueeze(2).broadcast_to([128, 2, 2, 128]),
                            hm[:].rearrange("p (a b) -> p a b", a=2).unsqueeze(3).broadcast_to([128, 2, 2, 128]), ALU.mult),
                 reads=[Bqkv, Bhm], writes=[Bqz[sl]])
            qq = qfb[sl]
            P.op("pool", TT(qq[:], Qr[:, :, cs_].unsqueeze(1).broadcast_to([128, 2, 2, 128]), QD[:], ALU.mult), reads=[Bqkv, Bc], writes=[Bqfb[sl]])
            yield
            pa_ = patt[sl]
            for h in range(4):
                P.op("pe", MM(pa_[:, h, :], Kr[:, h // 2, cs_], qz[:, h, :]), reads=[Bqkv, Bqz[sl]], writes=[Bpatt[sl]])
            yield
            a = aT[sl]
            P.op("dve", TT(a[:], pa_[:], Dm[:], ALU.mult), reads=[Bpatt[sl], Bc], writes=[BaT[sl]])
            yield
            o = po[sl]
            for h in range(4):
                hp = (h % 2) * 64
                P.op("pe", MM(o[:, h, :], Vtm[:, c, h * 128:(h + 1) * 128], a[:, h, :], start=True, stop=False), reads=[Bqkv, BaT[sl]], writes=[Bpo[sl]])
                P.op("pe", MM(o[:, h, :], SAll[0][hp:hp + 64, c, h // 2, :], qq[hp:hp + 64, 0, h // 2, :], start=False, stop=False),
                     reads=[BSAll[0], Bqfb[sl]], writes=[Bpo[sl]])
                P.op("pe", MM(o[:, h, :], SAll[1][hp:hp + 64, c, h // 2, :], qq[hp:hp + 64, 1, h // 2, :], start=False, stop=True),
                     reads=[BSAll[1], Bqfb[sl]], writes=[Bpo[sl]])
            yield
            P.op("act", ACT(ot[sl][:], o[:], AF.Copy), reads=[Bpo[sl]], writes=[Bot[sl]])
            P.op("act", ACT(osq[sl][:], ot[sl][:], AF.Square), reads=[Bot[sl]], writes=[Bosq[sl]])
            yield
            P.op("pe", MM(pss[sl][:], ones[:], osq[sl][:]), reads=[Bosq[sl], Bones], writes=[Bpss[sl]])
            P.op("act", ACT(orstd[sl][:], pss[sl][:], AF.Ln, bias=EPS, scale=1.0 / 128), reads=[Bpss[sl]], writes=[Borstd[sl]])
            P.op("act", ACT(orstd[sl][:], orstd[sl][:], AF.Exp, scale=-0.5), reads=[Borstd[sl]], writes=[Borstd[sl]])
            yield
            P.op("dve", TT(ot[sl][:], ot[sl][:], orstd[sl][:], ALU.mult), reads=[Bot[sl], Borstd[sl]], writes=[Bot[sl]])
            yield
            P.op("pool", TT(ob[sl][:], ot[sl][:], sgt[sl][:], ALU.mult), reads=[Bot[sl], Bsgt[sl]], writes=[Bob[sl]])
            P.dma("sp", DMA(mo[:, 0:4, cs_], ob[sl][:]), reads=[Bob[sl]], acc=[Bmo])
            yield

        interleave(((c, out_chain(c)) for c in range(NCH)), 2, slots=2)
        P.wait_all("sp", [Bmo])
    P.barrier()


TWO_PI = float(2.0 * np.pi)
NCK = NTOK // 8
S5_POW = (1, 2, 4, 6, 8, 16, 24, 32, 64, 96, 128, 256, 384, 512)
S5_LEVELS = ((1, (1,)), (2, (1, 2, 3)), (8, (1, 2, 3)), (32, (1, 2, 3)), (128, (1, 2, 3)), (512, (1,)))
NPOW = len(S5_POW)


def s5_scratch(nc):
    return {"Toep": nc.dram_tensor("s5Toep", [128, 64, 128], BF16).ap(), "Bz": nc.dram_tensor("s5Bz", [128, 64, 128], BF16).ap(),
            "Cy": nc.dram_tensor("s5Cy", [128, 64, 128], F32).ap(), "Vt": nc.dram_tensor("s5Vt", [128, 64, NPOW, 2], F32).ap()}


def s5_setup_gen(kb, d, S5M):
    P = kb.P
    with contextlib.ExitStack() as es0:
        Toep = kb.sb(es0, [128, 16, 128], BF16)
        Bz = kb.sb(es0, [128, 16, 128], BF16)
        Cy = kb.sb(es0, [128, 16, 128], F32)
        HR = kb.sb(es0, [128, 64, 14], F32)
        HI = kb.sb(es0, [128, 64, 14], F32)
        imat = kb.sb(es0, [128, 2, 128], F32)
        Vt = kb.sb(es0, [128, 64, NPOW, 2], F32)
        BToep, BBz, BCy, BH, Bimat, Bst = P.bufs(6, "s5p")
        P.dma("sp", DMA(imat[:], d["imat"]), writes=[Bimat])
        with contextlib.ExitStack() as es:
            a = kb.sb(es, [128, 2, 64], F32)
            dtl = kb.sb(es, [128, 64], F32)
            bb = kb.sb(es, [128, 2, 64, 16], F32)
            cc = kb.sb(es, [128, 2, 64, 16], F32)
            nvec = kb.sb(es, [128, 2, 16], F32)
            tmask = kb.sb(es, [128, 2, 128], F32)
            Bin = P.bufs(6, "s5in")
            P.dma("sp", DMA(a[:], d["s5a"]), writes=[Bin[0]])
            P.dma("sp", DMA(dtl[:], d["s5dt"]), writes=[Bin[1]])
            P.dma("sp", DMA(bb[:], d["s5b"]), writes=[Bin[2]])
            P.dma("sp", DMA(cc[:], d["s5c"]), writes=[Bin[3]])
            P.dma("sp", DMA(nvec[:], d["nvec"]), writes=[Bin[4]])
            P.dma("sp", DMA(tmask[:], d["tmask"]), writes=[Bin[5]])
            P.op("dve", TS(bb[0:64, 1], bb[0:64, 1], -1.0, 0.0, ALU.mult, ALU.add), reads=[Bin[2]], writes=[Bin[2]])
            P.op("dve", TS(cc[64:128, 0], cc[64:128, 0], -1.0, 0.0, ALU.mult, ALU.add), reads=[Bin[3]], writes=[Bin[3]])
            P.op("dve", TS(cc[:, 1], cc[:, 1], -1.0, 0.0, ALU.mult, ALU.add), reads=[Bin[3]], writes=[Bin[3]])
            yield
            rho = kb.sb(es, [128, 64], F32)
            th = kb.sb(es, [128, 64], F32)
            Bs = P.buf("s5s")
            P.op("act", ACT(dtl[:], dtl[:], AF.Exp), reads=[Bin[1]], writes=[Bin[1]])
            P.op("dve", TT(rho[:], a[:, 0, :], dtl[:], ALU.mult), reads=[Bin[0], Bin[1]], writes=[Bs])
            P.op("dve", TT(th[:], a[:, 1, :], dtl[:], ALU.mult), reads=[Bin[0], Bin[1], Bs], writes=[Bs])
            PwR = kb.sb(es, [128, 2, 64, 16], F32)
            PwI = kb.sb(es, [128, 2, 64, 16], F32)
            arg = kb.sb(es, [128, 64, 16], F32)
            r1 = kb.sb(es, [128, 64, 16], F32)
            r2 = kb.sb(es, [128, 64, 16], F32)
            ri = kb.sb(es, [128, 64, 16], I32)
            mg = kb.sb(es, [128, 64, 16], F32)
            for tb in range(2):
                nb_ = nvec[:, tb, :].unsqueeze(1).broadcast_to([128, 64, 16])
                P.op("dve", TT(arg[:], th[:].unsqueeze(2).broadcast_to([128, 64, 16]), nb_, ALU.mult), reads=[Bs, Bin[4]], writes=[Bs])
                P.op("dve", TT(mg[:], rho[:].unsqueeze(2).broadcast_to([128, 64, 16]), nb_, ALU.mult), reads=[Bs, Bin[4]], writes=[Bs])
                P.op("act", ACT(mg[:], mg[:], AF.Exp), reads=[Bs], writes=[Bs])
                for (off, dst) in ((0.0, PwI), (0.25, PwR)):
                    P.op("dve", TS(r1[:], arg[:], 1.0 / TWO_PI, off, ALU.mult, ALU.add), reads=[Bs], writes=[Bs])
                    P.op("dve", CP(ri[:], r1[:]), reads=[Bs], writes=[Bs])
                    P.op("dve", CP(r2[:], ri[:]), reads=[Bs], writes=[Bs])
                    P.op("dve", TT(r1[:], r1[:], r2[:], ALU.subtract), reads=[Bs], writes=[Bs])
                    P.op("act", ACT(r2[:], r1[:], AF.Sin, scale=TWO_PI), reads=[Bs], writes=[Bs])
                    P.op("dve", TT(dst[:, tb], r2[:], mg[:], ALU.mult), reads=[Bs], writes=[Bs])
                    yield
            yield
            lr = PwR[:, 0, :, 8]
            li = PwI[:, 0, :, 8]
            nr = kb.sb(es, [128, 64], F32)
            den = kb.sb(es, [128, 64], F32)
            fr = kb.sb(es, [128, 64], F32)
            fi = kb.sb(es, [128, 64], F32)
            tq = kb.sb(es, [128, 64], F32)
            P.op("dve", TS(nr[:], lr, 1.0, -1.0, ALU.mult, ALU.add), reads=[Bs], writes=[Bs])
            P.op("dve", TT(den[:], a[:, 0, :], a[:, 0, :], ALU.mult), reads=[Bs, Bin[0]], writes=[Bs])
            P.op("dve", TT(tq[:], a[:, 1, :], a[:, 1, :], ALU.mult), reads=[Bs, Bin[0]], writes=[Bs])
            P.op("dve", TT(den[:], den[:], tq[:], ALU.add), reads=[Bs], writes=[Bs])
            P.op("dve", RCP(den[:], den[:]), reads=[Bs], writes=[Bs])
            P.op("dve", TT(fr[:], nr[:], a[:, 0, :], ALU.mult), reads=[Bs, Bin[0]], writes=[Bs])
            P.op("dve", TT(tq[:], li, a[:, 1, :], ALU.mult), reads=[Bs, Bin[0]], writes=[Bs])
            P.op("dve", TT(fr[:], fr[:], tq[:], ALU.add), reads=[Bs], writes=[Bs])
            P.op("dve", TT(fr[:], fr[:], den[:], ALU.mult), reads=[Bs], writes=[Bs])
            P.op("dve", TT(fi[:], li, a[:, 0, :], ALU.mult), reads=[Bs, Bin[0]], writes=[Bs])
            P.op("dve", TT(tq[:], nr[:], a[:, 1, :], ALU.mult), reads=[Bs, Bin[0]], writes=[Bs])
            P.op("dve", TT(fi[:], fi[:], tq[:], ALU.subtract), reads=[Bs], writes=[Bs])
            P.op("dve", TT(fi[:], fi[:], den[:], ALU.mult), reads=[Bs], writes=[Bs])
            bbq = kb.sb(es, [128, 2, 64, 16], F32)
            t64 = kb.sb(es, [128, 64, 16], F32)
            frb = fr[:].unsqueeze(2).broadcast_to([128, 64, 16])
            fib = fi[:].unsqueeze(2).broadcast_to([128, 64, 16])
            P.op("dve", TT(bbq[:, 0], bb[:, 0], frb, ALU.mult), reads=[Bs, Bin[2]], writes=[Bs])
            P.op("dve", TT(t64[:], bb[:, 1], fib, ALU.mult), reads=[Bs, Bin[2]], writes=[Bs])
            P.op("dve", TT(bbq[:, 0], bbq[:, 0], t64[:], ALU.add), reads=[Bs], writes=[Bs])
            P.op("dve", TT(bbq[:, 1], bb[:, 1], frb, ALU.mult), reads=[Bs, Bin[2]], writes=[Bs])
            P.op("dve", TT(t64[:], bb[:, 0], fib, ALU.mult), reads=[Bs, Bin[2]], writes=[Bs])
            P.op("dve", TT(bbq[:, 1], bbq[:, 1], t64[:], ALU.subtract), reads=[Bs], writes=[Bs])
            yield
            pidx = {n: i for i, n in enumerate(S5_POW)}
            P.op("dve", CP(HR[:, :, 0], PwR[:, 0, :, 15]), reads=[Bs], writes=[BH])
            P.op("dve", CP(HI[:, :, 0], PwI[:, 0, :, 15]), reads=[Bs, BH], writes=[BH])
            for k in range(9):
                a_, b_ = pidx[1 << k], pidx[1 << (k + 1)]
                P.op("dve", TT(nr[:], HR[:, :, a_], HR[:, :, a_], ALU.mult), reads=[BH, Bs], writes=[Bs])
                P.op("dve", TT(tq[:], HI[:, :, a_], HI[:, :, a_], ALU.mult), reads=[BH, Bs], writes=[Bs])
                P.op("dve", TT(HR[:, :, b_], nr[:], tq[:], ALU.subtract), reads=[Bs, BH], writes=[BH])
                P.op("dve", STT(HI[:, :, b_], HR[:, :, a_], 2.0, HI[:, :, a_], ALU.mult, ALU.mult), reads=[BH], writes=[BH])
            for (x_, y_) in ((4, 2), (16, 8), (64, 32), (256, 128)):
                a_, b_, c_ = pidx[x_], pidx[y_], pidx[x_ + y_]
                P.op("dve", TT(nr[:], HR[:, :, a_], HR[:, :, b_], ALU.mult), reads=[BH, Bs], writes=[Bs])
                P.op("dve", TT(tq[:], HI[:, :, a_], HI[:, :, b_], ALU.mult), reads=[BH, Bs], writes=[Bs])
                P.op("dve", TT(HR[:, :, c_], nr[:], tq[:], ALU.subtract), reads=[Bs, BH], writes=[BH])
                P.op("dve", TT(nr[:], HR[:, :, a_], HI[:, :, b_], ALU.mult), reads=[BH, Bs], writes=[Bs])
                P.op("dve", TT(tq[:], HI[:, :, a_], HR[:, :, b_], ALU.mult), reads=[BH, Bs], writes=[Bs])
                P.op("dve", TT(HI[:, :, c_], nr[:], tq[:], ALU.add), reads=[Bs, BH], writes=[BH])
            yield
            P.op("dve", CP(Vt[0:64, :, :, 0], HR[0:64]), reads=[BH], writes=[BH])
            P.op("dve", TS(Vt[64:128, :, :, 0], HI[64:128], -1.0, 0.0, ALU.mult, ALU.add), reads=[BH], writes=[BH])
            P.op("dve", CP(Vt[0:64, :, :, 1], HI[0:64]), reads=[BH], writes=[BH])
            P.op("dve", CP(Vt[64:128, :, :, 1], HR[64:128]), reads=[BH], writes=[BH])
            T1 = kb.sb(es, [128, 16, 8, 16], F32)
            T2 = kb.sb(es, [128, 16, 8, 16], F32)
            Lm = kb.sb(es, [128, 16, 128], BF16)
            Rm = kb.sb(es, [128, 16, 128], BF16)
            Lz = kb.sb(es, [128, 16, 128], F32)
            pT = kb.ps(es, [128, 4, 128], F32)
            pB = kb.ps(es, [128, 4, 128], F32)
            BpT, BpB, BL = P.bufs(3, "s5m")
            spec = {0: dict(L=(1, 8), R=(0, 7), C=(0, 8), Z=(1, 1)),
                    1: dict(L=(0, 7), R=(1, 8), C=(1, 0), Z=(0, 7))}

            def build(dst, which, dd, g0, X0, X1, out_is_tensor=True):
                tb, st = spec[dd][which]
                gs = slice(dd * 32 + g0, dd * 32 + g0 + 16)
                pr = PwR[:, tb, gs, st:st + 8].unsqueeze(3).broadcast_to([128, 16, 8, 16])
                pi_ = PwI[:, tb, gs, st:st + 8].unsqueeze(3).broadcast_to([128, 16, 8, 16])
                x0 = X0[:, gs, :].unsqueeze(2).broadcast_to([128, 16, 8, 16])
                x1 = X1[:, gs, :].unsqueeze(2).broadcast_to([128, 16, 8, 16])
                P.op("dve", TT(T1[:], pr, x0, ALU.mult), reads=[Bs, Bin[2], Bin[3], BL], writes=[BL])
                P.op("pool", TT(T2[:], pi_, x1, ALU.mult), reads=[Bs, Bin[2], Bin[3], BL], writes=[BL])
                P.op("dve", TT(dst, T1[:].rearrange("p g s m -> p g (s m)"), T2[:].rearrange("p g s m -> p g (s m)"), ALU.add),
                     reads=[BL, BCy], writes=[BL, BCy])

            for dd in range(2):
                for g0 in (0, 16):
                    gd0 = dd * 32 + g0
                    build(Lm[:], "L", dd, g0, bbq[:, 0], bbq[:, 1])
                    build(Rm[:], "R", dd, g0, cc[:, 0], cc[:, 1])
                    for i4 in range(4):
                        for gi in range(4):
                            g = i4 * 4 + gi
                            P.op("pe", MM(pT[:, gi, :], Lm[:, g, :], Rm[:, g, :]), reads=[BL], writes=[BpT])
                        P.op("dve", TT(Toep[:, i4 * 4:i4 * 4 + 4, :], pT[:],
                                       tmask[:, dd, :].unsqueeze(1).broadcast_to([128, 4, 128]), ALU.mult),
                             reads=[BpT, Bin[5]], writes=[BToep])
                    build(Cy[:], "C", dd, g0, cc[:, 0], cc[:, 1])
                    build(Lz[:], "Z", dd, g0, bbq[:, 0], bbq[:, 1])
                    for i4 in range(4):
                        for gi in range(4):
                            g = i4 * 4 + gi
                            P.op("pe", TR(pB[:, gi, :], Lz[:, g, :], imat[:, 0, :]), reads=[BL, Bimat], writes=[BpB])
                        P.op("act", ACT(Bz[:, i4 * 4:i4 * 4 + 4, :], pB[:], AF.Copy), reads=[BpB], writes=[BBz])
                    P.dma("sp", DMA(S5M["Toep"][:, gd0:gd0 + 16, :], Toep[:]), reads=[BToep], acc=[Bst])
                    P.dma("sp", DMA(S5M["Bz"][:, gd0:gd0 + 16, :], Bz[:]), reads=[BBz], acc=[Bst])
                    P.dma("sp", DMA(S5M["Cy"][:, gd0:gd0 + 16, :], Cy[:]), reads=[BCy], acc=[Bst])
                    yield
            P.dma("sp", DMA(S5M["Vt"], Vt[:]), reads=[BH], acc=[Bst])
            P.wait_all("sp", [Bst])
    yield


def phase_s5_main(kb, d, UT, mo, S5M):
    P = kb.P
    with contextlib.ExitStack() as es0:
        Toep = kb.sb(es0, [128, 64, 128], BF16)
        Bz = kb.sb(es0, [128, 64, 128], BF16)
        Cy = kb.sb(es0, [128, 64, 128], F32)
        psel = kb.sb(es0, [128, 8, 240], BF16)
        imat = kb.sb(es0, [128, 2, 128], F32)
        Vt = kb.sb(es0, [128, 64, NPOW, 2], F32)
        BToep, BBz, BCy, BH, Bpsel, Bimat = P.bufs(6, "s5p")
        P.dma("pool", DMA(psel[:], d["psel"]), writes=[Bpsel])
        P.dma("sp", DMA(imat[:], d["imat"]), writes=[Bimat])
        P.dma("sp", DMA(Toep[:], S5M["Toep"]), writes=[BToep])
        P.dma("sp", DMA(Bz[:], S5M["Bz"]), writes=[BBz])
        P.dma("sp", DMA(Cy[:], S5M["Cy"]), writes=[BCy])
        P.dma("sp", DMA(Vt[:], S5M["Vt"]), writes=[BH])
        P.phase = "s5main"
        with contextlib.ExitStack() as es:
            ygT = kb.sb(es, [128, 4, NTOK], BF16)
            Bygt = P.buf("ygT")
            with contextlib.ExitStack() as es2:
                UTs = [kb.sb(es2, [128, NTOK], BF16) for _ in range(2)]
                BUTs = P.bufs(2, "UTs")
                Ug = kb.sb(es2, [128, NCK], BF16)
                usm = kb.sb(es2, [128, 8, NCK], BF16)
                Busm = P.buf("usm")
                H = [kb.sb(es2, [128, NCK], F32R) for _ in range(2)]
                Mall = [kb.sb(es2, [128, NPOW, 128], F32R) for _ in range(2)]
                Yall = kb.sb(es2, [128, 8, NCK], BF16)
                s5d = kb.sb(es2, [128, 4], F32)
                ytmp = kb.sb(es2, [128, 512], F32)
                yt2 = kb.sb(es2, [128, 512], F32)
                Byt2 = P.buf("yt2")
                BUg, BMtmp, BYall, Bs5d, Bytmp = P.bufs(5, "s5l")
                BHh = P.bufs(2, "H")
                BMall = P.bufs(2, "Mall")
                pA = [kb.ps(es2, [128, 512], F32) for _ in range(2)]
                pB2 = [kb.ps(es2, [128, 32], F32) for _ in range(2)]
                pU = [kb.ps(es2, [128, 512], F32) for _ in range(2)]
                pX = kb.ps(es2, [128, 512], F32)
                BpA = P.bufs(2, "pA")
                BpB2 = P.bufs(2, "pB2")
                BpU = P.bufs(2, "pU")
                BpX = P.buf("pX")
                P.dma("sp", DMA(s5d[:], d["s5d"]), writes=[Bs5d])
                dmat = kb.sb(es2, [128, 64], F32)
                P.op("dve", CP(dmat[0:64, :], imat[0:64, 0, 0:64]), reads=[Bimat], writes=[Bimat])
                P.op("dve", CP(dmat[64:128, :], imat[64:128, 0, 64:128]), reads=[Bimat], writes=[Bimat])
                P.dma("sp", DMA(UTs[0][:], UT[:, 0, :]), writes=[BUTs[0]])
                for J in range(4):
                    if J + 1 < 4:
                        P.dma("sp", DMA(UTs[(J + 1) % 2][:], UT[:, J + 1, :]), writes=[BUTs[(J + 1) % 2]])
                    uj = UTs[J % 2]
                    buj = BUTs[J % 2]
                    P.op("act", ACT(usm[:], uj[:].rearrange("p (c s) -> p s c", s=8), AF.Copy), reads=[buj], writes=[Busm])
                    for j in range(8):
                        g = 8 * J + j
                        for half in range(2):
                            c0 = 272 * half
                            for s in range(8):
                                P.op("pe", MM(pU[half][:, 0:272], psel[:, j, 112 - 16 * s:240 - 16 * s],
                                              usm[:, s, c0:c0 + 272], start=(s == 0), stop=(s == 7)),
                                     reads=[Bpsel, Busm], writes=[BpU[half]])
                            P.op("act", ACT(Ug[:, c0:c0 + 272], pU[half][:, 0:272], AF.Copy), reads=[BpU[half]], writes=[BUg])
                        for dd in range(2):
                            gd = dd * 32 + g
                            Mk = Mall[dd]
                            P.op("dve" if dd == 0 else "pool",
                                 TT(Mk[:].rearrange("p k (a c) -> p k a c", a=2),
                                    dmat[:].unsqueeze(1).unsqueeze(1).broadcast_to([128, NPOW, 2, 64]),
                                    Vt[:, gd, :, :].unsqueeze(3).broadcast_to([128, NPOW, 2, 64]), ALU.mult),
                                 reads=[Bimat, BH], writes=[BMall[dd]])
                            if dd == 0:
                                P.op("pe", MM(pA[dd][:, 0:512], Bz[:, gd, :], Ug[:, 0:512]), reads=[BBz, BUg], writes=[BpA[dd]])
                                P.op("pe", MM(pB2[dd][:, 0:32], Bz[:, gd, :], Ug[:, 512:544]), reads=[BBz, BUg], writes=[BpB2[dd]])
                            else:
                                P.op("pe", MM(pA[dd][:, 0:512], Bz[:, gd, :], Ug[:, 32:544]), reads=[BBz, BUg], writes=[BpA[dd]])
                                P.op("pe", MM(pB2[dd][:, 0:32], Bz[:, gd, :], Ug[:, 0:32]), reads=[BBz, BUg], writes=[BpB2[dd]])
                            P.op("dve", CP(H[dd][:, 0:512], pA[dd][:, 0:512]), reads=[BpA[dd]], writes=[BHh[dd]])
                            P.op("dve", CP(H[dd][:, 512:544], pB2[dd][:, 0:32]), reads=[BpB2[dd]], writes=[BHh[dd]])
                        pidx = {n: i for i, n in enumerate(S5_POW)}

                        def seg_mm(dd, mat, pa0, src0, n, first, last, plain):
                            cast = (lambda ap: ap.bitcast(F32)) if plain else (lambda ap: ap)
                            segs = []
                            if pa0 < 512:
                                na = min(n, 512 - pa0)
                                segs.append((pA[dd][:, pa0:pa0 + na], src0, na, BpA[dd]))
                                if n > na:
                                    segs.append((pB2[dd][:, 0:n - na], src0 + na, n - na, BpB2[dd]))
                            else:
                                segs.append((pB2[dd][:, pa0 - 512:pa0 - 512 + n], src0, n, BpB2[dd]))
                            for (out_, s0_, n_, bout) in segs:
                                P.op("pe", MM(out_, cast(mat), cast(H[dd][:, s0_:s0_ + n_]), start=first, stop=last),
                                     reads=[BMall[dd], BHh[dd]], writes=[bout])

                        for (st, mults) in S5_LEVELS:
                            for dd in range(2):
                                Hd = H[dd]
                                plain = (st == 1)
                                W_ = NCK - st
                                for ji, jm in enumerate(mults):
                                    sh = jm * st
                                    if sh >= NCK:
                                        continue
                                    n = NCK - sh
                                    mat = Mall[dd][:, pidx[sh], :]
                                    pa0 = (sh - st) if dd == 0 else 0
                                    src0 = 0 if dd == 0 else sh
                                    seg_mm(dd, mat, pa0, src0, n, ji == 0, ji == len(mults) - 1, plain)
                                do = st if dd == 0 else 0
                                n0 = min(W_, 512)
                                P.op("dve", TT(Hd[:, do:do + n0], Hd[:, do:do + n0], pA[dd][:, 0:n0], ALU.add), reads=[BHh[dd], BpA[dd]], writes=[BHh[dd]])
                                if W_ > 512:
                                    P.op("dve", TT(Hd[:, do + 512:do + W_], Hd[:, do + 512:do + W_], pB2[dd][:, 0:W_ - 512], ALU.add),
                                         reads=[BHh[dd], BpB2[dd]], writes=[BHh[dd]])
                        y0 = pU[0]
                        y1 = pU[1]
                        g0_, g1_ = g, 32 + g
                        P.op("pe", MM(y0[:, 0:512], Toep[:, g0_, :], Ug[:, 0:512], start=True, stop=False), reads=[BToep, BUg], writes=[BpU[0]])
                        P.op("pe", MM(y0[:, 0:512], Toep[:, g1_, :], Ug[:, 0:512], start=False, stop=False), reads=[BToep, BUg], writes=[BpU[0]])
                        P.op("pe", MM(y0[:, 1:512], Cy[:, g0_, :], H[0][:, 0:511].bitcast(F32), start=False, stop=False), reads=[BCy, BHh[0]], writes=[BpU[0]])
                        P.op("pe", MM(y0[:, 32:512], Cy[:, g1_, :], H[1][:, 1:481].bitcast(F32), start=False, stop=False), reads=[BCy, BHh[1]], writes=[BpU[0]])
                        P.op("pe", MM(y0[:, 0:31], Cy[:, g1_, :], H[1][:, 513:544].bitcast(F32), start=False, stop=True), reads=[BCy, BHh[1]], writes=[BpU[0]])
                        P.op("pe", MM(y1[:, 0:32], Toep[:, g0_, :], Ug[:, 512:544], start=True, stop=False), reads=[BToep, BUg], writes=[BpU[1]])
                        P.op("pe", MM(y1[:, 0:32], Toep[:, g1_, :], Ug[:, 512:544], start=False, stop=False), reads=[BToep, BUg], writes=[BpU[1]])
                        P.op("pe", MM(y1[:, 0:32], Cy[:, g0_, :], H[0][:, 511:543].bitcast(F32), start=False, stop=False), reads=[BCy, BHh[0]], writes=[BpU[1]])
                        P.op("pe", MM(y1[:, 0:32], Cy[:, g1_, :], H[1][:, 481:513].bitcast(F32), start=False, stop=True), reads=[BCy, BHh[1]], writes=[BpU[1]])
                        P.op("act", ACT(Yall[:, j, 0:512], y0[:, 0:512], AF.Copy), reads=[BpU[0]], writes=[BYall])
                        P.op("act", ACT(Yall[:, j, 512:544], y1[:, 0:32], AF.Copy), reads=[BpU[1]], writes=[BYall])
                    for tt in range(9):
                        c0 = 64 * tt
                        ncol = min(64, NCK - c0)
                        ntk = 8 * ncol
                        for t in range(8):
                            for j in range(8):
                                P.op("pe", MM(pX[:, t:ntk:8], psel[:, t, 112 - 16 * j:240 - 16 * j], Yall[:, j, c0:c0 + ncol],
                                              start=(j == 0), stop=(j == 7)), reads=[Bpsel, BYall], writes=[BpX])
                        P.op("dve", STT(ytmp[:, :ntk], uj[:, 8 * c0:8 * c0 + ntk], s5d[:, J:J + 1], pX[:, :ntk], ALU.mult, ALU.add),
                             reads=[buj, Bs5d, BpX], writes=[Bytmp])
                        P.op("dve", TT(yt2[:, :ntk], ytmp[:, :ntk], ytmp[:, :ntk], ALU.mult), reads=[Bytmp], writes=[Byt2])
                        P.op("dve", TS(yt2[:, :ntk], yt2[:, :ntk], 0.044715, 1.0, ALU.mult, ALU.add), reads=[Byt2], writes=[Byt2])
                        P.op("dve", TT(yt2[:, :ntk], yt2[:, :ntk], ytmp[:, :ntk], ALU.mult), reads=[Byt2, Bytmp], writes=[Byt2])
                        P.op("act", ACT(yt2[:, :ntk], yt2[:, :ntk], AF.Tanh, scale=0.7978845608028654), reads=[Byt2], writes=[Byt2])
                        P.op("dve", TS(yt2[:, :ntk], yt2[:, :ntk], 0.5, 0.5, ALU.mult, ALU.add), reads=[Byt2], writes=[Byt2])
                        P.op("dve", TT(ygT[:, J, 8 * c0:8 * c0 + ntk], yt2[:, :ntk], ytmp[:, :ntk], ALU.mult), reads=[Byt2, Bytmp], writes=[Bygt])
            P.barrier()
            P.phase = "s5glu"
            with contextlib.ExitStack() as es2:
                Wg = kb.sb(es2, [128, 4, 512], BF16)
                bg = kb.sb(es2, [128, 4], F32)
                sgm = kb.sb(es2, [128, 512], BF16)
                ob = [kb.sb(es2, [128, 4, 512], BF16) for _ in range(2)]
                pG = [kb.ps(es2, [128, 512], F32) for _ in range(2)]
                BWg, Bbg, Bsgm = P.bufs(3, "glu")
                Bob = P.bufs(2, "gob")
                BpG = P.bufs(2, "pG")
                Bmo = P.buf("mo")
                P.dma("pool", DMA(Wg[:], d["wglu"]), writes=[BWg])
                P.dma("sp", DMA(bg[:], d["bglu"]), writes=[Bbg])
                tiles = [(0, 256)] + [(256 + 512 * i, 512) for i in range(8)]
                for i, (t0, nt) in enumerate(tiles):
                    o = ob[i % 2]
                    for co in range(4):
                        p = pG[co % 2]
                        for kc in range(4):
                            P.op("pe", MM(p[:, :nt], Wg[:, kc, co * 128:(co + 1) * 128], ygT[:, kc, t0:t0 + nt], start=(kc == 0), stop=(kc == 3)),
                                 reads=[BWg, Bygt], writes=[BpG[co % 2]])
                        P.op("act", ACT(sgm[:, :nt], p[:, :nt], AF.Sigmoid, bias=bg[:, co:co + 1]), reads=[BpG[co % 2], Bbg], writes=[Bsgm])
                        P.op("dve", TT(o[:, co, :nt], ygT[:, co, t0:t0 + nt], sgm[:, :nt], ALU.mult), reads=[Bygt, Bsgm], writes=[Bob[i % 2]])
                    P.dma("sp", DMA(mo[:, 4:8, t0:t0 + nt], o[:, :, :nt]), reads=[Bob[i % 2]], acc=[Bmo])
                P.wait_all("sp", [Bmo])
    P.barrier()


def phase_s5(kb, d, UT, mo):
    P = kb.P
    S5M = s5_scratch(kb.nc)
    P.phase = "s5setup"
    for _ in s5_setup_gen(kb, d, S5M):
        pass
    P.barrier()
    phase_s5_main(kb, d, UT, mo, S5M)


ANNOTATE = False
ADALN_BG = False


def build_program(shapes):
    nc = bass.Bass("TRN2", target_bir_lowering=False)
    d = {k: nc.dram_tensor(k, list(v), F32, kind="ExternalInput").ap() for k, v in shapes.items()}
    out = nc.dram_tensor("out", [128, KC, NLAT], F32, kind="ExternalOutput").ap()
    H1 = nc.dram_tensor("H1", [128, KC, NTOK], F32).ap()
    H2 = nc.dram_tensor("H2", [128, KC, NTOK], F32).ap()
    H3 = nc.dram_tensor("H3", [128, KC, NLAT], F32).ap()
    mo0 = nc.dram_tensor("mo0", [128, KC, NTOK], BF16).ap()
    mo1 = nc.dram_tensor("mo1", [128, KC, NLAT], BF16).ap()
    UT = nc.dram_tensor("UTs", [128, 4, NTOK], BF16).ap()
    XM1 = nc.dram_tensor("XM1", [128, KC, NTOK], BF16).ap()
    SG = nc.dram_tensor("SGs", [128, 4, NTOK], BF16).ap()
    with contextlib.ExitStack() as es:
        kb = KB(nc, es)
        P = kb.P
        P.annotate = ANNOTATE
        vec = kb.sb(es, [128, 2, 2, 6, 8], F32)
        ones = kb.sb(es, [128, 128], BF16)
        ident = kb.sb(es, [128, 128], BF16)
        mfb = kb.sb(es, [128, 2, 128], F32)
        nfin = kb.sb(es, [128, KC], F32)
        Bvec, Bones, Bident, Bmfb = P.bufs(4, "const")
        P.op("pool", MSET(ones[:], 1.0), writes=[Bones])
        P.dma("pool", DMA(ident[:], d["ident"]), writes=[Bident])
        P.dma("sp", DMA(mfb[:], d["mfb"]), writes=[Bmfb])
        P.dma("sp", DMA(nfin[:], d["nrm"][:, 4, :]), writes=[Bvec])
        xin = d["xin"]
        all_tiles512 = [(0, 256, 1)] + [(256 + 512 * i, 512, 0) for i in range(8)]
        all_tiles256 = [(0, 256, 1)] + [(256 + 384 * i, 384, 0) for i in range(10)] + [(256 + 3840, 256, 0)]
        lat_tiles512 = [(256 + 512 * i, 512, 0) for i in range(8)]
        lat_tiles256 = [(256 + 384 * i, 384, 0) for i in range(10)] + [(256 + 3840, 256, 0)]
        with contextlib.ExitStack() as esA:
            P.phase = "adaln"
            ada = Adaln(kb, d, esA, vec, Bvec)
            S5M = s5_scratch(nc)

            def ada_all():
                yield from ada.layer(0)
                if not ADALN_BG:
                    yield from ada.layer(1)

            interleave([(0, ada_all()), (0, s5_setup_gen(kb, d, S5M))], 2)
            P.barrier()

            def side():
                for _ in ada.layer(1):
                    yield

            with contextlib.ExitStack() as es2:
                Qr = kb.sb(es2, [128, 2, NTOK], BF16)
                Kr = kb.sb(es2, [128, 2, NTOK], BF16)
                Ktm = kb.sb(es2, [128, NTOK // 128, 256], BF16)
                Vtm = kb.sb(es2, [128, NTOK // 128, 512], BF16)
                Bqkv = P.buf("qkv")
                phase_l0_proj(kb, d, xin, Qr, Kr, Ktm, Vtm, Bqkv, UT, SG, vec, Bvec, ones, Bones, ident, Bident, side=(side() if ADALN_BG else None))
                phase_ret(kb, d, Qr, Kr, Ktm, Vtm, Bqkv, SG, mo0, ones, Bones, mfb, Bmfb)
        P.barrier()
        phase_s5_main(kb, d, UT, mo0, S5M)
        phase_outproj(kb, d["abwo"], 0, mo0, 0, {"ap": xin, "off": 0}, {"ap": H1, "off": 0}, all_tiles512, vec, Bvec)
        phase_mlp(kb, d, 0, {"ap": H1, "off": 0}, {"ap": H2, "off": 0}, all_tiles256, vec, Bvec, ones, Bones)
        phase_hgrn(kb, d, H2, mo1, vec, Bvec, ones, Bones, ident, Bident, xm1=XM1)
        phase_outproj(kb, d["hgwo"], 1, mo1, NCTX, {"ap": H2, "off": 0}, {"ap": H3, "off": NCTX}, lat_tiles512, vec, Bvec)
        phase_mlp(kb, d, 1, {"ap": H3, "off": NCTX}, {"ap": out, "off": NCTX}, lat_tiles256, vec, Bvec, ones, Bones, final_g=nfin)
        P.emit()
    return nc


def host_shared(inp):
    f = lambda k: np.asarray(inp[k], dtype=np.float32)
    d = {}
    d["bmod"] = np.ascontiguousarray(np.stack([lay_vec(f("b_mod")[l]) for l in range(2)], axis=1))
    d["nrm"] = np.ascontiguousarray(np.stack([lay_vec(f("norm_mix")[0]), lay_vec(f("norm_mix")[1]), lay_vec(f("norm_mlp")[0]),
                                              lay_vec(f("norm_mlp")[1]), lay_vec(f("norm_final"))], axis=1))
    wm = np.stack([lay_kc(f("w_mod")[l]) for l in range(2)])
    d["wmod"] = np.ascontiguousarray(wm.reshape(2, 128, 8, 12, 512).transpose(0, 3, 1, 2, 4))
    d["w1"] = np.stack([lay_kc(f("w_mlp_in")[l]) for l in range(2)])
    d["w2"] = np.stack([lay_kc(f("w_mlp_out")[l]) for l in range(2)])
    wk = lay_kc(f("hg_w_in")[0]).reshape(128, 8, 5, 8, 128)
    d["hgw"] = np.ascontiguousarray(wk.transpose(3, 0, 1, 2, 4))
    d["hgwo"] = lay_kc(f("hg_w_out")[0])
    d["lbl"] = np.ascontiguousarray(f("hg_lb_logits").reshape(2, 2, 8, 128).transpose(3, 0, 1, 2))
    d["hgn"] = np.ascontiguousarray(f("hg_norm")[0].reshape(128, 1))
    s = np.arange(128)
    d["mfb"] = np.ascontiguousarray(np.stack([(s[:, None] <= s[None, :]), (s[:, None] >= s[None, :])], axis=1).astype(np.float32))
    cm = np.ones((128, 512), np.float32)
    cm[:, ::128] = 0
    d["cmask"] = cm
    d["ident"] = np.eye(128, dtype=np.float32)
    w = f("ab_w_in")[0]
    q = w[:, 0:256].reshape(1024, 4, 64)
    k = w[:, 256:512].reshape(1024, 4, 64)
    qrot = np.concatenate([q[:, :, 32:], q[:, :, :32]], -1).reshape(1024, 256)
    krot = np.concatenate([k[:, :, 32:], k[:, :, :32]], -1).reshape(1024, 256)
    d["abw"] = lay_kc(np.concatenate([w, qrot, krot], 1))
    rows = NLAT // 64
    row = np.repeat(np.arange(rows, dtype=np.float32), 64)
    col = np.tile(np.arange(64, dtype=np.float32), rows)
    inv = (10000.0 ** (-np.arange(16, dtype=np.float32) / 16)).astype(np.float32)
    ang = np.concatenate([row[:, None] * inv, col[:, None] * inv], -1).astype(np.float32)
    idx = np.arange(128) % 32
    d["rope"] = np.ascontiguousarray(np.stack([np.cos(ang).astype(np.float32)[:, idx].T, np.sin(ang).astype(np.float32)[:, idx].T], 1))
    rl = f("ret_logit")[0]
    d["retl"] = np.ascontiguousarray(np.broadcast_to(rl.reshape(1, 8), (128, 8))).astype(np.float32)
    rp = np.zeros((128, 2, 2), np.float32)
    for dr in range(2):
        for pair in range(2):
            rp[:64, dr, pair] = rl[dr, 2 * pair]
            rp[64:, dr, pair] = rl[dr, 2 * pair + 1]
    d["retlP"] = rp
    j = np.arange(128, dtype=np.float32)
    d["dF"] = np.ascontiguousarray(np.stack([np.maximum(j[None, :] - j[:, None], 0), np.maximum(j[:, None] - j[None, :], 0)], 1)).astype(np.float32)
    d["posc"] = np.ascontiguousarray(np.broadcast_to(np.stack([j + 1, 128 - j], 0)[None], (128, 2, 128))).astype(np.float32)
    d["jcol"] = np.ascontiguousarray(np.stack([np.repeat((127 - j)[:, None], 4, 1), np.repeat(j[:, None], 4, 1)], 1)).astype(np.float32)
    hm = np.zeros((128, 4), np.float32)
    hm[:64, 0] = 1
    hm[:64, 2] = 1
    hm[64:, 1] = 1
    hm[64:, 3] = 1
    d["hm"] = hm
    are = f("s5_a_re")[0].reshape(64, 64)
    aim = f("s5_a_im")[0].reshape(64, 64)
    a2 = np.stack([are.T, aim.T], 1)
    d["s5a"] = np.ascontiguousarray(np.concatenate([a2, a2], 0)).astype(np.float32)
    d["s5dt"] = np.ascontiguousarray(np.broadcast_to(f("s5_log_dt")[0].reshape(1, 64), (128, 64))).astype(np.float32)
    bre = f("s5_b_re")[0].reshape(64, 64, 16).transpose(1, 0, 2)
    bim = f("s5_b_im")[0].reshape(64, 64, 16).transpose(1, 0, 2)
    d["s5b"] = np.ascontiguousarray(np.stack([np.concatenate([bre, bim], 0), np.concatenate([bim, bre], 0)], 1)).astype(np.float32)
    cre = f("s5_c_re")[0].reshape(64, 16, 64).transpose(2, 0, 1)
    cim = f("s5_c_im")[0].reshape(64, 16, 64).transpose(2, 0, 1)
    d["s5c"] = np.ascontiguousarray(np.stack([np.concatenate([cre, cim], 0), np.concatenate([cim, cre], 0)], 1)).astype(np.float32)
    nv = np.stack([np.arange(16) - 7.0, 8.0 - np.arange(16)], 0).astype(np.float32)
    d["nvec"] = np.ascontiguousarray(np.broadcast_to(nv[None], (128, 2, 16))).astype(np.float32)
    ps = np.zeros((128, 8, 240), np.float32)
    for jj in range(8):
        for m in range(16):
            ps[16 * jj + m, jj, 112 + m] = 1.0
    d["psel"] = ps
    s8 = np.arange(128) // 16
    d["tmask"] = np.ascontiguousarray(np.stack([(s8[None, :] >= s8[:, None]), (s8[:, None] >= s8[None, :])], 1)).astype(np.float32)
    im = np.zeros((128, 2, 128), np.float32)
    im[:, 0, :] = np.eye(128)
    for p in range(64):
        im[p, 1, 64 + p] = 1.0
        im[64 + p, 1, p] = -1.0
    d["imat"] = im
    d["s5d"] = lay_vec(f("s5_d")[0])
    d["bglu"] = lay_vec(f("s5_b_glu")[0])
    d["wglu"] = lay_kc(f("s5_w_glu")[0])
    d["abwo"] = lay_kc(f("ab_w_out")[0])
    return d


_CACHE = {}


def kernel(**inp):
    shared = host_shared(inp)
    x = np.asarray(inp["x"], dtype=np.float32)
    ctx = np.asarray(inp["ctx"], dtype=np.float32)
    c = np.asarray(inp["c"], dtype=np.float32)
    c_ctx = np.asarray(inp["c_ctx"], dtype=np.float32)
    B = x.shape[0]
    in_maps = []
    for b in range(B):
        m = dict(shared)
        full = np.concatenate([ctx[b], x[b]], 0)
        m["xin"] = lay_kc(np.ascontiguousarray(full.T))
        m["cc"] = np.ascontiguousarray(np.stack([lay_vec(c[b]), lay_vec(c_ctx)], axis=-1))
        in_maps.append(m)
    nc = build_program({k: v.shape for k, v in in_maps[0].items()})
    res = run_bass_kernel_spmd(nc, in_maps, core_ids=list(range(B)))
    outs = []
    for b in range(B):
        o = np.asarray(res.results[b]["out"], dtype=np.float32).reshape(128, KC, NLAT)
        outs.append(o.transpose(2, 1, 0).reshape(NLAT, D))
    return np.stack(outs, 0)
```

```python
import contextlib
import numpy as np
import concourse.bass as bass
import concourse.mybir as mybir
from concourse.bass_utils import run_bass_kernel_spmd

F32 = mybir.dt.float32
BF16 = mybir.dt.bfloat16
I32 = mybir.dt.int32
F32R = mybir.dt.float32r
AF = mybir.ActivationFunctionType
ALU = mybir.AluOpType

D = 1024
KC = 8
NCTX = 256
NLAT = 4096
NTOK = NCTX + NLAT
DFF = 4096
EPS = 1e-6
DEBUG_BAR = False

ENGS = ("pe", "dve", "act", "pool", "sp")
NDMA_SEM = 44
NDMA_SW = 14
SAME_ENGINE_SYNC = False


class Buf:
    __slots__ = ("name", "w", "r", "dsem")

    def __init__(self, name):
        self.name = name
        self.w = {}
        self.r = {}
        self.dsem = {}


class Prog:
    def __init__(self, nc, es):
        self.nc = nc
        self.es = es
        self.q = {e: [] for e in ENGS}
        self.chan_ops = {}
        self.needed = {}
        self.seen = {e: {} for e in ENGS}
        for e in ENGS:
            self.chan_ops[e] = 0
            self.needed[e] = set()
        self.free_dma = {"sw": [], "hw": []}
        for i in range(NDMA_SEM):
            c = "dma%d" % i
            self.chan_ops[c] = 0
            self.needed[c] = set()
            self.free_dma["sw" if i < NDMA_SW else "hw"].append(c)
        self.nbuf = 0
        self.cur_bufs = []
        self.phase = "init"
        self.annotate = False

    def buf(self, name=None):
        self.nbuf += 1
        b = Buf(name or ("b%d" % self.nbuf))
        self.cur_bufs.append(b)
        return b

    def bufs(self, n, name="b"):
        return [self.buf("%s%d" % (name, i)) for i in range(n)]

    def _deps(self, eng, chan, reads, writes):
        waits = {}
        is_dma = chan.startswith("dma")

        def need(c, i):
            if c == eng and not is_dma and (eng == "pe" or not SAME_ENGINE_SYNC):
                return
            if self.seen[eng].get(c, -1) >= i:
                return
            if waits.get(c, -1) < i:
                waits[c] = i

        for b in reads:
            for c, i in b.w.items():
                need(c, i)
        for b in writes:
            for c, i in b.w.items():
                need(c, i)
            for c, i in b.r.items():
                need(c, i)
        for c, i in waits.items():
            self.seen[eng][c] = i
            self.needed[c].add(i)
        return waits

    def _commit(self, chan, idx, reads, writes, multi=False):
        for b in reads:
            if b.r.get(chan, -1) < idx:
                b.r[chan] = idx
        for b in writes:
            if multi:
                b.w[chan] = idx
            else:
                b.w = {chan: idx}
                b.r = {}

    def op(self, eng, fn, reads=(), writes=()):
        waits = self._deps(eng, eng, reads, writes)
        idx = self.chan_ops[eng]
        self.chan_ops[eng] = idx + 1
        self.q[eng].append((waits, fn, eng, idx, self.phase))
        self._commit(eng, idx, reads, writes)

    def dma(self, eng, fn, reads=(), writes=(), prim=None, acc=()):
        if prim is None:
            prim = writes[0] if writes else reads[0]
        kind = "sw" if eng == "pool" else "hw"
        if kind not in prim.dsem:
            assert self.free_dma[kind], "out of dma semaphores"
            prim.dsem[kind] = self.free_dma[kind].pop(0)
        chan = prim.dsem[kind]
        waits = self._deps(eng, chan, list(reads), list(writes))
        idx = self.chan_ops[chan]
        self.chan_ops[chan] = idx + 1
        self.needed[chan].add(idx)
        self.q[eng].append((waits, fn, chan, idx, self.phase))
        self._commit(chan, idx, reads, writes)
        self._commit(chan, idx, (), acc, multi=True)

    def wait_all(self, eng, bufs):
        waits = self._deps(eng, eng, bufs, ())
        self.q[eng].append((waits, None, None, None, self.phase))

    def barrier(self):
        last = {c: n - 1 for c, n in self.chan_ops.items() if n > 0}
        for e in ENGS:
            waits = {}
            for c, i in last.items():
                if c == e:
                    continue
                if self.seen[e].get(c, -1) >= i:
                    continue
                waits[c] = i
                self.seen[e][c] = i
                self.needed[c].add(i)
            if waits:
                self.q[e].append((waits, None, None, None, self.phase))
        for b in self.cur_bufs:
            for kind, c in b.dsem.items():
                self.free_dma[kind].append(c)
            b.dsem = {}
        self.cur_bufs = []

    def emit(self):
        nc = self.nc
        val = {}
        for c in self.chan_ops:
            if c.startswith("dma"):
                val[c] = {i: (i + 1) * 16 for i in range(self.chan_ops[c])}
            else:
                v = 0
                d = {}
                need = self.needed[c]
                for i in range(self.chan_ops[c]):
                    if i in need:
                        v += 1
                    d[i] = v
                val[c] = d
        sems = {}
        for c in self.chan_ops:
            if self.chan_ops[c] > 0:
                sems[c] = self.es.enter_context(nc.semaphore("s_" + c))
        handles = {"pe": "tensor", "dve": "vector", "act": "scalar", "pool": "gpsimd", "sp": "sync"}
        self.maxval = {c: (max(val[c].values()) if val[c] else 0) for c in val}

        def run(eng_name):
            def body(e):
                for waits, fn, chan, idx, phase in self.q[eng_name]:
                    for c, i in waits.items():
                        e.wait_ge(sems[c], val[c][i])
                    if fn is None:
                        continue
                    ins = fn(e)
                    if self.annotate:
                        ins.annotate(phase)
                    if chan.startswith("dma"):
                        ins.then_inc(sems[chan], 16)
                    elif idx in self.needed[chan]:
                        ins.then_inc(sems[chan], 1)
            return body

        with nc.Block() as block:
            for en in ENGS:
                if self.q[en]:
                    getattr(block, handles[en])(run(en))


class KB:
    def __init__(self, nc, es):
        self.nc = nc
        self.es = es
        self.P = Prog(nc, es)
        self.n = 0
        self.debug = False
        self.dbg_outs = {}
        self.Bdbg = None

    def dump(self, name, ap, bufs, dt=None):
        if not self.debug or name in self.dbg_outs:
            return
        shape = list(ap.shape)
        t = self.nc.dram_tensor("dbg_" + name, shape, dt or ap.dtype, kind="ExternalOutput").ap()
        self.dbg_outs[name] = t
        if self.Bdbg is None:
            self.Bdbg = self.P.buf("dbg")
        self.P.dma("sp", DMA(t, ap), reads=list(bufs), acc=[self.Bdbg])
        self.P.wait_all("sp", [self.Bdbg])

    def sb(self, es, shape, dt, name=None):
        self.n += 1
        return es.enter_context(self.nc.sbuf_tensor(name or ("t%d" % self.n), list(shape), dt))

    def ps(self, es, shape, dt=F32, name=None):
        self.n += 1
        return es.enter_context(self.nc.psum_tensor(name or ("p%d" % self.n), list(shape), dt))


def lay_kc(w):
    K, N = w.shape
    return np.ascontiguousarray(w.reshape(K // 128, 128, N).transpose(1, 0, 2))


def lay_vec(v):
    return np.ascontiguousarray(v.reshape(-1, 128).T)


def MM(out, lhsT, rhs, start=True, stop=True):
    return lambda e: e.matmul(out, lhsT, rhs, start=start, stop=stop)


def TR(out, in_, ident):
    return lambda e: e.transpose(out, in_, ident)


def ACT(out, in_, func, bias=None, scale=None):
    kw = {}
    if bias is not None:
        kw["bias"] = bias
    if scale is not None:
        kw["scale"] = scale
    return lambda e: e.activation(out, in_, func, **kw)


def TT(out, a, b, op):
    return lambda e: e.tensor_tensor(out, a, b, op)


def STT(out, in0, scalar, in1, op0, op1):
    return lambda e: e.scalar_tensor_tensor(out, in0, scalar, in1, op0, op1)


def TS(out, in0, s1, s2, op0, op1=None):
    if op1 is None:
        return lambda e: e.tensor_scalar(out, in0, s1, None, op0)
    return lambda e: e.tensor_scalar(out, in0, s1, s2, op0, op1)


def CP(out, in_):
    return lambda e: e.tensor_copy(out, in_)


def RCP(out, in_):
    return lambda e: e.reciprocal(out, in_)


def MSET(out, v):
    return lambda e: e.memset(out, v)


def DMA(out, in_, **kw):
    return lambda e: e.dma_start(out=out, in_=in_, **kw)


def SCAN(out, d0, d1, init, op0, op1):
    return lambda e: e.tensor_tensor_scan(out, d0, d1, init, op0, op1)


def interleave(gens, width, slots=2):
    active = []
    it = iter(gens)
    pending = None
    more = True
    while True:
        while more and len(active) < width:
            if pending is None:
                try:
                    pending = next(it)
                except StopIteration:
                    more = False
                    break
            if any(t_ <= pending[0] - slots for t_, _ in active):
                break
            active.append(pending)
            pending = None
        if not active:
            break
        for item in list(active):
            try:
                next(item[1])
            except StopIteration:
                active.remove(item)


class Adaln:
    def __init__(self, kb, d, es, vec, Bvec):
        P = kb.P
        self.kb, self.d, self.vec, self.Bvec = kb, d, vec, Bvec
        self.cc = kb.sb(es, [128, KC, 2], F32)
        self.sc = kb.sb(es, [128, KC, 2], F32)
        self.bm = kb.sb(es, [128, 2, 48], F32)
        self.nr = kb.sb(es, [128, 5, KC], F32)
        self.wbuf = [kb.sb(es, [128, KC, 256], F32) for _ in range(2)]
        self.raw = kb.sb(es, [128, 2, 48], F32)
        self.pm = kb.ps(es, [128, 48, 2], F32)
        self.Bcc, self.Bsc, self.Bbm, self.Bnr, self.Braw, self.Bpm = P.bufs(6, "ada")
        self.Bw = P.bufs(2, "adaw")
        P.dma("sp", DMA(self.cc[:], d["cc"]), writes=[self.Bcc])
        P.dma("sp", DMA(self.bm[:], d["bmod"]), writes=[self.Bbm])
        P.dma("sp", DMA(self.nr[:], d["nrm"]), writes=[self.Bnr])
        P.op("act", ACT(self.sc[:], self.cc[:], AF.Silu), reads=[self.Bcc], writes=[self.Bsc])

    def layer(self, l):
        P = self.kb.P
        d, vec, Bvec = self.d, self.vec, self.Bvec
        pm, sc, raw, nr, bm = self.pm, self.sc, self.raw, self.nr, self.bm
        def load(j):
            P.dma("sp", DMA(self.wbuf[j % 2][:], d["wmod"][l, j // 2][:, :, 256 * (j % 2):256 * (j % 2) + 256]), writes=[self.Bw[j % 2]])
        load(0)
        for j in range(24):
            wb = self.wbuf[j % 2]
            bw = self.Bw[j % 2]
            if j + 1 < 24:
                load(j + 1)
            for c2 in range(2):
                c48 = j * 2 + c2
                for kc in range(KC):
                    P.op("pe", MM(pm[:, c48, :], wb[:, kc, c2 * 128:(c2 + 1) * 128], sc[:, kc, :],
                                  start=(kc == 0), stop=(kc == KC - 1)), reads=[bw, self.Bsc], writes=[self.Bpm])
            yield
        P.op("dve", TT(raw[:], pm[:].rearrange("p c s -> p s c"),
                       bm[:, l, :].unsqueeze(1).broadcast_to([128, 2, 48]), ALU.add),
             reads=[self.Bpm, self.Bbm], writes=[self.Braw])
        for s in range(2):
            r6 = raw[:, s, :].rearrange("p (j k) -> p j k", j=6)
            for (jo, jscale, jshift, jgate, nidx) in ((0, 1, 0, 2, l), (3, 4, 3, 5, 2 + l)):
                P.op("dve", STT(vec[:, l, s, jo, :], r6[:, jscale, :], 1.0, nr[:, nidx, :], ALU.add, ALU.mult),
                     reads=[self.Braw, self.Bnr], writes=[Bvec])
                P.op("dve", CP(vec[:, l, s, jo + 1, :], r6[:, jshift, :]), reads=[self.Braw], writes=[Bvec])
                P.op("dve", CP(vec[:, l, s, jo + 2, :], r6[:, jgate, :]), reads=[self.Braw], writes=[Bvec])
        yield


def phase_adaln(kb, d, vec, Bvec):
    P = kb.P
    P.phase = "adaln"
    with contextlib.ExitStack() as es:
        ada = Adaln(kb, d, es, vec, Bvec)
        for l in range(2):
            for _ in ada.layer(l):
                pass
    P.barrier()


class NormBufs:
    def __init__(self, kb, es, NT):
        P = kb.P
        self.sq = kb.sb(es, [128, KC, NT], BF16)
        self.rstd = kb.sb(es, [128, NT], F32)
        self.tmp = [kb.sb(es, [128, NT], F32) for _ in range(2)]
        self.ss = kb.ps(es, [128, NT], F32)
        self.Bsq, self.Brstd, self.Bss = P.bufs(3, "nb")
        self.Btmp = P.bufs(2, "nbt")


def rstd_of(kb, nb, hT, Bh, nt, ones, Bones):
    P = kb.P
    for kc in range(KC):
        P.op("act", ACT(nb.sq[:, kc, :nt], hT[:, kc, :nt], AF.Square), reads=[Bh], writes=[nb.Bsq])
    for kc in range(KC):
        P.op("pe", MM(nb.ss[:, :nt], ones[:], nb.sq[:, kc, :nt], start=(kc == 0), stop=(kc == KC - 1)),
             reads=[nb.Bsq, Bones], writes=[nb.Bss])
    P.op("act", ACT(nb.rstd[:, :nt], nb.ss[:, :nt], AF.Ln, bias=EPS, scale=1.0 / D), reads=[nb.Bss], writes=[nb.Brstd])
    P.op("act", ACT(nb.rstd[:, :nt], nb.rstd[:, :nt], AF.Exp, scale=-0.5), reads=[nb.Brstd], writes=[nb.Brstd])


def norm_stage_a(kb, nb, hT, Bh, nt):
    P = kb.P
    for kc in range(KC):
        P.op("act", ACT(nb.sq[:, kc, :nt], hT[:, kc, :nt], AF.Square), reads=[Bh], writes=[nb.Bsq])


def norm_stage_b(kb, nb, nt, ones, Bones):
    P = kb.P
    for kc in range(KC):
        P.op("pe", MM(nb.ss[:, :nt], ones[:], nb.sq[:, kc, :nt], start=(kc == 0), stop=(kc == KC - 1)),
             reads=[nb.Bsq, Bones], writes=[nb.Bss])
    P.op("act", ACT(nb.rstd[:, :nt], nb.ss[:, :nt], AF.Ln, bias=EPS, scale=1.0 / D), reads=[nb.Bss], writes=[nb.Brstd])
    P.op("act", ACT(nb.rstd[:, :nt], nb.rstd[:, :nt], AF.Exp, scale=-0.5), reads=[nb.Brstd], writes=[nb.Brstd])


def norm_stage_c(kb, nb, hT, Bh, nt, G, S, Bvec, xm, Bxm):
    P = kb.P
    for kc in range(KC):
        t = nb.tmp[kc % 2]
        bt = nb.Btmp[kc % 2]
        P.op("dve", STT(t[:, :nt], hT[:, kc, :nt], G[:, kc:kc + 1], nb.rstd[:, :nt], ALU.mult, ALU.mult),
             reads=[Bh, nb.Brstd, Bvec], writes=[bt])
        P.op("act", ACT(xm[:, kc, :nt], t[:, :nt], AF.Identity, bias=S[:, kc:kc + 1]), reads=[bt, Bvec], writes=[Bxm])


def norm_mod(kb, nb, hT, Bh, nt, G, S, Bvec, ones, Bones, xm, Bxm):
    P = kb.P
    rstd_of(kb, nb, hT, Bh, nt, ones, Bones)
    for kc in range(KC):
        t = nb.tmp[kc % 2]
        bt = nb.Btmp[kc % 2]
        P.op("dve", STT(t[:, :nt], hT[:, kc, :nt], G[:, kc:kc + 1], nb.rstd[:, :nt], ALU.mult, ALU.mult),
             reads=[Bh, nb.Brstd, Bvec], writes=[bt])
        P.op("act", ACT(xm[:, kc, :nt], t[:, :nt], AF.Identity, bias=S[:, kc:kc + 1]), reads=[bt, Bvec], writes=[Bxm])


def phase_mlp(kb, d, l, hin, hout, tiles, vec, Bvec, ones, Bones, final_g=None):
    P = kb.P
    P.phase = "mlp%d" % l
    NT = max(t[1] for t in tiles)
    NF = DFF // 128
    with contextlib.ExitStack() as es:
        W1 = kb.sb(es, [128, KC, DFF], BF16)
        W2 = kb.sb(es, [128, NF, D], BF16)
        hT = [kb.sb(es, [128, KC, NT], F32) for _ in range(2)]
        xm = [kb.sb(es, [128, KC, NT], BF16) for _ in range(2)]
        a1 = kb.sb(es, [128, NF, NT], BF16)
        rl = [kb.sb(es, [128, NT], BF16) for _ in range(2)]
        nb = NormBufs(kb, es, NT)
        pa = [kb.ps(es, [128, NT], F32) for _ in range(3)]
        po = [kb.ps(es, [128, NT], F32) for _ in range(2)]
        BW1 = P.bufs(4, "W1")
        BW2 = P.bufs(4, "W2")
        Bh = P.bufs(2, "h")
        Bxm = P.bufs(2, "xm")
        Ba1 = P.buf("a1")
        Brl = P.bufs(2, "rl")
        Bpa = P.bufs(3, "pa")
        Bpo = P.bufs(2, "po")
        Bout = P.buf("hout")
        for i in range(4):
            P.dma("pool", DMA(W1[:, :, 1024 * i:1024 * (i + 1)], d["w1"][l, :, :, 1024 * i:1024 * (i + 1)]), writes=[BW1[i]])
        for i in range(4):
            P.dma("pool", DMA(W2[:, 8 * i:8 * i + 8, :], d["w2"][l, :, 8 * i:8 * i + 8, :]), writes=[BW2[i]])

        def load(i):
            t0, nt, s = tiles[i]
            P.dma("sp", DMA(hT[i % 2][:, :, :nt], hin["ap"][:, :, t0 - hin["off"]:t0 - hin["off"] + nt]), writes=[Bh[i % 2]])

        load(0)

        def prep(i, stage):
            t0_, nt_, s_ = tiles[i]
            if stage == "a":
                norm_stage_a(kb, nb, hT[i % 2], Bh[i % 2], nt_)
            elif stage == "b":
                norm_stage_b(kb, nb, nt_, ones, Bones)
            else:
                norm_stage_c(kb, nb, hT[i % 2], Bh[i % 2], nt_, vec[:, l, s_, 3, :], vec[:, l, s_, 4, :], Bvec, xm[i % 2], Bxm[i % 2])

        for st_ in ("a", "b", "c"):
            prep(0, st_)
        for i, (t0, nt, s) in enumerate(tiles):
            if i + 1 < len(tiles):
                load(i + 1)
            h = hT[i % 2]
            bh = Bh[i % 2]
            xm_ = xm[i % 2]
            bxm = Bxm[i % 2]
            gate = vec[:, l, s, 5, :]
            nxt = i + 1 < len(tiles)
            for fc in range(NF):
                p = pa[fc % 3]
                bp = Bpa[fc % 3]
                for kc in range(KC):
                    P.op("pe", MM(p[:, :nt], W1[:, kc, fc * 128:(fc + 1) * 128], xm_[:, kc, :nt],
                                  start=(kc == 0), stop=(kc == KC - 1)), reads=[BW1[fc // 8], bxm], writes=[bp])
                r = rl[fc % 2]
                br = Brl[fc % 2]
                P.op("act", ACT(r[:, :nt], p[:, :nt], AF.Relu), reads=[bp], writes=[br])
                eng = "pool" if fc % 4 == 0 else "dve"
                P.op(eng, TT(a1[:, fc, :nt], r[:, :nt], r[:, :nt], ALU.mult), reads=[br], writes=[Ba1])
            if nxt:
                prep(i + 1, "a")
            for dc in range(KC):
                p = po[dc % 2]
                bp = Bpo[dc % 2]
                for fc in range(NF):
                    P.op("pe", MM(p[:, :nt], W2[:, fc, dc * 128:(dc + 1) * 128], a1[:, fc, :nt],
                                  start=(fc == 0), stop=(fc == NF - 1)), reads=[BW2[fc // 8], Ba1], writes=[bp])
                P.op("dve", STT(h[:, dc, :nt], p[:, :nt], gate[:, dc:dc + 1], h[:, dc, :nt], ALU.mult, ALU.add),
                     reads=[bp, bh, Bvec], writes=[bh])
                if nxt and dc == 1:
                    prep(i + 1, "b")
                if nxt and dc == 3:
                    prep(i + 1, "c")
            if final_g is not None:
                rstd_of(kb, nb, h, bh, nt, ones, Bones)
                for kc in range(KC):
                    P.op("dve", STT(h[:, kc, :nt], h[:, kc, :nt], final_g[:, kc:kc + 1], nb.rstd[:, :nt], ALU.mult, ALU.mult),
                         reads=[bh, nb.Brstd, Bvec], writes=[bh])
            P.dma("sp", DMA(hout["ap"][:, :, t0 - hout["off"]:t0 - hout["off"] + nt], h[:, :, :nt]), reads=[bh], acc=[Bout])
        P.wait_all("sp", [Bout])
    P.barrier()


def phase_outproj(kb, wout, l, mo, mo_off, hin, hout, tiles, vec, Bvec):
    P = kb.P
    P.phase = "outproj%d" % l
    NT = 512
    with contextlib.ExitStack() as es:
        W = kb.sb(es, [128, KC, D], BF16)
        hT = [kb.sb(es, [128, KC, NT], F32) for _ in range(2)]
        mT = [kb.sb(es, [128, KC, NT], BF16) for _ in range(2)]
        po = [kb.ps(es, [128, NT], F32) for _ in range(2)]
        BW = P.buf("Wo")
        Bh = P.bufs(2, "h")
        Bm = P.bufs(2, "mo")
        Bpo = P.bufs(2, "po")
        Bout = P.buf("hout")
        P.dma("pool", DMA(W[:], wout), writes=[BW])

        def load(i):
            t0, nt, s = tiles[i]
            P.dma("sp", DMA(hT[i % 2][:, :, :nt], hin["ap"][:, :, t0 - hin["off"]:t0 - hin["off"] + nt]), writes=[Bh[i % 2]])
            P.dma("sp", DMA(mT[i % 2][:, :, :nt], mo[:, :, t0 - mo_off:t0 - mo_off + nt]), writes=[Bm[i % 2]])

        load(0)
        for i, (t0, nt, s) in enumerate(tiles):
            if i + 1 < len(tiles):
                load(i + 1)
            h = hT[i % 2]
            bh = Bh[i % 2]
            m = mT[i % 2]
            gate = vec[:, l, s, 2, :]
            for dc in range(KC):
                p = po[dc % 2]
                bp = Bpo[dc % 2]
                for kc in range(KC):
                    P.op("pe", MM(p[:, :nt], W[:, kc, dc * 128:(dc + 1) * 128], m[:, kc, :nt],
                                  start=(kc == 0), stop=(kc == KC - 1)), reads=[BW, Bm[i % 2]], writes=[bp])
                P.op("dve", STT(h[:, dc, :nt], p[:, :nt], gate[:, dc:dc + 1], h[:, dc, :nt], ALU.mult, ALU.add),
                     reads=[bp, bh, Bvec], writes=[bh])
            P.dma("sp", DMA(hout["ap"][:, :, t0 - hout["off"]:t0 - hout["off"] + nt], h[:, :, :nt]), reads=[bh], acc=[Bout])
        P.wait_all("sp", [Bout])
    P.barrier()


def phase_hgrn(kb, d, hin, mo, vec, Bvec, ones, Bones, ident, Bident, xm1=None):
    P = kb.P
    P.phase = "hgrn"
    if xm1 is None:
        xm1 = kb.nc.dram_tensor("xm1_scratch", [128, KC, NTOK], BF16).ap()
    NT = 512
    NCH = NTOK // 128
    tiles = [(0, 256, 1)] + [(256 + 512 * i, 512, 0) for i in range(8)]
    with contextlib.ExitStack() as es0:
        lbv = kb.sb(es0, [128, 2, 8], F32)
        oml = kb.sb(es0, [128, 2, 8], F32)
        noml = kb.sb(es0, [128, 2, 8], F32)
        hgn = kb.sb(es0, [128, 1], F32)
        mfb = kb.sb(es0, [128, 2, 128], F32)
        cmask = kb.sb(es0, [128, NT], F32)
        Blb, Bhgn, Bmfb, Bcm = P.bufs(4, "hgc")
        with contextlib.ExitStack() as es:
            hT = [kb.sb(es, [128, KC, NT], F32) for _ in range(2)]
            Bh = P.bufs(2, "h")
            nb = NormBufs(kb, es, NT)
            xmT = [kb.sb(es, [128, KC, NT], BF16) for _ in range(2)]
            BxmT = P.bufs(2, "xmT")
            Bxm1 = P.buf("xm1")
            lraw = kb.sb(es, [128, 2, 2, 8], F32)
            Blraw = P.buf("lraw")
            P.dma("sp", DMA(lraw[:], d["lbl"]), writes=[Blraw])
            P.dma("sp", DMA(hgn[:], d["hgn"]), writes=[Bhgn])
            P.dma("sp", DMA(mfb[:], d["mfb"]), writes=[Bmfb])
            P.dma("sp", DMA(cmask[:], d["cmask"]), writes=[Bcm])
            P.op("dve", TT(lbv[:], lraw[:, :, 1, :], lraw[:, :, 0, :], ALU.subtract), reads=[Blraw], writes=[Blb])
            P.op("act", ACT(lbv[:], lbv[:], AF.Sigmoid), reads=[Blb], writes=[Blb])
            P.op("dve", TS(noml[:], lbv[:], 1.0, -1.0, ALU.mult, ALU.add), reads=[Blb], writes=[Blb])
            P.op("dve", TS(oml[:], lbv[:], -1.0, 1.0, ALU.mult, ALU.add), reads=[Blb], writes=[Blb])

            def load(i):
                t0, nt, s = tiles[i]
                P.dma("sp", DMA(hT[i % 2][:, :, :nt], hin[:, :, t0:t0 + nt]), writes=[Bh[i % 2]])
            load(0)
            for i, (t0, nt, s) in enumerate(tiles):
                if i + 1 < len(tiles):
                    load(i + 1)
                norm_mod(kb, nb, hT[i % 2], Bh[i % 2], nt, vec[:, 1, s, 0, :], vec[:, 1, s, 1, :], Bvec, ones, Bones,
                         xmT[i % 2], BxmT[i % 2])
                P.dma("sp", DMA(xm1[:, :, t0:t0 + nt], xmT[i % 2][:, :, :nt]), reads=[BxmT[i % 2]], acc=[Bxm1])
            P.wait_all("sp", [Bxm1])
        P.barrier()
        P.phase = "hgrn_heads"
        HT = 256
        htiles = [(0, 256, 1)] + [(256 + HT * i, HT, 0) for i in range(NLAT // HT)]
        with contextlib.ExitStack() as es:
            NS = 4
            Whs = [kb.sb(es, [128, KC, 5, 128], BF16) for _ in range(2)]
            BWhs = P.bufs(2, "Wh")
            xsb = [kb.sb(es, [128, KC, HT], BF16) for _ in range(NS)]
            Bxs = P.bufs(NS, "xs")
            qfb = [kb.sb(es, [128, NTOK], BF16) for _ in range(2)]
            kifb = [kb.sb(es, [128, NTOK], BF16) for _ in range(2)]
            kofb = [kb.sb(es, [128, NCH, 128], BF16) for _ in range(2)]
            ivt = kb.sb(es, [128, NCH, 128], BF16)
            sg = kb.sb(es, [128, NLAT], BF16)
            etfb = [kb.sb(es, [128, NCH], F32) for _ in range(2)]
            SAll = [kb.sb(es, [128, NCH, 128], BF16) for _ in range(2)]
            Sst = [kb.sb(es, [128, 128], F32) for _ in range(2)]
            oh = [kb.sb(es, [128, NT], BF16) for _ in range(2)]
            Bq = P.bufs(2, "q_in")
            Bk = P.bufs(2, "k_in")
            Bko = P.bufs(2, "k_out")
            Bet = P.bufs(2, "et")
            BSAll = P.bufs(2, "SAll")
            BS = P.bufs(2, "S")
            Biv, Bsg = P.bufs(2, "hh")
            Boh = P.bufs(2, "oh")
            Bmo = P.buf("mo")
            def mk(n, dt=F32):
                return [[kb.sb(es, [128, HT], dt) for _ in range(2)] for _ in range(n)]
            sig, lf, kk, cum = mk(NS), mk(NS), mk(NS), mk(NS)
            kot = mk(NS, BF16)
            Bsig = [P.bufs(2, "sig") for _ in range(NS)]
            Blf = [P.bufs(2, "lf") for _ in range(NS)]
            Bkk = [P.bufs(2, "kk") for _ in range(NS)]
            Bcum = [P.bufs(2, "cum") for _ in range(NS)]
            Bkot = [P.bufs(2, "kot") for _ in range(NS)]
            t1 = [kb.sb(es, [128, HT], F32) for _ in range(NS)]
            qs = [kb.sb(es, [128, HT], F32) for _ in range(NS)]
            Bt1 = P.bufs(NS, "t1")
            Bqs = P.bufs(NS, "qs")
            aT = [kb.sb(es, [128, 2, 128], BF16) for _ in range(2)]
            BaT = P.bufs(2, "aT")
            ot = kb.sb(es, [128, NT], F32)
            orstd = kb.sb(es, [128, NT], F32)
            osq = kb.sb(es, [128, NT], BF16)
            Bot, Borstd, Bosq = P.bufs(3, "ho")
            pq = kb.ps(es, [128, NT], F32)
            pi = kb.ps(es, [128, 4, 128], F32)
            pgs = [[kb.ps(es, [128, NT], F32) for _ in range(2)] for _ in range(2)]
            ptr = [kb.ps(es, [128, 4, 128], BF16) for _ in range(2)]
            Bpq, Bpi = P.bufs(2, "hp")
            Bpgs = [P.bufs(2, "pg") for _ in range(2)]
            Bptr = P.bufs(2, "ptr")
            patt = [pgs[0][0][:, 0:256].rearrange("p (a t) -> p a t", a=2), pgs[0][1][:, 0:256].rearrange("p (a t) -> p a t", a=2)]
            Bpatt = [Bpgs[0][0], Bpgs[0][1]]
            psoL = [pgs[1][0], pq]
            BpsoL = [Bpgs[1][0], Bpq]
            puL = [pgs[1][1][:, 0:128], pi[:, 0, :]]
            BpuL = [Bpgs[1][1], Bpi]

            def interleave(gens, width, slots=2):
                active = []
                it = iter(gens)
                pending = None
                more = True
                while True:
                    while more and len(active) < width:
                        if pending is None:
                            try:
                                pending = next(it)
                            except StopIteration:
                                more = False
                                break
                        if any(t_ <= pending[0] - slots for t_, _ in active):
                            break
                        active.append(pending)
                        pending = None
                    if not active:
                        break
                    for item in list(active):
                        try:
                            next(item[1])
                        except StopIteration:
                            active.remove(item)

            for h in range(8):
                P.phase = "hg_tile"
                if h == 0:
                    P.dma("pool", DMA(Whs[0][:], d["hgw"][0]), writes=[BWhs[0]])
                if h + 1 < 8:
                    P.dma("pool", DMA(Whs[(h + 1) % 2][:], d["hgw"][h + 1]), writes=[BWhs[(h + 1) % 2]])
                W = Whs[h % 2]
                bW = BWhs[h % 2]

                def prelude(i, t0, nt, s):
                    sl = i % NS
                    nblk = nt // 128
                    c0 = t0 // 128
                    xs = xsb[sl]
                    BxmAll = Bxs[sl]
                    P.dma("sp", DMA(xs[:, :, :nt], xm1[:, :, t0:t0 + nt]), writes=[Bxs[sl]])
                    for kc in range(KC):
                        P.op("pe", MM(pq[:, :nt], W[:, kc, 0, :], xs[:, kc, :], start=(kc == 0), stop=(kc == KC - 1)),
                             reads=[bW, BxmAll], writes=[Bpq])
                    P.op("act", ACT(qs[sl][:, :nt], pq[:, :nt], AF.Copy), reads=[Bpq], writes=[Bqs[sl]])
                    yield
                    for b in range(nblk):
                        for kc in range(KC):
                            P.op("pe", MM(pi[:, b, :], xs[:, kc, b * 128:(b + 1) * 128], W[:, kc, 3, :], start=(kc == 0), stop=(kc == KC - 1)),
                                 reads=[bW, BxmAll], writes=[Bpi])
                    P.op("act", ACT(ivt[:, c0:c0 + nblk, :], pi[:, :nblk, :], AF.Copy), reads=[Bpi], writes=[Biv])
                    yield
                    if s == 0:
                        for kc in range(KC):
                            P.op("pe", MM(pq[:, :nt], W[:, kc, 4, :], xs[:, kc, :], start=(kc == 0), stop=(kc == KC - 1)),
                                 reads=[bW, BxmAll], writes=[Bpq])
                        P.op("act", ACT(sg[:, t0 - NCTX:t0 - NCTX + nt], pq[:, :nt], AF.Silu), reads=[Bpq], writes=[Bsg])
                        yield

                def chain(i, t0, nt, s, dr):
                    sl = i % NS
                    nblk = nt // 128
                    c0 = t0 // 128
                    xs = xsb[sl]
                    BxmAll = Bxs[sl]
                    p = pgs[i % 2][dr]
                    bp = Bpgs[i % 2][dr]
                    sg_, lf_, kk_, cum_, kot_ = sig[sl][dr], lf[sl][dr], kk[sl][dr], cum[sl][dr], kot[sl][dr]
                    bsg, blf, bkk, bcum, bkot = Bsig[sl][dr], Blf[sl][dr], Bkk[sl][dr], Bcum[sl][dr], Bkot[sl][dr]
                    for kc in range(KC):
                        P.op("pe", MM(p[:, :nt], W[:, kc, 1 + dr, :], xs[:, kc, :], start=(kc == 0), stop=(kc == KC - 1)),
                             reads=[bW, BxmAll], writes=[bp])
                    P.op("act", ACT(sg_[:, :nt], p[:, :nt], AF.Sigmoid), reads=[bp], writes=[bsg])
                    yield
                    P.op("act", ACT(lf_[:, :nt], sg_[:, :nt], AF.Ln, bias=lbv[:, dr, h:h + 1], scale=oml[:, dr, h:h + 1]),
                         reads=[bsg, Blb], writes=[blf])
                    P.op("dve", TS(kk_[:, :nt], sg_[:, :nt], noml[:, dr, h:h + 1], oml[:, dr, h:h + 1], ALU.mult, ALU.add),
                         reads=[bsg, Blb], writes=[bkk])
                    yield
                    P.op("dve", SCAN(cum_[:, :nt], cmask[:, :nt], lf_[:, :nt], 0.0, ALU.mult, ALU.add), reads=[blf, Bcm], writes=[bcum])
                    yield
                    cum3 = cum_[:, :nt].rearrange("p (c t) -> p c t", t=128)
                    et = etfb[dr]
                    P.op("act", ACT(et[:, c0:c0 + nblk], cum3[:, :, 127], AF.Exp), reads=[bcum], writes=[Bet[dr]])
                    if dr == 0:
                        cq = cum_[:, :nt]
                        bcq = bcum
                    else:
                        totb = cum3[:, :, 127:128].broadcast_to([128, nblk, 128])
                        t13 = t1[sl][:, :nt].rearrange("p (c t) -> p c t", t=128)
                        lf3 = lf_[:, :nt].rearrange("p (c t) -> p c t", t=128)
                        P.op("pool", TT(t13, totb, cum3, ALU.subtract), reads=[bcum], writes=[Bt1[sl]])
                        P.op("pool", TT(t13, t13, lf3, ALU.add), reads=[Bt1[sl], blf], writes=[Bt1[sl]])
                        cq = t1[sl][:, :nt]
                        bcq = Bt1[sl]
                    yield
                    P.op("act", ACT(lf_[:, :nt], cq, AF.Exp), reads=[bcq, blf], writes=[blf])
                    P.op("act", ACT(sg_[:, :nt], cq, AF.Exp, scale=-1.0), reads=[bcq, bsg], writes=[bsg])
                    yield
                    P.op("dve", TT(qfb[dr][:, t0:t0 + nt], qs[sl][:, :nt], lf_[:, :nt], ALU.mult), reads=[Bqs[sl], blf], writes=[Bq[dr]])
                    P.op("dve", TT(kk_[:, :nt], kk_[:, :nt], sg_[:, :nt], ALU.mult), reads=[bkk, bsg], writes=[bkk])
                    yield
                    P.op("act", ACT(kifb[dr][:, t0:t0 + nt], kk_[:, :nt], AF.Copy), reads=[bkk], writes=[Bk[dr]])
                    P.op("dve", TT(kot_[:, :nt].rearrange("p (c t) -> p c t", t=128), kk_[:, :nt].rearrange("p (c t) -> p c t", t=128),
                                    et[:, c0:c0 + nblk].unsqueeze(2).broadcast_to([128, nblk, 128]), ALU.mult),
                         reads=[bkk, Bet[dr]], writes=[bkot])
                    yield
                    for b in range(nblk):
                        P.op("pe", TR(ptr[dr][:, b, :], kot_[:, b * 128:(b + 1) * 128], ident[:]), reads=[bkot, Bident], writes=[Bptr[dr]])
                    P.op("dve", CP(kofb[dr][:, c0:c0 + nblk, :], ptr[dr][:, :nblk, :]), reads=[Bptr[dr]], writes=[Bko[dr]])
                    yield

                def all_chains():
                    for i, (t0, nt, s) in enumerate(htiles):
                        yield (i, prelude(i, t0, nt, s))
                        yield (i, chain(i, t0, nt, s, 0))
                        yield (i, chain(i, t0, nt, s, 1))
                interleave(all_chains(), 3 * NS if HG_WIDTH > 1 else 1, slots=NS)
                P.phase = "hg_state"
                for dr in range(2):
                    P.op("pool", MSET(Sst[dr][:], 0.0), writes=[BS[dr]])
                orderF = list(range(NCH))
                orderB = [1, 0] + list(range(NCH - 1, 1, -1))
                for n in range(NCH):
                    for dr, c in ((0, orderF[n]), (1, orderB[n])):
                        P.op("act", ACT(SAll[dr][:, c, :], Sst[dr][:], AF.Copy), reads=[BS[dr]], writes=[BSAll[dr]])
                        if n == NCH - 1:
                            continue
                        u = puL[dr]
                        P.op("pe", MM(u, kofb[dr][:, c, :], ivt[:, c, :]), reads=[Bko[dr], Biv], writes=[BpuL[dr]])
                        P.op("dve", STT(Sst[dr][:], Sst[dr][:], etfb[dr][:, c:c + 1], u, ALU.mult, ALU.add),
                             reads=[BS[dr], Bet[dr], BpuL[dr]], writes=[BS[dr]])
                P.phase = "hg_out"

                def att(c):
                    cs = slice(c * 128, (c + 1) * 128)
                    pa_ = patt[c % 2]
                    P.op("pe", MM(pa_[:, 0, :], kifb[0][:, cs], qfb[0][:, cs]), reads=[Bk[0], Bq[0]], writes=[Bpatt[c % 2]])
                    P.op("pe", MM(pa_[:, 1, :], kifb[1][:, cs], qfb[1][:, cs]), reads=[Bk[1], Bq[1]], writes=[Bpatt[c % 2]])
                    P.op("dve", TT(aT[c % 2][:], pa_, mfb[:], ALU.mult), reads=[Bpatt[c % 2], Bmfb], writes=[BaT[c % 2]])

                att(2)
                for c in range(2, NCH):
                    if c + 1 < NCH:
                        att(c + 1)
                    cs = slice(c * 128, (c + 1) * 128)
                    lt = c - 2
                    ti = lt // 4
                    j = lt % 4
                    a = aT[c % 2]
                    o = psoL[ti % 2]
                    bo = BpsoL[ti % 2]
                    oc = o[:, j * 128:(j + 1) * 128]
                    P.op("pe", MM(oc, ivt[:, c, :], a[:, 0, :], start=True, stop=False), reads=[Biv, BaT[c % 2]], writes=[bo])
                    P.op("pe", MM(oc, ivt[:, c, :], a[:, 1, :], start=False, stop=False), reads=[Biv, BaT[c % 2]], writes=[bo])
                    P.op("pe", MM(oc, SAll[0][:, c, :], qfb[0][:, cs], start=False, stop=False), reads=[BSAll[0], Bq[0]], writes=[bo])
                    P.op("pe", MM(oc, SAll[1][:, c, :], qfb[1][:, cs], start=False, stop=True), reads=[BSAll[1], Bq[1]], writes=[bo])
                    if j == 3:
                        tl = ti * 512
                        P.op("act", ACT(ot[:], o[:], AF.Copy), reads=[bo], writes=[Bot])
                        P.op("act", ACT(osq[:], ot[:], AF.Square), reads=[Bot], writes=[Bosq])
                        P.op("pe", MM(o[:], ones[:], osq[:]), reads=[Bosq, Bones], writes=[bo])
                        P.op("act", ACT(orstd[:], o[:], AF.Ln, bias=EPS, scale=1.0 / 128), reads=[bo], writes=[Borstd])
                        P.op("act", ACT(orstd[:], orstd[:], AF.Exp, scale=-0.5), reads=[Borstd], writes=[Borstd])
                        P.op("dve", STT(ot[:], ot[:], hgn[:, 0:1], orstd[:], ALU.mult, ALU.mult), reads=[Bot, Borstd, Bhgn], writes=[Bot])
                        ob = oh[ti % 2]
                        P.op("pool", TT(ob[:], ot[:], sg[:, tl:tl + 512], ALU.mult), reads=[Bot, Bsg], writes=[Boh[ti % 2]])
                        P.dma("sp", DMA(mo[:, h, tl:tl + 512], ob[:]), reads=[Boh[ti % 2]], acc=[Bmo])
            P.wait_all("sp", [Bmo])
    P.barrier()


LN8 = float(np.log(0.125))
HG_WIDTH = 6
L0_STOP = 99
L0_NT = 99


def phase_l0_proj(kb, d, hin, Qr, Kr, Ktm, Vtm, Bqkv, UT, SG, vec, Bvec, ones, Bones, ident, Bident, side=None):
    P = kb.P
    P.phase = "l0proj"
    NT = 256
    tiles = [(0, 256, 1)] + [(256 + NT * i, NT, 0) for i in range(NLAT // NT)]
    tiles = tiles[:L0_NT]
    with contextlib.ExitStack() as es:
        W = kb.sb(es, [128, KC, 2560], BF16)
        BW = P.bufs(5, "abw")
        hT = [kb.sb(es, [128, KC, NT], F32) for _ in range(2)]
        Bh = P.bufs(2, "h")
        xm = [kb.sb(es, [128, KC, NT], BF16) for _ in range(2)]
        Bxm = P.bufs(2, "xm")
        nb = [NormBufs(kb, es, NT) for _ in range(2)]
        cs = [kb.sb(es, [128, 2, NT], F32) for _ in range(2)]
        Bcs = P.bufs(2, "cs")
        t1 = [kb.sb(es, [128, NT], F32) for _ in range(2)]
        t2 = [kb.sb(es, [128, NT], F32) for _ in range(2)]
        Bt1 = P.bufs(2, "rt1")
        Bt2 = P.bufs(2, "rt2")
        ust = [kb.sb(es, [128, 4, NT], BF16) for _ in range(2)]
        gst = [kb.sb(es, [128, 4, NT], BF16) for _ in range(2)]
        Bust = P.bufs(2, "ust")
        Bgst = P.bufs(2, "gst")
        pq = kb.ps(es, [128, 2, NT], F32)
        ptr = kb.ps(es, [128, 2, 2, 128], BF16)
        pv = kb.ps(es, [128, 512], F32)
        pu_l = [kb.ps(es, [128, NT], F32) for _ in range(2)]
        Bpq, Bptr, Bpv = P.bufs(3, "pp")
        Bpu = P.bufs(2, "pu")
        Bsc = P.buf("scr")
        for i in range(5):
            P.dma("pool", DMA(W[:, :, 512 * i:512 * (i + 1)], d["abw"][:, :, 512 * i:512 * (i + 1)]), writes=[BW[i]])
        rot = W[:, :, 2048:2560].rearrange("p k (h x) -> p k h x", x=64)
        P.op("dve", TS(rot[:, :, :, 0:32], rot[:, :, :, 0:32], -1.0, 0.0, ALU.mult, ALU.add), reads=[BW[4]], writes=[BW[4]])

        def wb(col):
            return BW[col // 512]

        def tile_chain(i):
            t0, nt, s = tiles[i]
            sl = i % 2
            P.dma("sp", DMA(hT[sl][:, :, :nt], hin[:, :, t0:t0 + nt]), writes=[Bh[sl]])
            if s == 0:
                P.dma("sp", DMA(cs[sl][:, :, :nt], d["rope"][:, :, t0 - NCTX:t0 - NCTX + nt]), writes=[Bcs[sl]])
            yield
            c0 = t0 // 128
            nblk = nt // 128
            xm_ = xm[sl]
            bxm = Bxm[sl]
            norm_stage_a(kb, nb[sl], hT[sl], Bh[sl], nt)
            yield
            norm_stage_b(kb, nb[sl], nt, ones, Bones)
            yield
            G_, S_ = vec[:, 0, s, 0, :], vec[:, 0, s, 1, :]
            for kc in range(KC):
                t_ = nb[sl].tmp[kc % 2]
                bt_ = nb[sl].Btmp[kc % 2]
                P.op("dve", STT(t_[:, :nt], hT[sl][:, kc, :nt], G_[:, kc:kc + 1], nb[sl].rstd[:, :nt], ALU.mult, ALU.mult),
                     reads=[Bh[sl], nb[sl].Brstd, Bvec], writes=[bt_])
                P.op("act", ACT(xm_[:, kc, :nt], t_[:, :nt], AF.Identity, bias=S_[:, kc:kc + 1]), reads=[bt_, Bvec], writes=[bxm])
                if kc % 2 == 1:
                    yield
            cst = cs[sl]
            for (dst, base, rbase) in ((Qr, 0, 2048), (Kr, 256, 2304)):
                for pair in range(2):
                    col = base + 128 * pair
                    rcol = rbase + 128 * pair
                    for kc in range(KC):
                        P.op("pe", MM(pq[:, 0, :nt], W[:, kc, col:col + 128], xm_[:, kc, :nt], start=(kc == 0), stop=(kc == KC - 1)),
                             reads=[wb(col), bxm], writes=[Bpq])
                    if s == 0:
                        for kc in range(KC):
                            P.op("pe", MM(pq[:, 1, :nt], W[:, kc, rcol:rcol + 128], xm_[:, kc, :nt], start=(kc == 0), stop=(kc == KC - 1)),
                                 reads=[wb(rcol), bxm], writes=[Bpq])
                        P.op("dve", TT(t1[sl][:, :nt], pq[:, 0, :nt], cst[:, 0, :nt], ALU.mult), reads=[Bpq, Bcs[sl]], writes=[Bt1[sl]])
                        P.op("dve", TT(t2[sl][:, :nt], pq[:, 1, :nt], cst[:, 1, :nt], ALU.mult), reads=[Bpq, Bcs[sl]], writes=[Bt2[sl]])
                        yield
                        P.op("pool", TT(dst[:, pair, t0:t0 + nt], t1[sl][:, :nt], t2[sl][:, :nt], ALU.add), reads=[Bt1[sl], Bt2[sl]], writes=[Bqkv])
                    else:
                        P.op("act", ACT(dst[:, pair, t0:t0 + nt], pq[:, 0, :nt], AF.Copy), reads=[Bpq], writes=[Bqkv])
                        yield
            for b in range(nblk):
                for pair in range(2):
                    P.op("pe", TR(ptr[:, b, pair, :], Kr[:, pair, t0 + b * 128:t0 + (b + 1) * 128], ident[:]), reads=[Bqkv, Bident], writes=[Bptr])
            P.op("dve", CP(Ktm[:, c0:c0 + nblk, :].rearrange("p c (a x) -> p c a x", a=2), ptr[:, :nblk]), reads=[Bptr], writes=[Bqkv])
            yield
            for b in range(nblk):
                for kc in range(KC):
                    P.op("pe", MM(pv[:], xm_[:, kc, b * 128:(b + 1) * 128], W[:, kc, 512:1024], start=(kc == 0), stop=(kc == KC - 1)),
                         reads=[BW[1], bxm], writes=[Bpv])
                P.op("act", ACT(Vtm[:, c0 + b, :], pv[:], AF.Copy), reads=[Bpv], writes=[Bqkv])
                yield
            us = ust[sl]
            gs = gst[sl]
            for j in range(4):
                pp = pu_l[j % 2][:, :nt]
                col = 1024 + 128 * j
                for kc in range(KC):
                    P.op("pe", MM(pp, W[:, kc, col:col + 128], xm_[:, kc, :nt], start=(kc == 0), stop=(kc == KC - 1)),
                         reads=[wb(col), bxm], writes=[Bpu[j % 2]])
                P.op("act", ACT(us[:, j, :nt], pp, AF.Copy), reads=[Bpu[j % 2]], writes=[Bust[sl]])
                yield
            P.dma("sp", DMA(UT[:, :, t0:t0 + nt], us[:, :, :nt]), reads=[Bust[sl]], acc=[Bsc])
            for j in range(4):
                pp = pu_l[j % 2][:, :nt]
                col = 1536 + 128 * j
                for kc in range(KC):
                    P.op("pe", MM(pp, W[:, kc, col:col + 128], xm_[:, kc, :nt], start=(kc == 0), stop=(kc == KC - 1)),
                         reads=[wb(col), bxm], writes=[Bpu[j % 2]])
                P.op("act", ACT(gs[:, j, :nt], pp, AF.Silu), reads=[Bpu[j % 2]], writes=[Bgst[sl]])
                yield
            P.dma("sp", DMA(SG[:, :, t0:t0 + nt], gs[:, :, :nt]), reads=[Bgst[sl]], acc=[Bsc])
            if side is not None:
                for _ in range(2):
                    try:
                        next(side)
                    except StopIteration:
                        pass
            yield

        interleave(((i, tile_chain(i)) for i in range(len(tiles))), 2, slots=2)
        if side is not None:
            for _ in side:
                pass
        P.wait_all("sp", [Bsc])
    P.barrier()


def phase_ret(kb, d, Qr, Kr, Ktm, Vtm, Bqkv, SG, mo, ones, Bones, mfb, Bmfb):
    P = kb.P
    P.phase = "ret"
    NCH = NTOK // 128
    with contextlib.ExitStack() as es:
        retl = kb.sb(es, [128, 8], F32)
        retlP = kb.sb(es, [128, 2, 2], F32)
        cdP = kb.sb(es, [128, 2, 2], F32)
        dF = kb.sb(es, [128, 2, 128], F32)
        posc = kb.sb(es, [128, 2, 128], F32)
        jcol = kb.sb(es, [128, 2, 4], F32)
        hm = kb.sb(es, [128, 4], F32)
        Bhm = P.buf("hm")
        P.dma("sp", DMA(hm[:], d["hm"]), writes=[Bhm])
        qzb = [kb.sb(es, [128, 4, 128], BF16) for _ in range(2)]
        Bqz = P.bufs(2, "qz")
        Dm = kb.sb(es, [128, 4, 128], F32)
        QD = kb.sb(es, [128, 2, 2, 128], F32)
        kd = kb.sb(es, [128, 2, 4], F32)
        e1 = kb.sb(es, [128, 2, 128], F32)
        Bc = P.buf("retc")
        Bin = P.bufs(5, "retin")
        P.dma("sp", DMA(retl[:], d["retl"]), writes=[Bin[0]])
        P.dma("sp", DMA(retlP[:], d["retlP"]), writes=[Bin[1]])
        P.dma("sp", DMA(dF[:], d["dF"]), writes=[Bin[2]])
        P.dma("sp", DMA(posc[:], d["posc"]), writes=[Bin[3]])
        P.dma("sp", DMA(jcol[:], d["jcol"]), writes=[Bin[4]])
        P.op("act", ACT(retl[:], retl[:], AF.Sigmoid), reads=[Bin[0]], writes=[Bin[0]])
        P.op("act", ACT(retlP[:], retlP[:], AF.Sigmoid), reads=[Bin[1]], writes=[Bin[1]])
        P.op("act", ACT(retl[:], retl[:], AF.Ln), reads=[Bin[0]], writes=[Bin[0]])
        P.op("act", ACT(retlP[:], retlP[:], AF.Ln), reads=[Bin[1]], writes=[Bin[1]])
        for h in range(4):
            P.op("act", ACT(e1[:, 0, :], dF[:, 0, :], AF.Exp, bias=LN8, scale=retl[:, h:h + 1]), reads=[Bin[0], Bin[2], Bc], writes=[Bc])
            P.op("act", ACT(e1[:, 1, :], dF[:, 1, :], AF.Exp, bias=LN8, scale=retl[:, 4 + h:5 + h]), reads=[Bin[0], Bin[2], Bc], writes=[Bc])
            P.op("dve", TT(e1[:], e1[:], mfb[:], ALU.mult), reads=[Bc, Bmfb], writes=[Bc])
            P.op("dve", TT(Dm[:, h, :], e1[:, 0, :], e1[:, 1, :], ALU.add), reads=[Bc], writes=[Bc])
        for dr in range(2):
            for pair in range(2):
                P.op("act", ACT(QD[:, dr, pair, :], posc[:, dr, :], AF.Exp, scale=retlP[:, dr, pair:pair + 1]), reads=[Bin[1], Bin[3], Bc], writes=[Bc])
            P.op("dve", TT(kd[:, dr, :], jcol[:, dr, :], retl[:, 4 * dr:4 * dr + 4], ALU.mult), reads=[Bin[0], Bin[4], Bc], writes=[Bc])
        P.op("act", ACT(kd[:], kd[:], AF.Exp, bias=LN8), reads=[Bc], writes=[Bc])
        P.op("act", ACT(cdP[:], retlP[:], AF.Exp, scale=128.0), reads=[Bin[1], Bc], writes=[Bc])
        SAll = [kb.sb(es, [128, NCH, 2, 128], BF16) for _ in range(2)]
        Sst = [kb.sb(es, [128, 2, 128], F32) for _ in range(2)]
        kdt = [kb.sb(es, [128, 4, 64], BF16) for _ in range(2)]
        aT = [kb.sb(es, [128, 4, 128], BF16) for _ in range(2)]
        qfb = [kb.sb(es, [128, 2, 2, 128], BF16) for _ in range(2)]
        sgt = [kb.sb(es, [128, 4, 128], BF16) for _ in range(2)]
        ot = [kb.sb(es, [128, 4, 128], F32) for _ in range(2)]
        osq = [kb.sb(es, [128, 4, 128], BF16) for _ in range(2)]
        orstd = [kb.sb(es, [128, 4, 128], F32) for _ in range(2)]
        ob = [kb.sb(es, [128, 4, 128], BF16) for _ in range(2)]
        BSAll = P.bufs(2, "SAll")
        BS = P.bufs(2, "S")
        Bkdt = P.bufs(2, "kdt")
        BaT = P.bufs(2, "aT")
        Bqfb = P.bufs(2, "qfb")
        Bsgt = P.bufs(2, "sgt")
        Bot = P.bufs(2, "ot")
        Bosq = P.bufs(2, "osq")
        Borstd = P.bufs(2, "orstd")
        Bob = P.bufs(2, "ob")
        Bmo = P.buf("mo")
        pu_r = [kb.ps(es, [128, 2, 128], F32) for _ in range(2)]
        patt = [kb.ps(es, [128, 4, 128], F32) for _ in range(2)]
        po = [kb.ps(es, [128, 4, 128], F32) for _ in range(2)]
        pss0 = kb.ps(es, [128, 4, 128], F32)
        pss = [pss0, pss0]
        Bpu = P.bufs(2, "pu")
        Bpatt = P.bufs(2, "patt")
        Bpo = P.bufs(2, "po")
        Bpss0 = P.buf("pss")
        Bpss = [Bpss0, Bpss0]
        for dr in range(2):
            P.op("pool", MSET(Sst[dr][:], 0.0), writes=[BS[dr]])
        orderF = list(range(NCH))
        orderB = [1, 0] + list(range(NCH - 1, 1, -1))
        for n in range(NCH):
            for dr, c in ((0, orderF[n]), (1, orderB[n])):
                S = Sst[dr]
                P.op("act", ACT(SAll[dr][:, c], S[:], AF.Copy), reads=[BS[dr]], writes=[BSAll[dr]])
                if n == NCH - 1:
                    continue
                k_ = kdt[dr]
                P.op("pool", TT(k_[:], Ktm[:, c, :].rearrange("p (h x) -> p h x", h=4),
                                kd[:, dr, :].unsqueeze(2).broadcast_to([128, 4, 64]), ALU.mult), reads=[Bqkv, Bc], writes=[Bkdt[dr]])
                u = pu_r[dr]
                for h in range(4):
                    hp = (h % 2) * 64
                    P.op("pe", MM(u[hp:hp + 64, h // 2, :], k_[:, h, :], Vtm[:, c, h * 128:(h + 1) * 128]), reads=[Bkdt[dr], Bqkv], writes=[Bpu[dr]])
                P.op("dve", TT(S[:], S[:], cdP[:, dr, :].unsqueeze(2).broadcast_to([128, 2, 128]), ALU.mult), reads=[BS[dr], Bc], writes=[BS[dr]])
                P.op("dve", TT(S[:], S[:], u[:], ALU.add), reads=[BS[dr], Bpu[dr]], writes=[BS[dr]])

        def out_chain(c):
            sl = c % 2
            cs_ = slice(c * 128, (c + 1) * 128)
            P.dma("sp", DMA(sgt[sl][:], SG[:, :, cs_]), writes=[Bsgt[sl]])
            qz = qzb[sl]
            P.op("pool", TT(qz[:].rearrange("p (a b) i -> p a b i", a=2), Qr[:, :, cs_].unsqueeze(2).broadcast_to([128, 2, 2, 128]),
                            hm[:].rearrange("p (a b) -> p a b", a=2).unsqueeze(3).broadcast_to([128, 2, 2, 128]), ALU.mult),
                 reads=[Bqkv, Bhm], writes=[Bqz[sl]])
            qq = qfb[sl]
            P.op("pool", TT(qq[:], Qr[:, :, cs_].unsqueeze(1).broadcast_to([128, 2, 2, 128]), QD[:], ALU.mult), reads=[Bqkv, Bc], writes=[Bqfb[sl]])
            yield
            pa_ = patt[sl]
            for h in range(4):
                P.op("pe", MM(pa_[:, h, :], Kr[:, h // 2, cs_], qz[:, h, :]), reads=[Bqkv, Bqz[sl]], writes=[Bpatt[sl]])
            yield
            a = aT[sl]
            P.op("dve", TT(a[:], pa_[:], Dm[:], ALU.mult), reads=[Bpatt[sl], Bc], writes=[BaT[sl]])
            yield
            o = po[sl]
            for h in range(4):
                hp = (h % 2) * 64
                P.op("pe", MM(o[:, h, :], Vtm[:, c, h * 128:(h + 1) * 128], a[:, h, :], start=True, stop=False), reads=[Bqkv, BaT[sl]], writes=[Bpo[sl]])
                P.op("pe", MM(o[:, h, :], SAll[0][hp:hp + 64, c, h // 2, :], qq[hp:hp + 64, 0, h // 2, :], start=False, stop=False),
                     reads=[BSAll[0], Bqfb[sl]], writes=[Bpo[sl]])
                P.op("pe", MM(o[:, h, :], SAll[1][hp:hp + 64, c, h // 2, :], qq[hp:hp + 64, 1, h // 2, :], start=False, stop=True),
                     reads=[BSAll[1], Bqfb[sl]], writes=[Bpo[sl]])
            yield
            P.op("act", ACT(ot[sl][:], o[:], AF.Copy), reads=[Bpo[sl]], writes=[Bot[sl]])
            P.op("act", ACT(osq[sl][:], ot[sl][:], AF.Square), reads=[Bot[sl]], writes=[Bosq[sl]])
            yield
            P.op("pe", MM(pss[sl][:], ones[:], osq[sl][:]), reads=[Bosq[sl], Bones], writes=[Bpss[sl]])
            P.op("act", ACT(orstd[sl][:], pss[sl][:], AF.Ln, bias=EPS, scale=1.0 / 128), reads=[Bpss[sl]], writes=[Borstd[sl]])
            P.op("act", ACT(orstd[sl][:], orstd[sl][:], AF.Exp, scale=-0.5), reads=[Borstd[sl]], writes=[Borstd[sl]])
            yield
            P.op("dve", TT(ot[sl][:], ot[sl][:], orstd[sl][:], ALU.mult), reads=[Bot[sl], Borstd[sl]], writes=[Bot[sl]])
            yield
            P.op("pool", TT(ob[sl][:], ot[sl][:], sgt[sl][:], ALU.mult), reads=[Bot[sl], Bsgt[sl]], writes=[Bob[sl]])
            P.dma("sp", DMA(mo[:, 0:4, cs_], ob[sl][:]), reads=[Bob[sl]], acc=[Bmo])
            yield

        interleave(((c, out_chain(c)) for c in range(NCH)), 2, slots=2)
        P.wait_all("sp", [Bmo])
    P.barrier()


TWO_PI = float(2.0 * np.pi)
NCK = NTOK // 8
S5_POW = (1, 2, 4, 6, 8, 16, 24, 32, 64, 96, 128, 256, 384, 512)
S5_LEVELS = ((1, (1,)), (2, (1, 2, 3)), (8, (1, 2, 3)), (32, (1, 2, 3)), (128, (1, 2, 3)), (512, (1,)))
NPOW = len(S5_POW)


def s5_scratch(nc):
    return {"Toep": nc.dram_tensor("s5Toep", [128, 64, 128], BF16).ap(), "Bz": nc.dram_tensor("s5Bz", [128, 64, 128], BF16).ap(),
            "Cy": nc.dram_tensor("s5Cy", [128, 64, 128], F32).ap(), "Vt": nc.dram_tensor("s5Vt", [128, 64, NPOW, 2], F32).ap()}


def s5_setup_gen(kb, d, S5M):
    P = kb.P
    with contextlib.ExitStack() as es0:
        Toep = kb.sb(es0, [128, 16, 128], BF16)
        Bz = kb.sb(es0, [128, 16, 128], BF16)
        Cy = kb.sb(es0, [128, 16, 128], F32)
        HR = kb.sb(es0, [128, 64, 14], F32)
        HI = kb.sb(es0, [128, 64, 14], F32)
        imat = kb.sb(es0, [128, 2, 128], F32)
        Vt = kb.sb(es0, [128, 64, NPOW, 2], F32)
        BToep, BBz, BCy, BH, Bimat, Bst = P.bufs(6, "s5p")
        P.dma("sp", DMA(imat[:], d["imat"]), writes=[Bimat])
        with contextlib.ExitStack() as es:
            a = kb.sb(es, [128, 2, 64], F32)
            dtl = kb.sb(es, [128, 64], F32)
            bb = kb.sb(es, [128, 2, 64, 16], F32)
            cc = kb.sb(es, [128, 2, 64, 16], F32)
            nvec = kb.sb(es, [128, 2, 16], F32)
            tmask = kb.sb(es, [128, 2, 128], F32)
            Bin = P.bufs(6, "s5in")
            P.dma("sp", DMA(a[:], d["s5a"]), writes=[Bin[0]])
            P.dma("sp", DMA(dtl[:], d["s5dt"]), writes=[Bin[1]])
            P.dma("sp", DMA(bb[:], d["s5b"]), writes=[Bin[2]])
            P.dma("sp", DMA(cc[:], d["s5c"]), writes=[Bin[3]])
            P.dma("sp", DMA(nvec[:], d["nvec"]), writes=[Bin[4]])
            P.dma("sp", DMA(tmask[:], d["tmask"]), writes=[Bin[5]])
            P.op("dve", TS(bb[0:64, 1], bb[0:64, 1], -1.0, 0.0, ALU.mult, ALU.add), reads=[Bin[2]], writes=[Bin[2]])
            P.op("dve", TS(cc[64:128, 0], cc[64:128, 0], -1.0, 0.0, ALU.mult, ALU.add), reads=[Bin[3]], writes=[Bin[3]])
            P.op("dve", TS(cc[:, 1], cc[:, 1], -1.0, 0.0, ALU.mult, ALU.add), reads=[Bin[3]], writes=[Bin[3]])
            yield
            rho = kb.sb(es, [128, 64], F32)
            th = kb.sb(es, [128, 64], F32)
            Bs = P.buf("s5s")
            P.op("act", ACT(dtl[:], dtl[:], AF.Exp), reads=[Bin[1]], writes=[Bin[1]])
            P.op("dve", TT(rho[:], a[:, 0, :], dtl[:], ALU.mult), reads=[Bin[0], Bin[1]], writes=[Bs])
            P.op("dve", TT(th[:], a[:, 1, :], dtl[:], ALU.mult), reads=[Bin[0], Bin[1], Bs], writes=[Bs])
            PwR = kb.sb(es, [128, 2, 64, 16], F32)
            PwI = kb.sb(es, [128, 2, 64, 16], F32)
            arg = kb.sb(es, [128, 64, 16], F32)
            r1 = kb.sb(es, [128, 64, 16], F32)
            r2 = kb.sb(es, [128, 64, 16], F32)
            ri = kb.sb(es, [128, 64, 16], I32)
            mg = kb.sb(es, [128, 64, 16], F32)
            for tb in range(2):
                nb_ = nvec[:, tb, :].unsqueeze(1).broadcast_to([128, 64, 16])
                P.op("dve", TT(arg[:], th[:].unsqueeze(2).broadcast_to([128, 64, 16]), nb_, ALU.mult), reads=[Bs, Bin[4]], writes=[Bs])
                P.op("dve", TT(mg[:], rho[:].unsqueeze(2).broadcast_to([128, 64, 16]), nb_, ALU.mult), reads=[Bs, Bin[4]], writes=[Bs])
                P.op("act", ACT(mg[:], mg[:], AF.Exp), reads=[Bs], writes=[Bs])
                for (off, dst) in ((0.0, PwI), (0.25, PwR)):
                    P.op("dve", TS(r1[:], arg[:], 1.0 / TWO_PI, off, ALU.mult, ALU.add), reads=[Bs], writes=[Bs])
                    P.op("dve", CP(ri[:], r1[:]), reads=[Bs], writes=[Bs])
                    P.op("dve", CP(r2[:], ri[:]), reads=[Bs], writes=[Bs])
                    P.op("dve", TT(r1[:], r1[:], r2[:], ALU.subtract), reads=[Bs], writes=[Bs])
                    P.op("act", ACT(r2[:], r1[:], AF.Sin, scale=TWO_PI), reads=[Bs], writes=[Bs])
                    P.op("dve", TT(dst[:, tb], r2[:], mg[:], ALU.mult), reads=[Bs], writes=[Bs])
                    yield
            yield
            lr = PwR[:, 0, :, 8]
            li = PwI[:, 0, :, 8]
            nr = kb.sb(es, [128, 64], F32)
            den = kb.sb(es, [128, 64], F32)
            fr = kb.sb(es, [128, 64], F32)
            fi = kb.sb(es, [128, 64], F32)
            tq = kb.sb(es, [128, 64], F32)
            P.op("dve", TS(nr[:], lr, 1.0, -1.0, ALU.mult, ALU.add), reads=[Bs], writes=[Bs])
            P.op("dve", TT(den[:], a[:, 0, :], a[:, 0, :], ALU.mult), reads=[Bs, Bin[0]], writes=[Bs])
            P.op("dve", TT(tq[:], a[:, 1, :], a[:, 1, :], ALU.mult), reads=[Bs, Bin[0]], writes=[Bs])
            P.op("dve", TT(den[:], den[:], tq[:], ALU.add), reads=[Bs], writes=[Bs])
            P.op("dve", RCP(den[:], den[:]), reads=[Bs], writes=[Bs])
            P.op("dve", TT(fr[:], nr[:], a[:, 0, :], ALU.mult), reads=[Bs, Bin[0]], writes=[Bs])
            P.op("dve", TT(tq[:], li, a[:, 1, :], ALU.mult), reads=[Bs, Bin[0]], writes=[Bs])
            P.op("dve", TT(fr[:], fr[:], tq[:], ALU.add), reads=[Bs], writes=[Bs])
            P.op("dve", TT(fr[:], fr[:], den[:], ALU.mult), reads=[Bs], writes=[Bs])
            P.op("dve", TT(fi[:], li, a[:, 0, :], ALU.mult), reads=[Bs, Bin[0]], writes=[Bs])
            P.op("dve", TT(tq[:], nr[:], a[:, 1, :], ALU.mult), reads=[Bs, Bin[0]], writes=[Bs])
            P.op("dve", TT(fi[:], fi[:], tq[:], ALU.subtract), reads=[Bs], writes=[Bs])
            P.op("dve", TT(fi[:], fi[:], den[:], ALU.mult), reads=[Bs], writes=[Bs])
            bbq = kb.sb(es, [128, 2, 64, 16], F32)
            t64 = kb.sb(es, [128, 64, 16], F32)
            frb = fr[:].unsqueeze(2).broadcast_to([128, 64, 16])
            fib = fi[:].unsqueeze(2).broadcast_to([128, 64, 16])
            P.op("dve", TT(bbq[:, 0], bb[:, 0], frb, ALU.mult), reads=[Bs, Bin[2]], writes=[Bs])
            P.op("dve", TT(t64[:], bb[:, 1], fib, ALU.mult), reads=[Bs, Bin[2]], writes=[Bs])
            P.op("dve", TT(bbq[:, 0], bbq[:, 0], t64[:], ALU.add), reads=[Bs], writes=[Bs])
            P.op("dve", TT(bbq[:, 1], bb[:, 1], frb, ALU.mult), reads=[Bs, Bin[2]], writes=[Bs])
            P.op("dve", TT(t64[:], bb[:, 0], fib, ALU.mult), reads=[Bs, Bin[2]], writes=[Bs])
            P.op("dve", TT(bbq[:, 1], bbq[:, 1], t64[:], ALU.subtract), reads=[Bs], writes=[Bs])
            yield
            pidx = {n: i for i, n in enumerate(S5_POW)}
            P.op("dve", CP(HR[:, :, 0], PwR[:, 0, :, 15]), reads=[Bs], writes=[BH])
            P.op("dve", CP(HI[:, :, 0], PwI[:, 0, :, 15]), reads=[Bs, BH], writes=[BH])
            for k in range(9):
                a_, b_ = pidx[1 << k], pidx[1 << (k + 1)]
                P.op("dve", TT(nr[:], HR[:, :, a_], HR[:, :, a_], ALU.mult), reads=[BH, Bs], writes=[Bs])
                P.op("dve", TT(tq[:], HI[:, :, a_], HI[:, :, a_], ALU.mult), reads=[BH, Bs], writes=[Bs])
                P.op("dve", TT(HR[:, :, b_], nr[:], tq[:], ALU.subtract), reads=[Bs, BH], writes=[BH])
                P.op("dve", STT(HI[:, :, b_], HR[:, :, a_], 2.0, HI[:, :, a_], ALU.mult, ALU.mult), reads=[BH], writes=[BH])
            for (x_, y_) in ((4, 2), (16, 8), (64, 32), (256, 128)):
                a_, b_, c_ = pidx[x_], pidx[y_], pidx[x_ + y_]
                P.op("dve", TT(nr[:], HR[:, :, a_], HR[:, :, b_], ALU.mult), reads=[BH, Bs], writes=[Bs])
                P.op("dve", TT(tq[:], HI[:, :, a_], HI[:, :, b_], ALU.mult), reads=[BH, Bs], writes=[Bs])
                P.op("dve", TT(HR[:, :, c_], nr[:], tq[:], ALU.subtract), reads=[Bs, BH], writes=[BH])
                P.op("dve", TT(nr[:], HR[:, :, a_], HI[:, :, b_], ALU.mult), reads=[BH, Bs], writes=[Bs])
                P.op("dve", TT(tq[:], HI[:, :, a_], HR[:, :, b_], ALU.mult), reads=[BH, Bs], writes=[Bs])
                P.op("dve", TT(HI[:, :, c_], nr[:], tq[:], ALU.add), reads=[Bs, BH], writes=[BH])
            yield
            P.op("dve", CP(Vt[0:64, :, :, 0], HR[0:64]), reads=[BH], writes=[BH])
            P.op("dve", TS(Vt[64:128, :, :, 0], HI[64:128], -1.0, 0.0, ALU.mult, ALU.add), reads=[BH], writes=[BH])
            P.op("dve", CP(Vt[0:64, :, :, 1], HI[0:64]), reads=[BH], writes=[BH])
            P.op("dve", CP(Vt[64:128, :, :, 1], HR[64:128]), reads=[BH], writes=[BH])
            T1 = kb.sb(es, [128, 16, 8, 16], F32)
            T2 = kb.sb(es, [128, 16, 8, 16], F32)
            Lm = kb.sb(es, [128, 16, 128], BF16)
            Rm = kb.sb(es, [128, 16, 128], BF16)
            Lz = kb.sb(es, [128, 16, 128], F32)
            pT = kb.ps(es, [128, 4, 128], F32)
            pB = kb.ps(es, [128, 4, 128], F32)
            BpT, BpB, BL = P.bufs(3, "s5m")
            spec = {0: dict(L=(1, 8), R=(0, 7), C=(0, 8), Z=(1, 1)),
                    1: dict(L=(0, 7), R=(1, 8), C=(1, 0), Z=(0, 7))}

            def build(dst, which, dd, g0, X0, X1, out_is_tensor=True):
                tb, st = spec[dd][which]
                gs = slice(dd * 32 + g0, dd * 32 + g0 + 16)
                pr = PwR[:, tb, gs, st:st + 8].unsqueeze(3).broadcast_to([128, 16, 8, 16])
                pi_ = PwI[:, tb, gs, st:st + 8].unsqueeze(3).broadcast_to([128, 16, 8, 16])
                x0 = X0[:, gs, :].unsqueeze(2).broadcast_to([128, 16, 8, 16])
                x1 = X1[:, gs, :].unsqueeze(2).broadcast_to([128, 16, 8, 16])
                P.op("dve", TT(T1[:], pr, x0, ALU.mult), reads=[Bs, Bin[2], Bin[3], BL], writes=[BL])
                P.op("pool", TT(T2[:], pi_, x1, ALU.mult), reads=[Bs, Bin[2], Bin[3], BL], writes=[BL])
                P.op("dve", TT(dst, T1[:].rearrange("p g s m -> p g (s m)"), T2[:].rearrange("p g s m -> p g (s m)"), ALU.add),
                     reads=[BL, BCy], writes=[BL, BCy])

            for dd in range(2):
                for g0 in (0, 16):
                    gd0 = dd * 32 + g0
                    build(Lm[:], "L", dd, g0, bbq[:, 0], bbq[:, 1])
                    build(Rm[:], "R", dd, g0, cc[:, 0], cc[:, 1])
                    for i4 in range(4):
                        for gi in range(4):
                            g = i4 * 4 + gi
                            P.op("pe", MM(pT[:, gi, :], Lm[:, g, :], Rm[:, g, :]), reads=[BL], writes=[BpT])
                        P.op("dve", TT(Toep[:, i4 * 4:i4 * 4 + 4, :], pT[:],
                                       tmask[:, dd, :].unsqueeze(1).broadcast_to([128, 4, 128]), ALU.mult),
                             reads=[BpT, Bin[5]], writes=[BToep])
                    build(Cy[:], "C", dd, g0, cc[:, 0], cc[:, 1])
                    build(Lz[:], "Z", dd, g0, bbq[:, 0], bbq[:, 1])
                    for i4 in range(4):
                        for gi in range(4):
                            g = i4 * 4 + gi
                            P.op("pe", TR(pB[:, gi, :], Lz[:, g, :], imat[:, 0, :]), reads=[BL, Bimat], writes=[BpB])
                        P.op("act", ACT(Bz[:, i4 * 4:i4 * 4 + 4, :], pB[:], AF.Copy), reads=[BpB], writes=[BBz])
                    P.dma("sp", DMA(S5M["Toep"][:, gd0:gd0 + 16, :], Toep[:]), reads=[BToep], acc=[Bst])
                    P.dma("sp", DMA(S5M["Bz"][:, gd0:gd0 + 16, :], Bz[:]), reads=[BBz], acc=[Bst])
                    P.dma("sp", DMA(S5M["Cy"][:, gd0:gd0 + 16, :], Cy[:]), reads=[BCy], acc=[Bst])
                    yield
            P.dma("sp", DMA(S5M["Vt"], Vt[:]), reads=[BH], acc=[Bst])
            P.wait_all("sp", [Bst])
    yield


def phase_s5_main(kb, d, UT, mo, S5M):
    P = kb.P
    with contextlib.ExitStack() as es0:
        Toep = kb.sb(es0, [128, 64, 128], BF16)
        Bz = kb.sb(es0, [128, 64, 128], BF16)
        Cy = kb.sb(es0, [128, 64, 128], F32)
        psel = kb.sb(es0, [128, 8, 240], BF16)
        imat = kb.sb(es0, [128, 2, 128], F32)
        Vt = kb.sb(es0, [128, 64, NPOW, 2], F32)
        BToep, BBz, BCy, BH, Bpsel, Bimat = P.bufs(6, "s5p")
        P.dma("pool", DMA(psel[:], d["psel"]), writes=[Bpsel])
        P.dma("sp", DMA(imat[:], d["imat"]), writes=[Bimat])
        P.dma("sp", DMA(Toep[:], S5M["Toep"]), writes=[BToep])
        P.dma("sp", DMA(Bz[:], S5M["Bz"]), writes=[BBz])
        P.dma("sp", DMA(Cy[:], S5M["Cy"]), writes=[BCy])
        P.dma("sp", DMA(Vt[:], S5M["Vt"]), writes=[BH])
        P.phase = "s5main"
        with contextlib.ExitStack() as es:
            ygT = kb.sb(es, [128, 4, NTOK], BF16)
            Bygt = P.buf("ygT")
            with contextlib.ExitStack() as es2:
                UTs = [kb.sb(es2, [128, NTOK], BF16) for _ in range(2)]
                BUTs = P.bufs(2, "UTs")
                Ug = kb.sb(es2, [128, NCK], BF16)
                usm = kb.sb(es2, [128, 8, NCK], BF16)
                Busm = P.buf("usm")
                H = [kb.sb(es2, [128, NCK], F32R) for _ in range(2)]
                Mall = [kb.sb(es2, [128, NPOW, 128], F32R) for _ in range(2)]
                Yall = kb.sb(es2, [128, 8, NCK], BF16)
                s5d = kb.sb(es2, [128, 4], F32)
                ytmp = kb.sb(es2, [128, 512], F32)
                yt2 = kb.sb(es2, [128, 512], F32)
                Byt2 = P.buf("yt2")
                BUg, BMtmp, BYall, Bs5d, Bytmp = P.bufs(5, "s5l")
                BHh = P.bufs(2, "H")
                BMall = P.bufs(2, "Mall")
                pA = [kb.ps(es2, [128, 512], F32) for _ in range(2)]
                pB2 = [kb.ps(es2, [128, 32], F32) for _ in range(2)]
                pU = [kb.ps(es2, [128, 512], F32) for _ in range(2)]
                pX = kb.ps(es2, [128, 512], F32)
                BpA = P.bufs(2, "pA")
                BpB2 = P.bufs(2, "pB2")
                BpU = P.bufs(2, "pU")
                BpX = P.buf("pX")
                P.dma("sp", DMA(s5d[:], d["s5d"]), writes=[Bs5d])
                dmat = kb.sb(es2, [128, 64], F32)
                P.op("dve", CP(dmat[0:64, :], imat[0:64, 0, 0:64]), reads=[Bimat], writes=[Bimat])
                P.op("dve", CP(dmat[64:128, :], imat[64:128, 0, 64:128]), reads=[Bimat], writes=[Bimat])
                P.dma("sp", DMA(UTs[0][:], UT[:, 0, :]), writes=[BUTs[0]])
                for J in range(4):
                    if J + 1 < 4:
                        P.dma("sp", DMA(UTs[(J + 1) % 2][:], UT[:, J + 1, :]), writes=[BUTs[(J + 1) % 2]])
                    uj = UTs[J % 2]
                    buj = BUTs[J % 2]
                    P.op("act", ACT(usm[:], uj[:].rearrange("p (c s) -> p s c", s=8), AF.Copy), reads=[buj], writes=[Busm])
                    for j in range(8):
                        g = 8 * J + j
                        for half in range(2):
                            c0 = 272 * half
                            for s in range(8):
                                P.op("pe", MM(pU[half][:, 0:272], psel[:, j, 112 - 16 * s:240 - 16 * s],
                                              usm[:, s, c0:c0 + 272], start=(s == 0), stop=(s == 7)),
                                     reads=[Bpsel, Busm], writes=[BpU[half]])
                            P.op("act", ACT(Ug[:, c0:c0 + 272], pU[half][:, 0:272], AF.Copy), reads=[BpU[half]], writes=[BUg])
                        for dd in range(2):
                            gd = dd * 32 + g
                            Mk = Mall[dd]
                            P.op("dve" if dd == 0 else "pool",
                                 TT(Mk[:].rearrange("p k (a c) -> p k a c", a=2),
                                    dmat[:].unsqueeze(1).unsqueeze(1).broadcast_to([128, NPOW, 2, 64]),
                                    Vt[:, gd, :, :].unsqueeze(3).broadcast_to([128, NPOW, 2, 64]), ALU.mult),
                                 reads=[Bimat, BH], writes=[BMall[dd]])
                            if dd == 0:
                                P.op("pe", MM(pA[dd][:, 0:512], Bz[:, gd, :], Ug[:, 0:512]), reads=[BBz, BUg], writes=[BpA[dd]])
                                P.op("pe", MM(pB2[dd][:, 0:32], Bz[:, gd, :], Ug[:, 512:544]), reads=[BBz, BUg], writes=[BpB2[dd]])
                            else:
                                P.op("pe", MM(pA[dd][:, 0:512], Bz[:, gd, :], Ug[:, 32:544]), reads=[BBz, BUg], writes=[BpA[dd]])
                                P.op("pe", MM(pB2[dd][:, 0:32], Bz[:, gd, :], Ug[:, 0:32]), reads=[BBz, BUg], writes=[BpB2[dd]])
                            P.op("dve", CP(H[dd][:, 0:512], pA[dd][:, 0:512]), reads=[BpA[dd]], writes=[BHh[dd]])
                            P.op("dve", CP(H[dd][:, 512:544], pB2[dd][:, 0:32]), reads=[BpB2[dd]], writes=[BHh[dd]])
                        pidx = {n: i for i, n in enumerate(S5_POW)}

                        def seg_mm(dd, mat, pa0, src0, n, first, last, plain):
                            cast = (lambda ap: ap.bitcast(F32)) if plain else (lambda ap: ap)
                            segs = []
                            if pa0 < 512:
                                na = min(n, 512 - pa0)
                                segs.append((pA[dd][:, pa0:pa0 + na], src0, na, BpA[dd]))
                                if n > na:
                                    segs.append((pB2[dd][:, 0:n - na], src0 + na, n - na, BpB2[dd]))
                            else:
                                segs.append((pB2[dd][:, pa0 - 512:pa0 - 512 + n], src0, n, BpB2[dd]))
                            for (out_, s0_, n_, bout) in segs:
                                P.op("pe", MM(out_, cast(mat), cast(H[dd][:, s0_:s0_ + n_]), start=first, stop=last),
                                     reads=[BMall[dd], BHh[dd]], writes=[bout])

                        for (st, mults) in S5_LEVELS:
                            for dd in range(2):
                                Hd = H[dd]
                                plain = (st == 1)
                                W_ = NCK - st
                                for ji, jm in enumerate(mults):
                                    sh = jm * st
                                    if sh >= NCK:
                                        continue
                                    n = NCK - sh
                                    mat = Mall[dd][:, pidx[sh], :]
                                    pa0 = (sh - st) if dd == 0 else 0
                                    src0 = 0 if dd == 0 else sh
                                    seg_mm(dd, mat, pa0, src0, n, ji == 0, ji == len(mults) - 1, plain)
                                do = st if dd == 0 else 0
                                n0 = min(W_, 512)
                                P.op("dve", TT(Hd[:, do:do + n0], Hd[:, do:do + n0], pA[dd][:, 0:n0], ALU.add), reads=[BHh[dd], BpA[dd]], writes=[BHh[dd]])
                                if W_ > 512:
                                    P.op("dve", TT(Hd[:, do + 512:do + W_], Hd[:, do + 512:do + W_], pB2[dd][:, 0:W_ - 512], ALU.add),
                                         reads=[BHh[dd], BpB2[dd]], writes=[BHh[dd]])
                        y0 = pU[0]
                        y1 = pU[1]
                        g0_, g1_ = g, 32 + g
                        P.op("pe", MM(y0[:, 0:512], Toep[:, g0_, :], Ug[:, 0:512], start=True, stop=False), reads=[BToep, BUg], writes=[BpU[0]])
                        P.op("pe", MM(y0[:, 0:512], Toep[:, g1_, :], Ug[:, 0:512], start=False, stop=False), reads=[BToep, BUg], writes=[BpU[0]])
                        P.op("pe", MM(y0[:, 1:512], Cy[:, g0_, :], H[0][:, 0:511].bitcast(F32), start=False, stop=False), reads=[BCy, BHh[0]], writes=[BpU[0]])
                        P.op("pe", MM(y0[:, 32:512], Cy[:, g1_, :], H[1][:, 1:481].bitcast(F32), start=False, stop=False), reads=[BCy, BHh[1]], writes=[BpU[0]])
                        P.op("pe", MM(y0[:, 0:31], Cy[:, g1_, :], H[1][:, 513:544].bitcast(F32), start=False, stop=True), reads=[BCy, BHh[1]], writes=[BpU[0]])
                        P.op("pe", MM(y1[:, 0:32], Toep[:, g0_, :], Ug[:, 512:544], start=True, stop=False), reads=[BToep, BUg], writes=[BpU[1]])
                        P.op("pe", MM(y1[:, 0:32], Toep[:, g1_, :], Ug[:, 512:544], start=False, stop=False), reads=[BToep, BUg], writes=[BpU[1]])
                        P.op("pe", MM(y1[:, 0:32], Cy[:, g0_, :], H[0][:, 511:543].bitcast(F32), start=False, stop=False), reads=[BCy, BHh[0]], writes=[BpU[1]])
                        P.op("pe", MM(y1[:, 0:32], Cy[:, g1_, :], H[1][:, 481:513].bitcast(F32), start=False, stop=True), reads=[BCy, BHh[1]], writes=[BpU[1]])
                        P.op("act", ACT(Yall[:, j, 0:512], y0[:, 0:512], AF.Copy), reads=[BpU[0]], writes=[BYall])
                        P.op("act", ACT(Yall[:, j, 512:544], y1[:, 0:32], AF.Copy), reads=[BpU[1]], writes=[BYall])
                    for tt in range(9):
                        c0 = 64 * tt
                        ncol = min(64, NCK - c0)
                        ntk = 8 * ncol
                        for t in range(8):
                            for j in range(8):
                                P.op("pe", MM(pX[:, t:ntk:8], psel[:, t, 112 - 16 * j:240 - 16 * j], Yall[:, j, c0:c0 + ncol],
                                              start=(j == 0), stop=(j == 7)), reads=[Bpsel, BYall], writes=[BpX])
                        P.op("dve", STT(ytmp[:, :ntk], uj[:, 8 * c0:8 * c0 + ntk], s5d[:, J:J + 1], pX[:, :ntk], ALU.mult, ALU.add),
                             reads=[buj, Bs5d, BpX], writes=[Bytmp])
                        P.op("dve", TT(yt2[:, :ntk], ytmp[:, :ntk], ytmp[:, :ntk], ALU.mult), reads=[Bytmp], writes=[Byt2])
                        P.op("dve", TS(yt2[:, :ntk], yt2[:, :ntk], 0.044715, 1.0, ALU.mult, ALU.add), reads=[Byt2], writes=[Byt2])
                        P.op("dve", TT(yt2[:, :ntk], yt2[:, :ntk], ytmp[:, :ntk], ALU.mult), reads=[Byt2, Bytmp], writes=[Byt2])
                        P.op("act", ACT(yt2[:, :ntk], yt2[:, :ntk], AF.Tanh, scale=0.7978845608028654), reads=[Byt2], writes=[Byt2])
                        P.op("dve", TS(yt2[:, :ntk], yt2[:, :ntk], 0.5, 0.5, ALU.mult, ALU.add), reads=[Byt2], writes=[Byt2])
                        P.op("dve", TT(ygT[:, J, 8 * c0:8 * c0 + ntk], yt2[:, :ntk], ytmp[:, :ntk], ALU.mult), reads=[Byt2, Bytmp], writes=[Bygt])
            P.barrier()
            P.phase = "s5glu"
            with contextlib.ExitStack() as es2:
                Wg = kb.sb(es2, [128, 4, 512], BF16)
                bg = kb.sb(es2, [128, 4], F32)
                sgm = kb.sb(es2, [128, 512], BF16)
                ob = [kb.sb(es2, [128, 4, 512], BF16) for _ in range(2)]
                pG = [kb.ps(es2, [128, 512], F32) for _ in range(2)]
                BWg, Bbg, Bsgm = P.bufs(3, "glu")
                Bob = P.bufs(2, "gob")
                BpG = P.bufs(2, "pG")
                Bmo = P.buf("mo")
                P.dma("pool", DMA(Wg[:], d["wglu"]), writes=[BWg])
                P.dma("sp", DMA(bg[:], d["bglu"]), writes=[Bbg])
                tiles = [(0, 256)] + [(256 + 512 * i, 512) for i in range(8)]
                for i, (t0, nt) in enumerate(tiles):
                    o = ob[i % 2]
                    for co in range(4):
                        p = pG[co % 2]
                        for kc in range(4):
                            P.op("pe", MM(p[:, :nt], Wg[:, kc, co * 128:(co + 1) * 128], ygT[:, kc, t0:t0 + nt], start=(kc == 0), stop=(kc == 3)),
                                 reads=[BWg, Bygt], writes=[BpG[co % 2]])
                        P.op("act", ACT(sgm[:, :nt], p[:, :nt], AF.Sigmoid, bias=bg[:, co:co + 1]), reads=[BpG[co % 2], Bbg], writes=[Bsgm])
                        P.op("dve", TT(o[:, co, :nt], ygT[:, co, t0:t0 + nt], sgm[:, :nt], ALU.mult), reads=[Bygt, Bsgm], writes=[Bob[i % 2]])
                    P.dma("sp", DMA(mo[:, 4:8, t0:t0 + nt], o[:, :, :nt]), reads=[Bob[i % 2]], acc=[Bmo])
                P.wait_all("sp", [Bmo])
    P.barrier()


def phase_s5(kb, d, UT, mo):
    P = kb.P
    S5M = s5_scratch(kb.nc)
    P.phase = "s5setup"
    for _ in s5_setup_gen(kb, d, S5M):
        pass
    P.barrier()
    phase_s5_main(kb, d, UT, mo, S5M)


ANNOTATE = False
ADALN_BG = False


def build_program(shapes):
    nc = bass.Bass("TRN2", target_bir_lowering=False)
    d = {k: nc.dram_tensor(k, list(v), F32, kind="ExternalInput").ap() for k, v in shapes.items()}
    out = nc.dram_tensor("out", [128, KC, NLAT], F32, kind="ExternalOutput").ap()
    H1 = nc.dram_tensor("H1", [128, KC, NTOK], F32).ap()
    H2 = nc.dram_tensor("H2", [128, KC, NTOK], F32).ap()
    H3 = nc.dram_tensor("H3", [128, KC, NLAT], F32).ap()
    mo0 = nc.dram_tensor("mo0", [128, KC, NTOK], BF16).ap()
    mo1 = nc.dram_tensor("mo1", [128, KC, NLAT], BF16).ap()
    UT = nc.dram_tensor("UTs", [128, 4, NTOK], BF16).ap()
    XM1 = nc.dram_tensor("XM1", [128, KC, NTOK], BF16).ap()
    SG = nc.dram_tensor("SGs", [128, 4, NTOK], BF16).ap()
    with contextlib.ExitStack() as es:
        kb = KB(nc, es)
        P = kb.P
        P.annotate = ANNOTATE
        vec = kb.sb(es, [128, 2, 2, 6, 8], F32)
        ones = kb.sb(es, [128, 128], BF16)
        ident = kb.sb(es, [128, 128], BF16)
        mfb = kb.sb(es, [128, 2, 128], F32)
        nfin = kb.sb(es, [128, KC], F32)
        Bvec, Bones, Bident, Bmfb = P.bufs(4, "const")
        P.op("pool", MSET(ones[:], 1.0), writes=[Bones])
        P.dma("pool", DMA(ident[:], d["ident"]), writes=[Bident])
        P.dma("sp", DMA(mfb[:], d["mfb"]), writes=[Bmfb])
        P.dma("sp", DMA(nfin[:], d["nrm"][:, 4, :]), writes=[Bvec])
        xin = d["xin"]
        all_tiles512 = [(0, 256, 1)] + [(256 + 512 * i, 512, 0) for i in range(8)]
        all_tiles256 = [(0, 256, 1)] + [(256 + 384 * i, 384, 0) for i in range(10)] + [(256 + 3840, 256, 0)]
        lat_tiles512 = [(256 + 512 * i, 512, 0) for i in range(8)]
        lat_tiles256 = [(256 + 384 * i, 384, 0) for i in range(10)] + [(256 + 3840, 256, 0)]
        with contextlib.ExitStack() as esA:
            P.phase = "adaln"
            ada = Adaln(kb, d, esA, vec, Bvec)
            S5M = s5_scratch(nc)

            def ada_all():
                yield from ada.layer(0)
                if not ADALN_BG:
                    yield from ada.layer(1)

            interleave([(0, ada_all()), (0, s5_setup_gen(kb, d, S5M))], 2)
            P.barrier()

            def side():
                for _ in ada.layer(1):
                    yield

            with contextlib.ExitStack() as es2:
                Qr = kb.sb(es2, [128, 2, NTOK], BF16)
                Kr = kb.sb(es2, [128, 2, NTOK], BF16)
                Ktm = kb.sb(es2, [128, NTOK // 128, 256], BF16)
                Vtm = kb.sb(es2, [128, NTOK // 128, 512], BF16)
                Bqkv = P.buf("qkv")
                phase_l0_proj(kb, d, xin, Qr, Kr, Ktm, Vtm, Bqkv, UT, SG, vec, Bvec, ones, Bones, ident, Bident, side=(side() if ADALN_BG else None))
                phase_ret(kb, d, Qr, Kr, Ktm, Vtm, Bqkv, SG, mo0, ones, Bones, mfb, Bmfb)
        P.barrier()
        phase_s5_main(kb, d, UT, mo0, S5M)
        phase_outproj(kb, d["abwo"], 0, mo0, 0, {"ap": xin, "off": 0}, {"ap": H1, "off": 0}, all_tiles512, vec, Bvec)
        phase_mlp(kb, d, 0, {"ap": H1, "off": 0}, {"ap": H2, "off": 0}, all_tiles256, vec, Bvec, ones, Bones)
        phase_hgrn(kb, d, H2, mo1, vec, Bvec, ones, Bones, ident, Bident, xm1=XM1)
        phase_outproj(kb, d["hgwo"], 1, mo1, NCTX, {"ap": H2, "off": 0}, {"ap": H3, "off": NCTX}, lat_tiles512, vec, Bvec)
        phase_mlp(kb, d, 1, {"ap": H3, "off": NCTX}, {"ap": out, "off": NCTX}, lat_tiles256, vec, Bvec, ones, Bones, final_g=nfin)
        P.emit()
    return nc


def host_shared(inp):
    f = lambda k: np.asarray(inp[k], dtype=np.float32)
    d = {}
    d["bmod"] = np.ascontiguousarray(np.stack([lay_vec(f("b_mod")[l]) for l in range(2)], axis=1))
    d["nrm"] = np.ascontiguousarray(np.stack([lay_vec(f("norm_mix")[0]), lay_vec(f("norm_mix")[1]), lay_vec(f("norm_mlp")[0]),
                                              lay_vec(f("norm_mlp")[1]), lay_vec(f("norm_final"))], axis=1))
    wm = np.stack([lay_kc(f("w_mod")[l]) for l in range(2)])
    d["wmod"] = np.ascontiguousarray(wm.reshape(2, 128, 8, 12, 512).transpose(0, 3, 1, 2, 4))
    d["w1"] = np.stack([lay_kc(f("w_mlp_in")[l]) for l in range(2)])
    d["w2"] = np.stack([lay_kc(f("w_mlp_out")[l]) for l in range(2)])
    wk = lay_kc(f("hg_w_in")[0]).reshape(128, 8, 5, 8, 128)
    d["hgw"] = np.ascontiguousarray(wk.transpose(3, 0, 1, 2, 4))
    d["hgwo"] = lay_kc(f("hg_w_out")[0])
    d["lbl"] = np.ascontiguousarray(f("hg_lb_logits").reshape(2, 2, 8, 128).transpose(3, 0, 1, 2))
    d["hgn"] = np.ascontiguousarray(f("hg_norm")[0].reshape(128, 1))
    s = np.arange(128)
    d["mfb"] = np.ascontiguousarray(np.stack([(s[:, None] <= s[None, :]), (s[:, None] >= s[None, :])], axis=1).astype(np.float32))
    cm = np.ones((128, 512), np.float32)
    cm[:, ::128] = 0
    d["cmask"] = cm
    d["ident"] = np.eye(128, dtype=np.float32)
    w = f("ab_w_in")[0]
    q = w[:, 0:256].reshape(1024, 4, 64)
    k = w[:, 256:512].reshape(1024, 4, 64)
    qrot = np.concatenate([q[:, :, 32:], q[:, :, :32]], -1).reshape(1024, 256)
    krot = np.concatenate([k[:, :, 32:], k[:, :, :32]], -1).reshape(1024, 256)
    d["abw"] = lay_kc(np.concatenate([w, qrot, krot], 1))
    rows = NLAT // 64
    row = np.repeat(np.arange(rows, dtype=np.float32), 64)
    col = np.tile(np.arange(64, dtype=np.float32), rows)
    inv = (10000.0 ** (-np.arange(16, dtype=np.float32) / 16)).astype(np.float32)
    ang = np.concatenate([row[:, None] * inv, col[:, None] * inv], -1).astype(np.float32)
    idx = np.arange(128) % 32
    d["rope"] = np.ascontiguousarray(np.stack([np.cos(ang).astype(np.float32)[:, idx].T, np.sin(ang).astype(np.float32)[:, idx].T], 1))
    rl = f("ret_logit")[0]
    d["retl"] = np.ascontiguousarray(np.broadcast_to(rl.reshape(1, 8), (128, 8))).astype(np.float32)
    rp = np.zeros((128, 2, 2), np.float32)
    for dr in range(2):
        for pair in range(2):
            rp[:64, dr, pair] = rl[dr, 2 * pair]
            rp[64:, dr, pair] = rl[dr, 2 * pair + 1]
    d["retlP"] = rp
    j = np.arange(128, dtype=np.float32)
    d["dF"] = np.ascontiguousarray(np.stack([np.maximum(j[None, :] - j[:, None], 0), np.maximum(j[:, None] - j[None, :], 0)], 1)).astype(np.float32)
    d["posc"] = np.ascontiguousarray(np.broadcast_to(np.stack([j + 1, 128 - j], 0)[None], (128, 2, 128))).astype(np.float32)
    d["jcol"] = np.ascontiguousarray(np.stack([np.repeat((127 - j)[:, None], 4, 1), np.repeat(j[:, None], 4, 1)], 1)).astype(np.float32)
    hm = np.zeros((128, 4), np.float32)
    hm[:64, 0] = 1
    hm[:64, 2] = 1
    hm[64:, 1] = 1
    hm[64:, 3] = 1
    d["hm"] = hm
    are = f("s5_a_re")[0].reshape(64, 64)
    aim = f("s5_a_im")[0].reshape(64, 64)
    a2 = np.stack([are.T, aim.T], 1)
    d["s5a"] = np.ascontiguousarray(np.concatenate([a2, a2], 0)).astype(np.float32)
    d["s5dt"] = np.ascontiguousarray(np.broadcast_to(f("s5_log_dt")[0].reshape(1, 64), (128, 64))).astype(np.float32)
    bre = f("s5_b_re")[0].reshape(64, 64, 16).transpose(1, 0, 2)
    bim = f("s5_b_im")[0].reshape(64, 64, 16).transpose(1, 0, 2)
    d["s5b"] = np.ascontiguousarray(np.stack([np.concatenate([bre, bim], 0), np.concatenate([bim, bre], 0)], 1)).astype(np.float32)
    cre = f("s5_c_re")[0].reshape(64, 16, 64).transpose(2, 0, 1)
    cim = f("s5_c_im")[0].reshape(64, 16, 64).transpose(2, 0, 1)
    d["s5c"] = np.ascontiguousarray(np.stack([np.concatenate([cre, cim], 0), np.concatenate([cim, cre], 0)], 1)).astype(np.float32)
    nv = np.stack([np.arange(16) - 7.0, 8.0 - np.arange(16)], 0).astype(np.float32)
    d["nvec"] = np.ascontiguousarray(np.broadcast_to(nv[None], (128, 2, 16))).astype(np.float32)
    ps = np.zeros((128, 8, 240), np.float32)
    for jj in range(8):
        for m in range(16):
            ps[16 * jj + m, jj, 112 + m] = 1.0
    d["psel"] = ps
    s8 = np.arange(128) // 16
    d["tmask"] = np.ascontiguousarray(np.stack([(s8[None, :] >= s8[:, None]), (s8[:, None] >= s8[None, :])], 1)).astype(np.float32)
    im = np.zeros((128, 2, 128), np.float32)
    im[:, 0, :] = np.eye(128)
    for p in range(64):
        im[p, 1, 64 + p] = 1.0
        im[64 + p, 1, p] = -1.0
    d["imat"] = im
    d["s5d"] = lay_vec(f("s5_d")[0])
    d["bglu"] = lay_vec(f("s5_b_glu")[0])
    d["wglu"] = lay_kc(f("s5_w_glu")[0])
    d["abwo"] = lay_kc(f("ab_w_out")[0])
    return d


_CACHE = {}


def kernel(**inp):
    shared = host_shared(inp)
    x = np.asarray(inp["x"], dtype=np.float32)
    ctx = np.asarray(inp["ctx"], dtype=np.float32)
    c = np.asarray(inp["c"], dtype=np.float32)
    c_ctx = np.asarray(inp["c_ctx"], dtype=np.float32)
    B = x.shape[0]
    in_maps = []
    for b in range(B):
        m = dict(shared)
        full = np.concatenate([ctx[b], x[b]], 0)
        m["xin"] = lay_kc(np.ascontiguousarray(full.T))
        m["cc"] = np.ascontiguousarray(np.stack([lay_vec(c[b]), lay_vec(c_ctx)], axis=-1))
        in_maps.append(m)
    nc = build_program({k: v.shape for k, v in in_maps[0].items()})
    res = run_bass_kernel_spmd(nc, in_maps, core_ids=list(range(B)))
    outs = []
    for b in range(B):
        o = np.asarray(res.results[b]["out"], dtype=np.float32).reshape(128, KC, NLAT)
        outs.append(o.transpose(2, 1, 0).reshape(NLAT, D))
    return np.stack(outs, 0)
```
